# Optimizing a Trainium2 kernel written in Bass

```python
import math
import numpy as np
import jax
import jax.numpy as jnp
from jax import lax

D_MODEL = 1024
BATCH = 8
SEQ = 4096
DEPTH = 2

HEAD_DIM = 64
BLOCK = 128
EPS = 1e-6
A_Q_HEADS = 8
A_KV_HEADS = 2
A_GROUP = A_Q_HEADS // A_KV_HEADS
A_WIDTH = A_Q_HEADS * HEAD_DIM
WINDOW = 128
B_WIDTH = D_MODEL // 2
CONV_WIDTH = 31
C_HEADS = 4
C_WIDTH = C_HEADS * 2 * HEAD_DIM
N_BRANCH = 3

A_Q_COLS = A_WIDTH
A_K_COLS = A_KV_HEADS * HEAD_DIM
A_V_COLS = A_KV_HEADS * HEAD_DIM
A_G_COLS = A_WIDTH
B_GLU_COLS = 2 * B_WIDTH
B_G_COLS = B_WIDTH
C_Q_COLS = C_WIDTH
C_K_COLS = C_WIDTH
C_V_COLS = C_WIDTH
C_G_COLS = C_WIDTH
MERGE_COLS = N_BRANCH * D_MODEL
SPLITS = (A_Q_COLS, A_K_COLS, A_V_COLS, A_G_COLS, B_GLU_COLS, B_G_COLS,
          C_Q_COLS, C_K_COLS, C_V_COLS, C_G_COLS, MERGE_COLS)
IN_WIDTH = sum(SPLITS)

kernel_name = "hybrid_swa_sink_conformer_diffattn_gated"


def rms_norm(x, g):
    xf = x.astype(jnp.float32)
    y = xf * lax.rsqrt(jnp.mean(xf * xf, axis=-1, keepdims=True) + EPS)
    return (y * g.astype(jnp.float32)).astype(x.dtype)


def sliding_window_attention(q, k, v, sinks):
    B, T, _, d = q.shape
    nb = T // BLOCK
    qb = q.reshape(B, nb, BLOCK, A_KV_HEADS, A_GROUP, d)

    def band(t):
        tb = t.reshape(B, nb, BLOCK, A_KV_HEADS, d)
        prev = jnp.concatenate([jnp.zeros_like(tb[:, :1]), tb[:, :-1]], axis=1)
        return jnp.concatenate([prev, tb], axis=2)

    kw, vw = band(k), band(v)
    s = jnp.einsum('bnqkgd,bnskd->bnkgqs', qb, kw,
                   preferred_element_type=jnp.float32) * (d ** -0.5)
    qi = jnp.arange(BLOCK)[:, None] + BLOCK
    kj = jnp.arange(2 * BLOCK)[None, :]
    rel = qi - kj
    local = (rel >= 0) & (rel < WINDOW)
    kabs = jnp.arange(nb)[:, None, None] * BLOCK - BLOCK + kj[None]
    mask = local[None] & (kabs >= 0)
    s = jnp.where(mask[None, :, None, None], s, -jnp.inf)
    sink = sinks.astype(jnp.float32).reshape(1, 1, A_KV_HEADS, A_GROUP, 1, 1)
    m = jnp.maximum(jnp.max(s, axis=-1, keepdims=True), sink)
    p = jnp.exp(s - m)
    p = p / (jnp.sum(p, axis=-1, keepdims=True) + jnp.exp(sink - m))
    o = jnp.einsum('bnkgqs,bnskd->bnqkgd', p.astype(v.dtype), vw)
    return o.reshape(B, T, A_Q_HEADS * d)


def conformer_conv(glu_in, conv_w, conv_b, ln_g, ln_b):
    a, b = jnp.split(glu_in, 2, axis=-1)
    u = a * jax.nn.sigmoid(b)
    u = lax.conv_general_dilated(
        u, conv_w[:, None, :], window_strides=(1,),
        padding=[(CONV_WIDTH - 1, 0)],
        dimension_numbers=('NWC', 'WIO', 'NWC'),
        feature_group_count=B_WIDTH) + conv_b
    uf = u.astype(jnp.float32)
    mu = jnp.mean(uf, axis=-1, keepdims=True)
    var = jnp.mean(jnp.square(uf - mu), axis=-1, keepdims=True)
    un = (uf - mu) * lax.rsqrt(var + EPS) * ln_g.astype(jnp.float32) + ln_b.astype(jnp.float32)
    return jax.nn.silu(un).astype(u.dtype)


def differential_attention(q, k, v, lam, lam_init, subln_g):
    B, T, H, _, d = q.shape
    nb = T // BLOCK
    qb = jnp.moveaxis(q.reshape(B, nb, BLOCK, H, 2, d), 1, 0)
    kpos = jnp.arange(T)

    def one_block(args):
        q_blk, n = args
        s = jnp.einsum('bqhcd,bshcd->bhcqs', q_blk, k,
                       preferred_element_type=jnp.float32) * (d ** -0.5)
        qpos = n * BLOCK + jnp.arange(BLOCK)
        causal = kpos[None, :] <= qpos[:, None]
        a = jax.nn.softmax(jnp.where(causal, s, -jnp.inf), axis=-1)
        diff = a[:, :, 0] - lam * a[:, :, 1]
        return jnp.einsum('bhqs,bshe->bqhe', diff.astype(v.dtype), v)

    o = lax.map(one_block, (qb, jnp.arange(nb)))
    o = jnp.moveaxis(o, 0, 1).reshape(B, T, H, 2 * d)
    o = rms_norm(o, subln_g) * (1.0 - lam_init)
    return o.reshape(B, T, H * 2 * d)


def setup_inputs(seed: int = 0) -> dict:
    key = jax.random.key(seed)
    ks = jax.random.split(key, 24)
    f = jnp.float32
    nrm = lambda k, shape, s: jax.random.normal(k, shape, f) * s
    L, D = DEPTH, D_MODEL
    return {
        "x": nrm(ks[0], (BATCH, SEQ, D), 1.0),
        "norm_g": 1.0 + nrm(ks[1], (L, D), 0.02),
        "w_in": nrm(ks[2], (L, D, IN_WIDTH), D ** -0.5),
        "attn_q_norm_g": 1.0 + nrm(ks[3], (L, HEAD_DIM), 0.02),
        "attn_k_norm_g": 1.0 + nrm(ks[4], (L, HEAD_DIM), 0.02),
        "attn_sinks": nrm(ks[5], (L, A_Q_HEADS), 0.5),
        "w_o_attn": nrm(ks[6], (L, A_WIDTH, D), A_WIDTH ** -0.5),
        "conv_w": nrm(ks[7], (L, CONV_WIDTH, B_WIDTH), CONV_WIDTH ** -0.5),
        "conv_b": nrm(ks[8], (L, B_WIDTH), 0.02),
        "conv_norm_g": 1.0 + nrm(ks[9], (L, B_WIDTH), 0.02),
        "conv_norm_b": nrm(ks[10], (L, B_WIDTH), 0.02),
        "w_o_conv": nrm(ks[11], (L, B_WIDTH, D), B_WIDTH ** -0.5),
        "diff_q_norm_g": 1.0 + nrm(ks[12], (L, HEAD_DIM), 0.02),
        "diff_k_norm_g": 1.0 + nrm(ks[13], (L, HEAD_DIM), 0.02),
        "lambda_q": nrm(ks[14], (L, 2, HEAD_DIM), 0.1),
        "lambda_k": nrm(ks[15], (L, 2, HEAD_DIM), 0.1),
        "diff_subln_g": 1.0 + nrm(ks[16], (L, 2 * HEAD_DIM), 0.02),
        "w_o_diff": nrm(ks[17], (L, C_WIDTH, D), C_WIDTH ** -0.5),
        "w_out": nrm(ks[18], (L, D, D), D ** -0.5),
    }


def reference(x, norm_g, w_in, attn_q_norm_g, attn_k_norm_g, attn_sinks, w_o_attn,
              conv_w, conv_b, conv_norm_g, conv_norm_b, w_o_conv,
              diff_q_norm_g, diff_k_norm_g, lambda_q, lambda_k, diff_subln_g, w_o_diff,
              w_out):
    B, T, D = x.shape
    split_points = [int(s) for s in np.cumsum(SPLITS)[:-1]]
    for l in range(DEPTH):
        h = rms_norm(x, norm_g[l])
        z = jnp.einsum('btd,de->bte', h, w_in[l])
        aq, ak, av, ag, bglu, bg, cq, ck, cv, cg, mg = jnp.split(z, split_points, axis=-1)

        aq = rms_norm(aq.reshape(B, T, A_Q_HEADS, HEAD_DIM), attn_q_norm_g[l])
        ak = rms_norm(ak.reshape(B, T, A_KV_HEADS, HEAD_DIM), attn_k_norm_g[l])
        av = av.reshape(B, T, A_KV_HEADS, HEAD_DIM)
        ya = sliding_window_attention(aq, ak, av, attn_sinks[l]) * jax.nn.silu(ag)
        ya = ya @ w_o_attn[l]

        yb = conformer_conv(bglu, conv_w[l], conv_b[l], conv_norm_g[l], conv_norm_b[l])
        yb = (yb * jax.nn.silu(bg)) @ w_o_conv[l]

        lam_init = 0.8 - 0.6 * math.exp(-0.3 * l)
        e = jnp.exp(jnp.sum(lambda_q[l].astype(jnp.float32) * lambda_k[l].astype(jnp.float32), axis=-1))
        lam = e[0] - e[1] + lam_init
        cq = rms_norm(cq.reshape(B, T, C_HEADS, 2, HEAD_DIM), diff_q_norm_g[l])
        ck = rms_norm(ck.reshape(B, T, C_HEADS, 2, HEAD_DIM), diff_k_norm_g[l])
        cv = cv.reshape(B, T, C_HEADS, 2 * HEAD_DIM)
        yc = differential_attention(cq, ck, cv, lam, lam_init, diff_subln_g[l]) * jax.nn.silu(cg)
        yc = yc @ w_o_diff[l]

        gates = jax.nn.sigmoid(mg).reshape(B, T, N_BRANCH, D)
        merged = gates[:, :, 0] * ya + gates[:, :, 1] * yb + gates[:, :, 2] * yc
        x = x + merged @ w_out[l]
    return x
```

```python
import contextlib
import math
import types
import numpy as np
import concourse.bass as bass
import concourse.mybir as mybir
from concourse.bass_utils import run_bass_kernel_spmd

F32 = mybir.dt.float32
BF16 = mybir.dt.bfloat16
AF = mybir.ActivationFunctionType
ALU = mybir.AluOpType
AX = mybir.AxisListType

ENGS = ("pe", "act", "dve", "pool", "sp")
SCHEDULE = True
CHECK = True
CONV_PRIO = 2500
NSLOT = 4
EPS = 1e-6
D = 1024
NG = 24
GSZ = 4096
C_GBC, C_AQG, C_AKG, C_DQG, C_DKG, C_SUBG, C_SINK, C_CONVW, C_CONVB, C_LNG, C_LNB, C_LQ, C_LK, C_SUBGC, NCST = (
    0, 1024, 1088, 1152, 1216, 1280, 1408, 1416, 1540, 1544, 1548, 1552, 1680, 1808, 1812)


class T:
    __slots__ = ("name", "w", "rs")

    def __init__(self, name):
        self.name = name
        self.w = None
        self.rs = []


def _freeze(fn):
    if fn is None or fn.__closure__ is None:
        return fn
    cells = []
    for c in fn.__closure__:
        try:
            cells.append(types.CellType(c.cell_contents))
        except ValueError:
            cells.append(c)
    g = types.FunctionType(fn.__code__, fn.__globals__, fn.__name__, fn.__defaults__, tuple(cells))
    g.__kwdefaults__ = fn.__kwdefaults__
    return g


class Op:
    __slots__ = ("id", "eng", "fn", "chan", "deps", "cost", "lat", "succ", "nun", "fin", "cnt", "semkey", "key")

    def __init__(self, id_, eng, fn, chan, cost, lat):
        self.id = id_
        self.key = id_
        self.eng = eng
        self.fn = fn
        self.chan = chan
        self.deps = {}
        self.cost = cost
        self.lat = lat
        self.succ = []
        self.nun = 0
        self.fin = 0.0
        self.cnt = 0
        self.semkey = None


DEFAULT_COST = {"pe": 0.25, "act": 0.5, "dve": 0.55, "pool": 0.9, "sp": 0.1}


class _MockIns:
    def then_inc(self, *a, **k):
        return self

    def _wait_ge(self, *a, **k):
        return self


def _fsize(ap):
    n = 1
    for d_ in ap.shape[1:]:
        n *= d_
    return n


class _MockEng:
    def __init__(self, eng):
        self.eng = eng
        self.cost = 0.0

    def __getattr__(self, name):
        def call(*args, **kw):
            eng = self.eng
            if eng == "pe":
                if name == "matmul":
                    rhs = args[2] if len(args) > 2 else kw["rhs"]
                    n = _fsize(rhs)
                    self.cost += 0.02 + n / 2000.0 * (4.0 if rhs.dtype == F32 else 1.0)
                else:
                    self.cost += 0.11
            else:
                out = kw.get("out", args[0] if args else None)
                n = _fsize(out) if out is not None else 64
                if eng == "act":
                    self.cost += 0.2 + n / 1200.0 + (0.1 if kw.get("accum_out") is not None else 0.0)
                elif eng == "dve":
                    self.cost += 0.17 + n / 960.0
                elif eng == "pool":
                    fp32 = out is not None and out.dtype == F32
                    self.cost += 0.2 + n * (0.0032 if fp32 else 0.0012)
                else:
                    self.cost += 0.1
            return _MockIns()
        return call


class Prog:
    def __init__(self, nc):
        self.nc = nc
        self.sems = {}
        self._stack = []
        self.segs = [[]]
        self.nops = 0
        for e in ENGS:
            self._mksem("E_" + e)

    def _mksem(self, key):
        cm = self.nc.semaphore(key)
        h = cm.__enter__()
        self._stack.append(cm)
        self.sems[key] = h
        return h

    def chan(self, name):
        key = "D_" + name
        if key not in self.sems:
            self._mksem(key)
        return key

    def _record(self, o, reads, writes):
        is_dma = o.chan is not None

        def add(d, raw):
            if d is None or d is o:
                return
            need = raw or is_dma or (d.chan is not None) or (d.eng != o.eng) or (o.eng != "pe")
            if need or d not in o.deps:
                o.deps[d] = need or o.deps.get(d, False)
        for t in reads:
            add(t.w, True)
        for t in writes:
            add(t.w, o.eng != "pe")
            for r in t.rs:
                add(r, False)
        for t in reads:
            t.rs.append(o)
        for t in writes:
            t.w = o
            t.rs = []
        self.segs[-1].append(o)

    def op(self, eng, fn, reads=(), writes=(), cost=None, prio=0):
        fn = _freeze(fn)
        try:
            m = _MockEng(eng)
            fn(m)
            c = m.cost
        except Exception:
            c = DEFAULT_COST[eng] if cost is None else cost
        o = Op(self.nops, eng, fn, None, c, c + 0.2)
        o.key = o.id + prio
        self.nops += 1
        self._record(o, reads, writes)
        return o

    def dma(self, eng, chan, fn, reads=(), writes=(), cost=3.0):
        o = Op(self.nops, eng, _freeze(fn), chan, 0.1 if eng == "sp" else 1.0, cost)
        self.nops += 1
        self._record(o, reads, writes)
        return o

    def barrier(self):
        self.segs.append([])

    def _schedule(self, ops):
        import heapq
        inseg = set(o.id for o in ops)
        for o in ops:
            o.succ = []
        for o in ops:
            n = 0
            for d in o.deps:
                if d.id in inseg:
                    d.succ.append(o)
                    n += 1
            o.nun = n
        fut = {e: [] for e in ENGS}
        avail = {e: [] for e in ENGS}
        free = {e: 0.0 for e in ENGS}
        order = {e: [] for e in ENGS}

        def push(o):
            rt = 0.0
            for d in o.deps:
                if d.id in inseg and d.fin > rt:
                    rt = d.fin
            heapq.heappush(fut[o.eng], (rt, o.id, o))
        for o in ops:
            if o.nun == 0:
                push(o)
        left = len(ops)
        while left:
            best = None
            for e in ENGS:
                f, a = fut[e], avail[e]
                while f and f[0][0] <= free[e]:
                    rt, i, o = heapq.heappop(f)
                    heapq.heappush(a, (o.key, i, o))
                if a:
                    st = free[e]
                elif f:
                    st = f[0][0]
                else:
                    continue
                if best is None or st < best[0]:
                    best = (st, e)
            st, e = best
            if avail[e]:
                _k, i, o = heapq.heappop(avail[e])
            else:
                rt, i, o = heapq.heappop(fut[e])
            free[e] = st + o.cost
            o.fin = st + o.lat
            order[e].append(o)
            left -= 1
            for s_ in o.succ:
                s_.nun -= 1
                if s_.nun == 0:
                    push(s_)
        return order, max(free.values())

    def emit(self, schedule=True):
        nc = self.nc
        sems = self.sems
        cnt = {k: 0 for k in sems}
        streams = {e: [] for e in ENGS}
        waited = {e: {} for e in ENGS}
        self.est = 0.0
        for si, seg in enumerate(self.segs):
            if schedule:
                order, est = self._schedule(seg)
                self.est += est
            else:
                order = {e: [o for o in seg if o.eng == e] for e in ENGS}
            for e in ENGS:
                for o in order[e]:
                    if o.chan is None:
                        o.semkey = "E_" + e
                        cnt[o.semkey] += 1
                    else:
                        o.semkey = o.chan
                        cnt[o.semkey] += 16
                    o.cnt = cnt[o.semkey]
            for e in ENGS:
                wd = waited[e]
                for o in order[e]:
                    need = {}
                    for d, ns in o.deps.items():
                        if not ns:
                            continue
                        if need.get(d.semkey, 0) < d.cnt:
                            need[d.semkey] = d.cnt
                    waits = []
                    for k, v in need.items():
                        if wd.get(k, 0) < v:
                            wd[k] = v
                            waits.append((k, v))
                    streams[e].append((waits, o.fn, (o.semkey, 1 if o.chan is None else 16), o))
            for e in ENGS:
                waits = []
                for k, v in cnt.items():
                    if v > 0 and waited[e].get(k, 0) < v:
                        waited[e][k] = v
                        waits.append((k, v))
                streams[e].append((waits, None, None, None))
        self.streams = streams
        if CHECK:
            self.check(streams)

        def replay(e, name):
            for waits, fn, inc, _o in streams[name]:
                for k, v in waits:
                    e.wait_ge(sems[k], v)
                if fn is None:
                    continue
                ins = fn(e)
                ins.then_inc(sems[inc[0]], inc[1])

        with nc.Block() as block:
            @block.tensor
            def _(e):
                replay(e, "pe")

            @block.scalar
            def _(e):
                replay(e, "act")

            @block.vector
            def _(e):
                replay(e, "dve")

            @block.gpsimd
            def _(e):
                replay(e, "pool")

            @block.sync
            def _(e):
                replay(e, "sp")

    def check(self, streams):
        val = {k: 0 for k in self.sems}
        ptr = {e: 0 for e in ENGS}
        done = set()
        total = sum(len(v) for v in streams.values())
        n = 0
        while n < total:
            prog = False
            for e in ENGS:
                st = streams[e]
                while ptr[e] < len(st):
                    waits, fn, inc, o = st[ptr[e]]
                    if any(val[k] < v for k, v in waits):
                        break
                    if o is not None:
                        for d in o.deps:
                            assert d.id in done, ("race", e, o.id, d.id, d.eng)
                        done.add(o.id)
                        val[inc[0]] += inc[1]
                        assert val[inc[0]] == o.cnt, ("count mismatch", e, o.id, inc, val[inc[0]], o.cnt)
                    ptr[e] += 1
                    n += 1
                    prog = True
            if not prog:
                info = {e: (ptr[e], len(streams[e]), streams[e][ptr[e]][0] if ptr[e] < len(streams[e]) else None) for e in ENGS}
                raise RuntimeError(f"deadlock: {info} vals={ {k: val[k] for e in ENGS for k, v in (streams[e][ptr[e]][0] if ptr[e] < len(streams[e]) else [])} }")

    def close(self):
        while self._stack:
            self._stack.pop().__exit__(None, None, None)


def build(Tn=4096, L=2, taps=()):
    NB = Tn // 128
    NT = Tn // 512
    nc = bass.Bass("TRN2", target_bir_lowering=False)
    x_d = nc.dram_tensor("x", [Tn, D], F32, kind="ExternalInput").ap()
    cpk_d = nc.dram_tensor("cpk", [L, 128, NCST], F32, kind="ExternalInput").ap()
    wpk_d = nc.dram_tensor("wpk", [L, NG, 128, GSZ], F32, kind="ExternalInput").ap()
    out_d = nc.dram_tensor("out", [Tn, D], F32, kind="ExternalOutput").ap()
    wbf_d = nc.dram_tensor("wbf", [L, NG, 128, GSZ], BF16, kind="Internal").ap()
    x1_d = nc.dram_tensor("x1s", [Tn, D], F32, kind="Internal").ap()
    tap_d = {}
    for name, shape, dt_ in taps:
        tap_d[name] = nc.dram_tensor("tap_" + name, list(shape), dt_, kind="ExternalOutput").ap()

    P = Prog(nc)
    es = contextlib.ExitStack()

    def sb(name, shape, dt, stack=es):
        return stack.enter_context(nc.sbuf_tensor(name, list(shape), dt))

    t_wbf = [[T(f"wbf{l}_{g}") for g in range(NG)] for l in range(L)]

    ident = sb("ident", [128, 128], BF16)
    ones_f = sb("ones_f", [128, 128], F32)
    eps_t = sb("eps_t", [128, 1], F32)
    cst = sb("cst", [128, NCST], F32)
    esink = sb("esink", [128, 8], F32)
    lamt = sb("lamt", [128, 8], F32)
    xld = [sb(f"xld{i}", [128, D], F32) for i in range(2)]
    xr = [sb(f"xr{i}", [128, D], F32) for i in range(2)]
    t_xr = [T("xr0"), T("xr1")]
    hbf0_ = sb("hbf0", [128, D], BF16)
    hbf = [hbf0_, hbf0_]
    nrm = [sb(f"nrm{i}", [128, 4], F32) for i in range(2)]
    hT = sb("hT", [128, 8, 512], BF16)
    wsl = [sb(f"wsl{i}", [128, GSZ], BF16) for i in range(NSLOT)]
    qk_f = sb("qk_f", [128, 512], F32)
    qk_sq = sb("qk_sq", [128, 512], F32)
    qk_ss = sb("qk_ss", [128, 8], F32)
    qk_n = sb("qk_n", [128, 512], BF16)
    mT = sb("mT", [128, 8, 512], BF16)
    aqT = mT[:, 0:4, :]
    akT2 = sb("akT2", [128, 2, 640], BF16)
    av1 = sb("av1", [128, 5, 2, 65], BF16)
    sag = sb("sag", [128, 4, 512], BF16)
    scg = sb("scg", [128, 4, 512], BF16)
    cqT = mT[:, 4:8, :]
    ckT = sb("ckT", [128, 4, Tn], BF16)
    cv1_flat = sb("cv1", [128, NB * 516], BF16)
    cv1 = cv1_flat[:].rearrange("p (a b c) -> p a b c", b=4, c=129)
    u_ext = sb("u_ext", [128, 4, 542], F32)
    sigb = sb("sigb", [128, 512], F32)
    sbg = sb("sbg", [128, 4, 512], BF16)
    vcv = sb("vcv", [128, 4, 512], F32)
    mean = sigb
    rstd = sb("rstd", [128, 512], F32)
    tmpf = sb("tmpf", [128, 512], F32)
    PTc = [[sb(f"PTc{i}{c}", [128, 512], BF16) for c in range(2)] for i in range(2)]
    PTa = PTc[1]
    den = sb("den", [128, 16], F32)
    ya_f = tmpf
    ya_bf = sb("ya_bf", [128, 512], BF16)
    yaT = sb("yaT", [128, 4, 512], BF16)
    ybT = sbg
    ycT = sb("ycT", [128, 4, 512], BF16)
    dsm = sb("dsm", [128, 8], F32)
    d_t = sb("d_t", [128, 128], F32)
    ones_b = sb("ones_b", [128, 128], BF16)
    gt = sb("gt", [128, 3, 512], F32)
    e_a = gt[:, 0, :]
    e_b = gt[:, 1, :]
    mt0 = qk_f
    mt1 = qk_sq

    pb = [es.enter_context(nc.psum_tensor(f"pb{i}", [128, 512], F32)) for i in range(8)]
    t_pb = [T(f"pb{i}") for i in range(8)]

    def TT(*names):
        return [T(n) for n in names]

    (t_ident, t_ones, t_cst, t_der, t_hT, t_qkf, t_qksq, t_qkss, t_qkn, t_aqT, t_sag, t_scg, t_cqT,
     t_sigb, t_sbg, t_mean, t_rstd, t_tmpf, t_den, t_yaf, t_yabf, t_yaT, t_ybT, t_ycT, t_dsm, t_dt, t_dd,
     t_dj, t_ycbf, t_gt, t_mT, t_mt0, t_mt1, t_zrow, t_ones_cols) = TT(
        "ident", "ones", "cst", "der", "hT", "qkf", "qksq", "qkss", "qkn", "aqT", "sag", "scg", "cqT",
        "sigb", "sbg", "mean", "rstd", "tmpf", "den", "yaf", "yabf", "yaT", "ybT", "ycT", "dsm", "dt", "dd",
        "dj", "ycbf", "gt", "mT", "mt0", "mt1", "zrow", "ones_cols")
    t_mt0, t_mt1 = t_qkf, t_qksq
    t_cgate = T("cgate")
    t_ea = t_eb = t_gt
    qkbufs = [(qk_f, qk_sq, qk_ss, qk_n, t_qkf, t_qksq, t_qkss, t_qkn),
              (sb("qk_f2", [128, 512], F32), sb("qk_sq2", [128, 512], F32), sb("qk_ss2", [128, 8], F32),
               sb("qk_n2", [128, 512], BF16), T("qkf2"), T("qksq2"), T("qkss2"), T("qkn2"))]
    qkpar = [0]
    t_ybT = t_sbg
    t_hTb = [T(f"hT{b}") for b in range(4)]
    t_mean = t_sigb
    t_dsmq = [T(f"dsmq{i}") for i in range(4)]
    t_ddq = [T(f"ddq{i}") for i in range(4)]
    t_ss4 = T("ss4")
    t_yaf = t_tmpf
    t_xld = TT("xld0", "xld1")
    t_hbf = [T("hbf0")] * 2
    t_nrm = TT("nrm0", "nrm1")
    t_wsl = [T(f"wsl{i}") for i in range(NSLOT)]
    t_akT = [T(f"akT{i}") for i in range(5)]
    t_av1 = [T(f"av1{i}") for i in range(5)]
    t_ckT = [T(f"ckT{b}") for b in range(NB)]
    t_cv1 = [T(f"cv1{b}") for b in range(NB)]
    t_uext = [T(f"uext{c}") for c in range(4)]
    t_vcv = [T(f"vcv{c}") for c in range(4)]
    t_PTc = [TT("PTc00", "PTc01"), TT("PTc10", "PTc11")]
    t_PTa = t_PTc[1]
    t_x1 = [T(f"x1_{b}") for b in range(NB)]
    t_out = T("out")
    t_tap = T("tap")

    tapped = set()

    def tap(name, src_ap, reads):
        if name in tap_d and name not in tapped:
            tapped.add(name)
            P.dma("sp", P.chan("tap_" + name), lambda e: e.dma_start(out=tap_d[name], in_=src_ap),
                  reads=reads, writes=[t_tap])

    P.op("pool", lambda e: e.memset(ident[:], 0.0), writes=[t_ident])
    P.op("pool", lambda e: e.affine_select(out=ident[:], in_=ident[:], compare_op=ALU.not_equal, fill=1.0,
                                            base=0, pattern=[[-1, 128]], channel_multiplier=1),
         reads=[t_ident], writes=[t_ident])
    P.op("pool", lambda e: e.memset(ones_f[:], 1.0), writes=[t_ones])
    P.op("pool", lambda e: e.memset(ones_b[:], 1.0), writes=[t_ones])
    P.op("pool", lambda e: e.memset(eps_t[:], EPS), writes=[t_ones])
    P.op("pool", lambda e: e.memset(av1[:, :, :, 64:65], 1.0), writes=t_av1)

    wctr = [0]

    if NB >= 28:
        stage = [cv1_flat[:, (4 + 8 * i) * 516:(4 + 8 * i) * 516 + 4096].bitcast(F32) for i in range(3)]
        t_stage = [[t_cv1[b] for b in range(4 + 8 * i, 12 + 8 * i)] for i in range(3)]
    else:
        stage = [sb(f"stage{i}", [128, 2048], F32)[:] for i in range(3)]
        t_stage = [[T(f"stage{i}")] for i in range(3)]
    sctr = [0]

    def wload(l, g, cast):
        i = wctr[0] % NSLOT
        wctr[0] += 1
        if not cast:
            src = wbf_d[l, g, :, :]
            P.dma("sp", P.chan(f"wsl{i}"), lambda e: e.dma_start(out=wsl[i][:], in_=src),
                  reads=[t_wbf[l][g]], writes=[t_wsl[i]], cost=5.0)
            return i
        for hf in range(2):
            si = sctr[0] % 3
            sctr[0] += 1
            src = wpk_d[l, g, :, hf * 2048:(hf + 1) * 2048]
            P.dma("sp", P.chan(f"stage{si}"), lambda e: e.dma_start(out=stage[si], in_=src),
                  writes=t_stage[si], cost=5.0)
            dst = wsl[i][:, hf * 2048:(hf + 1) * 2048]
            if hf == 0:
                P.op("act", lambda e: e.activation(out=dst, in_=stage[si], func=AF.Copy),
                     reads=t_stage[si], writes=[t_wsl[i]], cost=1.9)
            else:
                P.op("pool", lambda e: e.tensor_copy(out=dst, in_=stage[si]),
                     reads=t_stage[si], writes=[t_wsl[i]], cost=7.0)
        dstd = wbf_d[l, g, :, :]
        P.dma("sp", P.chan(f"wst{i}"), lambda e: e.dma_start(out=dstd, in_=wsl[i][:]),
              reads=[t_wsl[i]], writes=[t_wbf[l][g]], cost=4.0)
        return i

    bankctr = [0]

    def nextbank(lo, hi):
        b = lo + bankctr[0] % (hi - lo)
        bankctr[0] += 1
        return b

    def mm_group(out_ap, pairs, t_out_, reads, skip=False, extra_writes=()):
        n = len(pairs)

        def fn(e):
            ins = None
            for i, (lt, rh) in enumerate(pairs):
                ins = e.matmul(out_ap, lt, rh, start=(i == 0), stop=(i == n - 1))
            return ins
        ncol = 1
        for d_ in pairs[0][1].shape[1:]:
            ncol *= d_
        fp32 = pairs[0][1].dtype == F32
        P.op("pe", fn, reads=reads, writes=[t_out_] + list(extra_writes), cost=n * (0.06 + ncol / 2400.0 * (4 if fp32 else 1)))

    def pow_rstd(eng, out_ap, in_ap, scale, reads, writes):
        P.op("act", lambda e: e.activation(out=out_ap, in_=in_ap, func=AF.Ln, scale=scale, bias=eps_t[:, 0:1]),
             reads=list(reads) + [t_ones], writes=writes)
        P.op("act", lambda e: e.activation(out=out_ap, in_=out_ap, func=AF.Exp, scale=-0.5), reads=writes, writes=writes)

    for l in range(L):
        last = (l == L - 1)
        first = (l == 0)
        lam_init = 0.8 - 0.6 * math.exp(-0.3 * l)
        xin_d = x_d if first else x1_d
        xout_d = out_d if last else x1_d

        def xin_reads(b):
            return [] if first else [t_x1[b]]

        def xout_writes(b):
            return [t_out] if last else [t_x1[b]]

        P.dma("sp", P.chan("cst"), lambda e, l=l: e.dma_start(out=cst[:], in_=cpk_d[l, :, :]), writes=[t_cst])
        P.op("act", lambda e: e.activation(out=esink[:], in_=cst[:, C_SINK:C_SINK + 8], func=AF.Exp),
             reads=[t_cst], writes=[t_der])
        P.op("dve", lambda e: e.tensor_tensor(out=d_t[:, 0:128], in0=cst[:, C_LQ:C_LQ + 128],
                                              in1=cst[:, C_LK:C_LK + 128], op=ALU.mult),
             reads=[t_cst], writes=[t_dt])
        P.op("dve", lambda e: e.reduce_sum(out=lamt[:, 0:2], in_=d_t[:, 0:128].rearrange("p (a b) -> p a b", b=64),
                                           axis=AX.X), reads=[t_dt], writes=[t_der])
        P.op("act", lambda e: e.activation(out=lamt[:, 2:4], in_=lamt[:, 0:2], func=AF.Exp),
             reads=[t_der], writes=[t_der])
        P.op("dve", lambda e: e.tensor_tensor(out=lamt[:, 4:5], in0=lamt[:, 2:3], in1=lamt[:, 3:4], op=ALU.subtract),
             reads=[t_der], writes=[t_der])
        P.op("dve", lambda e, li=lam_init: e.tensor_scalar(out=lamt[:, 5:6], in0=lamt[:, 4:5], scalar1=li,
                                                           scalar2=-1.0, op0=ALU.add, op1=ALU.mult),
             reads=[t_der], writes=[t_der])
        P.op("dve", lambda e, li=lam_init: e.tensor_scalar(out=lamt[:, 6:7], in0=cst[:, C_SUBGC:C_SUBGC + 1],
                                                           scalar1=(1.0 - li), scalar2=None, op0=ALU.mult),
             reads=[t_cst], writes=[t_der])
        for c in range(4):
            P.op("pool", lambda e, c=c: e.memset(u_ext[:, c, 0:30], 0.0), writes=[t_uext[c]])

        wctr_l = []
        for ti in range(NT):
            seq = list(range(NG))
            slot_of = {}
            pend = [(l, g) for g in seq]
            issued = [0]

            def ensure(k):
                while issued[0] <= min(k, NG - 1):
                    g = seq[issued[0]]
                    slot_of[g] = wload(l, g, ti == 0)
                    issued[0] += 1

            ensure(NSLOT - 2)
            for b in range(4):
                blk = ti * 4 + b
                xi = blk % 2
                rows = slice(blk * 128, (blk + 1) * 128)
                P.dma("pool", P.chan(f"xld{xi}"), lambda e, xi=xi, rows=rows: e.dma_start(out=xld[xi][:], in_=xin_d[rows, :]),
                      reads=xin_reads(blk), writes=[t_xld[xi]])
                P.op("act", lambda e, xi=xi: e.activation(out=hbf[xi][:], in_=xld[xi][:], func=AF.Square,
                                                          accum_out=nrm[xi][:, 0:1]),
                     reads=[t_xld[xi]], writes=[t_hbf[xi], t_nrm[xi]])
                pow_rstd("dve", nrm[xi][:, 1:2], nrm[xi][:, 0:1], 1.0 / D, [t_nrm[xi]], [t_nrm[xi]])
                P.op("dve", lambda e, xi=xi: e.scalar_tensor_tensor(out=hbf[xi][:], in0=xld[xi][:], scalar=nrm[xi][:, 1:2],
                                                                    in1=cst[:, C_GBC:C_GBC + D], op0=ALU.mult, op1=ALU.mult),
                     reads=[t_xld[xi], t_nrm[xi], t_cst], writes=[t_hbf[xi]])
                bk = b % 2
                pT = pb[bk][:].bitcast(BF16)

                def ftr(e, xi=xi, pT=pT):
                    ins = None
                    for k in range(8):
                        ins = e.transpose(pT[:, k * 128:(k + 1) * 128], hbf[xi][:, k * 128:(k + 1) * 128], ident[:])
                    return ins
                P.op("pe", ftr, reads=[t_hbf[xi], t_ident], writes=[t_pb[bk]])
                P.op("act", lambda e, b=b, pT=pT: e.activation(out=hT[:, :, b * 128:(b + 1) * 128],
                                                               in_=pT.rearrange("p (k t) -> p k t", t=128), func=AF.Copy),
                     reads=[t_pb[bk]], writes=[t_hTb[b]], cost=0.9)

            tap("hT", hT[:], t_hTb)
            def tm_group(gk, ncols, post):
                ensure(gk + NSLOT - 1)
                s = slot_of[seq[gk]]
                wv = wsl[s][:].rearrange("p (k c) -> p k c", c=512)
                for b in range(4):
                    bk = nextbank(2, 8)
                    mm_group(pb[bk][:, 0:ncols],
                             [(hT[:, k, b * 128:(b + 1) * 128], wv[:, k, 0:ncols]) for k in range(8)],
                             t_pb[bk], [t_hTb[b], t_wsl[s]])
                    post(b, bk)

            def qk_norm(bk, ncols, gcol, nh):
                nonlocal qk_f, qk_sq, qk_ss, qk_n, t_qkf, t_qksq, t_qkss, t_qkn
                (qk_f, qk_sq, qk_ss, qk_n, t_qkf, t_qksq, t_qkss, t_qkn) = qkbufs[qkpar[0] % 2]
                qkpar[0] += 1
                P.op("act", lambda e: e.activation(out=qk_f[:, 0:ncols], in_=pb[bk][:, 0:ncols], func=AF.Copy),
                     reads=[t_pb[bk]], writes=[t_qkf])
                P.op("act", lambda e: e.activation(out=qk_sq[:, 0:ncols], in_=pb[bk][:, 0:ncols], func=AF.Square),
                     reads=[t_pb[bk]], writes=[t_qksq])
                P.op("dve", lambda e: e.reduce_sum(out=qk_ss[:, 0:nh],
                                                   in_=qk_sq[:, 0:ncols].rearrange("p (h d) -> p h d", d=64), axis=AX.X),
                     reads=[t_qksq], writes=[t_qkss])
                pow_rstd("dve", qk_ss[:, 0:nh], qk_ss[:, 0:nh], 1.0 / 64, [t_qkss], [t_qkss])
                P.op("dve", lambda e: e.tensor_tensor(
                    out=qk_sq[:, 0:ncols].rearrange("p (h d) -> p h d", d=64),
                    in0=qk_f[:, 0:ncols].rearrange("p (h d) -> p h d", d=64),
                    in1=qk_ss[:, 0:nh].unsqueeze(2).to_broadcast([128, nh, 64]), op=ALU.mult),
                    reads=[t_qkf, t_qkss], writes=[t_qksq])

            def post_aq(b, bk):
                qk_norm(bk, 512, C_AQG, 8)
                P.op("dve", lambda e: e.tensor_tensor(
                    out=qk_n[:].rearrange("p (h d) -> p h d", d=64),
                    in0=qk_sq[:].rearrange("p (h d) -> p h d", d=64),
                    in1=cst[:, C_AQG:C_AQG + 64].unsqueeze(1).to_broadcast([128, 8, 64]), op=ALU.mult),
                    reads=[t_qksq, t_cst], writes=[t_qkn])
                tb = 0 if b % 2 == 0 else 1
                pT = pb[tb][:].bitcast(BF16)

                def ftr(e):
                    ins = None
                    for c in range(4):
                        ins = e.transpose(pT[:, c * 128:(c + 1) * 128], qk_n[:, c * 128:(c + 1) * 128], ident[:])
                    return ins
                P.op("pe", ftr, reads=[t_qkn, t_ident], writes=[t_pb[tb]])
                P.op("act", lambda e: e.activation(out=aqT[:, :, b * 128:(b + 1) * 128],
                                                   in_=pT[:, 0:512].rearrange("p (c t) -> p c t", t=128), func=AF.Copy),
                     reads=[t_pb[tb]], writes=[t_aqT])
            tm_group(0, 512, post_aq)

            def post_akv(b, bk):
                qk_norm(bk, 128, C_AKG, 2)
                for dup in range(2):
                    P.op("dve", lambda e, dup=dup: e.tensor_tensor(
                        out=qk_n[:, 0:256].rearrange("p (h u d) -> p h u d", u=2, d=64)[:, :, dup, :],
                        in0=qk_sq[:, 0:128].rearrange("p (h d) -> p h d", d=64),
                        in1=cst[:, C_AKG:C_AKG + 64].unsqueeze(1).to_broadcast([128, 2, 64]), op=ALU.mult),
                        reads=[t_qksq, t_cst], writes=[t_qkn])
                tb = 0 if b % 2 == 0 else 1
                pT = pb[tb][:].bitcast(BF16)

                def ftr(e):
                    ins = None
                    for c in range(2):
                        ins = e.transpose(pT[:, c * 128:(c + 1) * 128], qk_n[:, c * 128:(c + 1) * 128], ident[:])
                    return ins
                P.op("pe", ftr, reads=[t_qkn, t_ident], writes=[t_pb[tb]])
                P.op("act", lambda e: e.activation(out=akT2[:, :, (b + 1) * 128:(b + 2) * 128],
                                                   in_=pT[:, 0:256].rearrange("p (c t) -> p c t", t=128), func=AF.Copy),
                     reads=[t_pb[tb]], writes=[t_akT[b + 1]])
                P.op("act", lambda e: e.activation(out=av1[:, b + 1, :, 0:64],
                                                   in_=pb[bk][:, 128:256].rearrange("p (j d) -> p j d", d=64), func=AF.Copy),
                     reads=[t_pb[bk]], writes=[t_av1[b + 1]])
            tm_group(1, 256, post_akv)

            def post_ag(b, bk):
                P.op("act", lambda e: e.activation(out=sag[:, b, :], in_=pb[bk][:, :], func=AF.Silu),
                     reads=[t_pb[bk]], writes=[t_sag])
            tm_group(2, 512, post_ag)

            def fm_chunk(s, col0, bk):
                wv = wsl[s][:].rearrange("p (k c) -> p k c", c=512)
                mm_group(pb[bk][:, :], [(wv[:, k, col0:col0 + 128], hT[:, k, :]) for k in range(8)],
                         t_pb[bk], t_hTb + [t_wsl[s]])

            for gi in range(2):
                ensure(3 + gi + NSLOT - 1)
                s = slot_of[seq[3 + gi]]
                for cc in range(2):
                    c = gi * 2 + cc
                    ba = nextbank(2, 8)
                    fm_chunk(s, cc * 256, ba)
                    bb = nextbank(2, 8)
                    fm_chunk(s, cc * 256 + 128, bb)
                    P.op("act", lambda e, bb=bb: e.activation(out=sigb[:], in_=pb[bb][:, :], func=AF.Sigmoid),
                         reads=[t_pb[bb]], writes=[t_sigb])
                    P.op("dve", lambda e, c=c, ba=ba: e.tensor_tensor(out=u_ext[:, c, 30:542], in0=pb[ba][:, :], in1=sigb[:],
                                                                      op=ALU.mult),
                         reads=[t_pb[ba], t_sigb], writes=[t_uext[c]])
            ensure(5 + NSLOT - 1)
            s = slot_of[seq[5]]
            for c in range(4):
                bk = nextbank(2, 8)
                fm_chunk(s, c * 128, bk)
                P.op("act", lambda e, c=c, bk=bk: e.activation(out=sbg[:, c, :], in_=pb[bk][:, :], func=AF.Silu),
                     reads=[t_pb[bk]], writes=[t_sbg])

            def post_cq(b, bk):
                qk_norm(bk, 512, C_DQG, 8)
                P.op("dve", lambda e: e.tensor_tensor(
                    out=qk_n[:].rearrange("p (h d) -> p h d", d=64),
                    in0=qk_sq[:].rearrange("p (h d) -> p h d", d=64),
                    in1=cst[:, C_DQG:C_DQG + 64].unsqueeze(1).to_broadcast([128, 8, 64]), op=ALU.mult),
                    reads=[t_qksq, t_cst], writes=[t_qkn])
                tb = 0 if b % 2 == 0 else 1
                pT = pb[tb][:].bitcast(BF16)

                def ftr(e):
                    ins = None
                    for c in range(4):
                        ins = e.transpose(pT[:, c * 128:(c + 1) * 128], qk_n[:, c * 128:(c + 1) * 128], ident[:])
                    return ins
                P.op("pe", ftr, reads=[t_qkn, t_ident], writes=[t_pb[tb]])
                P.op("act", lambda e: e.activation(out=cqT[:, :, b * 128:(b + 1) * 128],
                                                   in_=pT[:, 0:512].rearrange("p (c t) -> p c t", t=128), func=AF.Copy),
                     reads=[t_pb[tb]], writes=[t_cqT])
            tm_group(6, 512, post_cq)

            def post_ck(b, bk):
                blk = ti * 4 + b
                qk_norm(bk, 512, C_DKG, 8)
                P.op("dve", lambda e: e.tensor_tensor(
                    out=qk_n[:].rearrange("p (h d) -> p h d", d=64),
                    in0=qk_sq[:].rearrange("p (h d) -> p h d", d=64),
                    in1=cst[:, C_DKG:C_DKG + 64].unsqueeze(1).to_broadcast([128, 8, 64]), op=ALU.mult),
                    reads=[t_qksq, t_cst], writes=[t_qkn])
                tb = 0 if b % 2 == 0 else 1
                pT = pb[tb][:].bitcast(BF16)

                def ftr(e):
                    ins = None
                    for c in range(4):
                        ins = e.transpose(pT[:, c * 128:(c + 1) * 128], qk_n[:, c * 128:(c + 1) * 128], ident[:])
                    return ins
                P.op("pe", ftr, reads=[t_qkn, t_ident], writes=[t_pb[tb]])
                P.op("act", lambda e: e.activation(out=ckT[:, :, blk * 128:(blk + 1) * 128],
                                                   in_=pT[:, 0:512].rearrange("p (c t) -> p c t", t=128), func=AF.Copy),
                     reads=[t_pb[tb]], writes=[t_ckT[blk]])
            tm_group(7, 512, post_ck)

            def post_cv(b, bk):
                blk = ti * 4 + b
                P.op("act", lambda e: e.activation(out=cv1[:, blk, :, 0:128],
                                                   in_=pb[bk][:, :].rearrange("p (h d) -> p h d", d=128), func=AF.Copy),
                     reads=[t_pb[bk]], writes=[t_cv1[blk]])
                P.op("pool", lambda e: e.memset(cv1[:, blk, :, 128:129], 1.0), writes=[t_cv1[blk]], cost=0.2)
            tm_group(8, 512, post_cv)

            ensure(9 + NSLOT - 1)
            s = slot_of[seq[9]]
            for c in range(4):
                bk = nextbank(2, 8)
                fm_chunk(s, c * 128, bk)
                P.op("act", lambda e, c=c, bk=bk: e.activation(out=scg[:, c, :], in_=pb[bk][:, :], func=AF.Silu),
                     reads=[t_pb[bk]], writes=[t_scg])
            ensure(10 + 1)

            tap("aqT", aqT[:], [t_aqT])
            tap("akT2", akT2[:], t_akT)
            tap("av1", av1[:], t_av1)
            tap("sag", sag[:], [t_sag])
            tap("scg", scg[:], [t_scg])
            tap("cqT", cqT[:], [t_cqT])
            tap("ckT", ckT[:, :, 0:512], t_ckT)
            tap("cv1", cv1[:, 0:4], t_cv1)
            tap("u_ext", u_ext[:], t_uext)
            tap("sbg", sbg[:], [t_sbg])
            if ti == 0:
                pass
            for b in range(4):
                blk = ti * 4 + b
                sbs = ([] if blk == 0 else [0]) + [1]
                O = [pb[4], pb[5]]
                for j in range(2):
                    for hf in range(2):
                        bk = hf
                        for sbi in sbs:
                            slot = b + sbi
                            mm_group(pb[bk][:, sbi * 256:(sbi + 1) * 256].rearrange("p (c t) -> p c t", t=128),
                                     [(akT2[hf * 64:(hf + 1) * 64, j, slot * 128:(slot + 1) * 128],
                                       aqT[hf * 64:(hf + 1) * 64, 2 * j:2 * j + 2, b * 128:(b + 1) * 128])],
                                     t_pb[bk], [t_akT[slot], t_aqT])
                        lo = sbs[0] * 256
                        P.op("act", lambda e, hf=hf, bk=bk, lo=lo: e.activation(out=PTa[hf][:, lo:512], in_=pb[bk][:, lo:512],
                                                                                func=AF.Exp, scale=0.125),
                             reads=[t_pb[bk]], writes=[t_PTa[hf]])
                        if 0 in sbs:
                            P.op("pool", lambda e, hf=hf: e.affine_select(
                                out=PTa[hf][:, 0:256], in_=PTa[hf][:, 0:256], compare_op=ALU.is_ge, fill=0.0,
                                base=-1, pattern=[[0, 2], [-1, 128]], channel_multiplier=1),
                                reads=[t_PTa[hf]], writes=[t_PTa[hf]])
                        P.op("pool", lambda e, hf=hf: e.affine_select(
                            out=PTa[hf][:, 256:512], in_=PTa[hf][:, 256:512], compare_op=ALU.is_ge, fill=0.0,
                            base=0, pattern=[[0, 2], [1, 128]], channel_multiplier=-1),
                            reads=[t_PTa[hf]], writes=[t_PTa[hf]])
                    for hf in range(2):
                        for cc in range(2):
                            hl = 2 * cc + hf
                            pairs = []
                            rds = [t_PTa[hf]]
                            for sbi in sbs:
                                slot = b + sbi
                                pairs.append((PTa[hf][:, sbi * 256 + cc * 128: sbi * 256 + (cc + 1) * 128],
                                              av1[:, slot, j, :]))
                                rds.append(t_av1[slot])
                            mm_group(pb[4 + j][:, hl * 65:(hl + 1) * 65], pairs, t_pb[4 + j], rds)
                for j in range(2):
                    Ov = pb[4 + j][:, 0:260].rearrange("p (h e) -> p h e", e=65)
                    P.op("dve", lambda e, j=j, Ov=Ov: e.tensor_tensor(
                        out=den[:, j * 4:(j + 1) * 4].unsqueeze(2), in0=Ov[:, :, 64:65],
                        in1=esink[:, j * 4:(j + 1) * 4].unsqueeze(2), op=ALU.add),
                        reads=[t_pb[4 + j], t_der], writes=[t_den])
                    P.op("dve", lambda e, j=j: e.reciprocal(out=den[:, 8 + j * 4:8 + (j + 1) * 4], in_=den[:, j * 4:(j + 1) * 4]),
                         reads=[t_den], writes=[t_den])
                    P.op("dve", lambda e, j=j, Ov=Ov: e.tensor_tensor(
                        out=ya_f[:, j * 256:(j + 1) * 256].rearrange("p (h d) -> p h d", d=64), in0=Ov[:, :, 0:64],
                        in1=den[:, 8 + j * 4:8 + (j + 1) * 4].unsqueeze(2).to_broadcast([128, 4, 64]), op=ALU.mult),
                        reads=[t_pb[4 + j], t_den], writes=[t_yaf])
                P.op("dve", lambda e, b=b: e.tensor_tensor(out=ya_bf[:], in0=ya_f[:], in1=sag[:, b, :], op=ALU.mult),
                     reads=[t_yaf, t_sag], writes=[t_yabf])
                pT = pb[6][:].bitcast(BF16)

                def ftr(e, pT=pT):
                    ins = None
                    for c in range(4):
                        ins = e.transpose(pT[:, c * 128:(c + 1) * 128], ya_bf[:, c * 128:(c + 1) * 128], ident[:])
                    return ins
                P.op("pe", ftr, reads=[t_yabf, t_ident], writes=[t_pb[6]])
                P.op("act", lambda e, b=b, pT=pT: e.activation(out=yaT[:, :, b * 128:(b + 1) * 128],
                                                               in_=pT[:, 0:512].rearrange("p (c t) -> p c t", t=128), func=AF.Copy),
                     reads=[t_pb[6]], writes=[t_yaT])
            P.op("pool", lambda e: e.tensor_copy(out=akT2[:, :, 0:128], in_=akT2[:, :, 512:640]),
                 reads=[t_akT[4]], writes=[t_akT[0]])
            P.op("pool", lambda e: e.tensor_copy(out=av1[:, 0, :, 0:64], in_=av1[:, 4, :, 0:64]),
                 reads=[t_av1[4]], writes=[t_av1[0]])

            tap("yaT", yaT[:], [t_yaT])
            nsb = ti * 4 + 4
            for h in range(4):
                for jb in range(nsb):
                    pi = jb % 2
                    qb0 = max(0, jb - ti * 4)
                    q0 = qb0 * 128
                    def fsc(e, pi=pi, q0=q0, jb=jb, h=h):
                        ins = None
                        for c in range(2):
                            ins = e.matmul(pb[pi * 2 + c][:, q0:512], ckT[c * 64:(c + 1) * 64, h, jb * 128:(jb + 1) * 128],
                                           cqT[c * 64:(c + 1) * 64, h, q0:512], start=True, stop=True)
                        return ins
                    P.op("pe", fsc, reads=[t_ckT[jb], t_cqT],
                         writes=[t_pb[pi * 2], t_pb[pi * 2 + 1]] + ([t_cgate] if (h == 0 and jb == 0) else []))
                    for c in range(2):
                        bk = pi * 2 + c
                        P.op("act", lambda e, pi=pi, c=c, bk=bk, q0=q0: e.activation(
                            out=PTc[pi][c][:, q0:512], in_=pb[bk][:, q0:512], func=AF.Exp, scale=0.125),
                            reads=[t_pb[bk]], writes=[t_PTc[pi][c]])
                        if jb >= ti * 4:
                            P.op("pool", lambda e, pi=pi, c=c, q0=q0: e.affine_select(
                                out=PTc[pi][c][:, q0:q0 + 128], in_=PTc[pi][c][:, q0:q0 + 128], compare_op=ALU.is_ge,
                                fill=0.0, base=0, pattern=[[1, 128]], channel_multiplier=-1),
                                reads=[t_PTc[pi][c]], writes=[t_PTc[pi][c]])

                    def fpv(e, pi=pi, q0=q0, jb=jb, h=h, nsb=nsb):
                        ins = None
                        for c in range(2):
                            e.matmul(pb[4 + c][:, q0:512], cv1[:, jb, h, 0:128], PTc[pi][c][:, q0:512],
                                     start=(jb == 0), stop=(jb == nsb - 1))
                        for c in range(2):
                            ins = e.matmul(pb[6 + c][:, q0:512], ones_b[:], PTc[pi][c][:, q0:512],
                                           start=(jb == 0), stop=(jb == nsb - 1))
                        return ins
                    P.op("pe", fpv, reads=[t_PTc[pi][0], t_PTc[pi][1], t_cv1[jb], t_ones],
                         writes=[t_pb[4], t_pb[5], t_pb[6], t_pb[7]])
                P.op("dve", lambda e: e.reciprocal(out=e_a[:], in_=pb[6][:, :]), reads=[t_pb[6]], writes=[t_ea])
                P.op("dve", lambda e: e.reciprocal(out=e_b[:], in_=pb[7][:, :]), reads=[t_pb[7]], writes=[t_eb])
                P.op("dve", lambda e: e.tensor_tensor(out=e_a[:], in0=pb[4][:, :], in1=e_a[:], op=ALU.mult),
                     reads=[t_pb[4], t_ea], writes=[t_ea])
                P.op("dve", lambda e: e.tensor_tensor(out=e_b[:], in0=pb[5][:, :], in1=e_b[:], op=ALU.mult),
                     reads=[t_pb[5], t_eb], writes=[t_eb])
                P.op("dve", lambda e: e.scalar_tensor_tensor(out=e_a[:], in0=e_b[:], scalar=lamt[:, 5:6], in1=e_a[:],
                                                             op0=ALU.mult, op1=ALU.add),
                     reads=[t_ea, t_eb, t_der], writes=[t_ea], cost=0.65)
                P.op("act", lambda e: e.activation(out=e_b[:], in_=e_a[:], func=AF.Square), reads=[t_ea], writes=[t_eb])
                mm_group(pb[3][:, :], [(ones_f[:], e_b[:])], t_pb[3], [t_ones, t_eb])
                pow_rstd("dve", e_b[:], pb[3][:, :], 1.0 / 128, [t_pb[3]], [t_eb])
                P.op("dve", lambda e: e.tensor_tensor(out=e_a[:], in0=e_a[:], in1=e_b[:], op=ALU.mult),
                     reads=[t_ea, t_eb], writes=[t_ea])
                P.op("dve", lambda e, h=h: e.scalar_tensor_tensor(out=ycT[:, h, :], in0=e_a[:], scalar=lamt[:, 6:7], in1=scg[:, h, :],
                                                                  op0=ALU.mult, op1=ALU.mult),
                     reads=[t_ea, t_der, t_scg], writes=[t_ycT], cost=0.65)

            tap("ycT", ycT[:], [t_ycT])
            for c in range(4):
                eng = "dve"
                P.op(eng, lambda e, c=c: e.tensor_scalar(out=vcv[:, c, :], in0=u_ext[:, c, 0:512],
                                                         scalar1=cst[:, C_CONVW + c * 31:C_CONVW + c * 31 + 1],
                                                         scalar2=cst[:, C_CONVB + c:C_CONVB + c + 1], op0=ALU.mult, op1=ALU.add),
                     reads=[t_uext[c], t_cst, t_cgate], writes=[t_vcv[c]], prio=CONV_PRIO)
                for jj in range(1, 31):
                    P.op(eng, lambda e, c=c, jj=jj: e.scalar_tensor_tensor(
                        out=vcv[:, c, :], in0=u_ext[:, c, jj:jj + 512],
                        scalar=cst[:, C_CONVW + c * 31 + jj:C_CONVW + c * 31 + jj + 1], in1=vcv[:, c, :],
                        op0=ALU.mult, op1=ALU.add), reads=[t_uext[c], t_cst, t_vcv[c]], writes=[t_vcv[c]], prio=CONV_PRIO, cost=0.65)
                P.op(eng, lambda e, c=c: e.tensor_copy(out=u_ext[:, c, 0:30], in_=u_ext[:, c, 512:542]),
                     reads=[t_uext[c]], writes=[t_uext[c]])
            mm_group(pb[0][:, :], [(ones_f[:], vcv[:, c, :]) for c in range(4)], t_pb[0], [t_ones] + t_vcv)
            P.op("dve", lambda e: e.tensor_scalar(out=mean[:], in0=pb[0][:, :], scalar1=1.0 / 512, scalar2=None, op0=ALU.mult),
                 reads=[t_pb[0]], writes=[t_mean])
            for c in range(4):
                P.op("dve", lambda e, c=c: e.tensor_tensor(out=vcv[:, c, :], in0=vcv[:, c, :], in1=mean[:], op=ALU.subtract),
                     reads=[t_vcv[c], t_mean], writes=[t_vcv[c]])
            for c in range(4):
                P.op("act", lambda e, c=c: e.activation(out=(tmpf if c % 2 == 0 else sigb)[:], in_=vcv[:, c, :], func=AF.Square),
                     reads=[t_vcv[c]], writes=[t_tmpf if c % 2 == 0 else t_sigb])
                P.op("pe", lambda e, c=c: e.matmul(pb[1][:, :], ones_f[:], (tmpf if c % 2 == 0 else sigb)[:],
                                                   start=(c == 0), stop=(c == 3)),
                     reads=[t_ones, t_tmpf if c % 2 == 0 else t_sigb], writes=[t_pb[1]])
            pow_rstd("dve", rstd[:], pb[1][:, :], 1.0 / 512, [t_pb[1]], [t_rstd])
            for c in range(4):
                P.op("dve", lambda e, c=c: e.tensor_tensor(out=vcv[:, c, :], in0=vcv[:, c, :], in1=rstd[:], op=ALU.mult),
                     reads=[t_vcv[c], t_rstd], writes=[t_vcv[c]])
                P.op("act", lambda e, c=c: e.activation(out=vcv[:, c, :], in_=vcv[:, c, :], func=AF.Silu,
                                                        scale=cst[:, C_LNG + c:C_LNG + c + 1], bias=cst[:, C_LNB + c:C_LNB + c + 1]),
                     reads=[t_vcv[c], t_cst], writes=[t_vcv[c]])
                P.op("dve", lambda e, c=c: e.tensor_tensor(out=ybT[:, c, :], in0=vcv[:, c, :], in1=sbg[:, c, :], op=ALU.mult),
                     reads=[t_vcv[c], t_sbg], writes=[t_ybT])

            tap("ybT", ybT[:], [t_ybT])
            yTs = [(yaT, t_yaT), (ybT, t_ybT), (ycT, t_ycT)]
            for mp in range(4):
                gk_wop = 10 + mp * 3
                ensure(gk_wop + NSLOT - 1)
                s_w = slot_of[seq[gk_wop]]
                wop = wsl[s_w][:, 0:3072].rearrange("p (m i k c) -> p m i k c", m=2, i=3, k=4)
                for mm in range(2):
                    m = mp * 2 + mm
                    gk_mg = gk_wop + 1 + mm
                    s_g = slot_of[seq[gk_mg]]
                    mgv = wsl[s_g][:, 0:3072].rearrange("p (k c) -> p k c", c=384)
                    for i in range(3):
                        mm_group(pb[i][:, :], [(mgv[:, k, i * 128:(i + 1) * 128], hT[:, k, :]) for k in range(8)],
                                 t_pb[i], t_hTb + [t_wsl[s_g]])
                        P.op("act", lambda e, i=i: e.activation(out=gt[:, i, :], in_=pb[i][:, :], func=AF.Sigmoid),
                             reads=[t_pb[i]], writes=[t_gt])
                    for i in range(3):
                        yT_, ty_ = yTs[i]
                        mm_group(pb[3 + i][:, :], [(wop[:, mm, i, k, :], yT_[:, k, :]) for k in range(4)],
                                 t_pb[3 + i], [ty_, t_wsl[s_w]])
                    P.op("dve", lambda e: e.tensor_tensor(out=mt0[:], in0=pb[3][:, :], in1=gt[:, 0, :], op=ALU.mult),
                         reads=[t_pb[3], t_gt], writes=[t_mt0])
                    P.op("dve", lambda e: e.tensor_tensor(out=mt1[:], in0=pb[4][:, :], in1=gt[:, 1, :], op=ALU.mult),
                         reads=[t_pb[4], t_gt], writes=[t_mt1])
                    P.op("dve", lambda e: e.tensor_tensor(out=mt0[:], in0=mt0[:], in1=mt1[:], op=ALU.add),
                         reads=[t_mt0, t_mt1], writes=[t_mt0])
                    P.op("dve", lambda e: e.tensor_tensor(out=mt1[:], in0=pb[5][:, :], in1=gt[:, 2, :], op=ALU.mult),
                         reads=[t_pb[5], t_gt], writes=[t_mt1])
                    P.op("dve", lambda e, m=m: e.tensor_tensor(out=mT[:, m, :], in0=mt0[:], in1=mt1[:], op=ALU.add),
                         reads=[t_mt0, t_mt1], writes=[t_aqT if m < 4 else t_cqT])

            tap("mT", mT[:], [t_aqT, t_cqT])
            ensure(23)
            s0 = slot_of[seq[22]]
            s1 = slot_of[seq[23]]
            for b in range(4):
                blk = ti * 4 + b
                xi = blk % 2
                rows = slice(blk * 128, (blk + 1) * 128)
                P.dma("pool", P.chan(f"xr{xi}"), lambda e, xi=xi, rows=rows: e.dma_start(out=xr[xi][:], in_=xin_d[rows, :]),
                      reads=xin_reads(blk), writes=[t_xr[xi]])
                for n, s in enumerate((s0, s1)):
                    bk = 6 + n
                    wv = wsl[s][:].rearrange("p (k c) -> p k c", c=512)
                    mm_group(pb[bk][:, :], [(mT[:, k, b * 128:(b + 1) * 128], wv[:, k, :]) for k in range(8)],
                             t_pb[bk], [t_aqT, t_cqT, t_wsl[s]])
                    P.op("dve", lambda e, xi=xi, n=n, bk=bk: e.tensor_tensor(out=xr[xi][:, n * 512:(n + 1) * 512],
                                                                             in0=pb[bk][:, :], in1=xr[xi][:, n * 512:(n + 1) * 512], op=ALU.add),
                         reads=[t_pb[bk], t_xr[xi]], writes=[t_xr[xi]])
                P.dma("pool", P.chan(f"xst{xi}"), lambda e, xi=xi, rows=rows: e.dma_start(out=xout_d[rows, :], in_=xr[xi][:]),
                      reads=[t_xr[xi]], writes=xout_writes(blk))

    P.emit(schedule=SCHEDULE)
    P.close()
    es.close()
    return nc


def pack_weights(w_in, w_o_attn, w_o_conv, w_o_diff, w_out):
    L = w_in.shape[0]
    wpk = np.zeros((L, NG, 128, GSZ), np.float32)
    for l in range(L):
        wk = w_in[l].reshape(8, 128, 7936).transpose(1, 0, 2)

        def put(g, cols):
            blk = wk[:, :, cols]
            n = blk.shape[2]
            wpk[l, g].reshape(128, 8, 512)[:, :, :n] = blk
        ar = np.arange
        put(0, ar(0, 512))
        put(1, ar(512, 768))
        put(2, ar(768, 1280))
        for gi in range(2):
            cols = []
            for cc in range(2):
                c = gi * 2 + cc
                cols += list(range(1280 + c * 128, 1280 + (c + 1) * 128))
                cols += list(range(1792 + c * 128, 1792 + (c + 1) * 128))
            put(3 + gi, np.array(cols))
        put(5, ar(2304, 2816))
        put(6, ar(2816, 3328))
        put(7, ar(3328, 3840))
        put(8, ar(3840, 4352))
        put(9, ar(4352, 4864))
        wos = [w.reshape(4, 128, 1024).transpose(1, 0, 2) for w in (w_o_attn[l], w_o_conv[l], w_o_diff[l])]
        for mp in range(4):
            g = 10 + mp * 3
            v = wpk[l, g][:, 0:3072].reshape(128, 2, 3, 4, 128)
            for mm in range(2):
                m = mp * 2 + mm
                for i in range(3):
                    v[:, mm, i, :, :] = wos[i][:, :, m * 128:(m + 1) * 128]
                cols = np.concatenate([ar(4864 + i * 1024 + m * 128, 4864 + i * 1024 + (m + 1) * 128) for i in range(3)])
                wpk[l, g + 1 + mm][:, 0:3072].reshape(128, 8, 384)[:] = wk[:, :, cols]
        wo = w_out[l].reshape(8, 128, 1024).transpose(1, 0, 2)
        for n in range(2):
            wpk[l, 22 + n].reshape(128, 8, 512)[:] = wo[:, :, n * 512:(n + 1) * 512]
    return wpk


def pack_consts(norm_g, attn_q_norm_g, attn_k_norm_g, attn_sinks, conv_w, conv_b, conv_norm_g, conv_norm_b,
                diff_q_norm_g, diff_k_norm_g, lambda_q, lambda_k, diff_subln_g):
    L = norm_g.shape[0]
    c = np.zeros((L, 128, NCST), np.float32)
    for l in range(L):
        c[l, :, C_GBC:C_GBC + 1024] = norm_g[l][None, :]
        c[l, :, C_AQG:C_AQG + 64] = attn_q_norm_g[l][None, :]
        c[l, :, C_AKG:C_AKG + 64] = attn_k_norm_g[l][None, :]
        c[l, :, C_DQG:C_DQG + 64] = diff_q_norm_g[l][None, :]
        c[l, :, C_DKG:C_DKG + 64] = diff_k_norm_g[l][None, :]
        c[l, :, C_SUBG:C_SUBG + 128] = diff_subln_g[l][None, :]
        c[l, :, C_SINK:C_SINK + 8] = attn_sinks[l][None, :]
        c[l, :, C_CONVW:C_CONVW + 124] = conv_w[l].reshape(31, 4, 128).transpose(2, 1, 0).reshape(128, 124)
        c[l, :, C_CONVB:C_CONVB + 4] = conv_b[l].reshape(4, 128).T
        c[l, :, C_LNG:C_LNG + 4] = conv_norm_g[l].reshape(4, 128).T
        c[l, :, C_LNB:C_LNB + 4] = conv_norm_b[l].reshape(4, 128).T
        c[l, :, C_LQ:C_LQ + 128] = lambda_q[l].reshape(1, 128)
        c[l, :, C_LK:C_LK + 128] = lambda_k[l].reshape(1, 128)
        c[l, :, C_SUBGC] = diff_subln_g[l]
    return c


_NC_CACHE = {}


def kernel(x, norm_g, w_in, attn_q_norm_g, attn_k_norm_g, attn_sinks, w_o_attn,
           conv_w, conv_b, conv_norm_g, conv_norm_b, w_o_conv,
           diff_q_norm_g, diff_k_norm_g, lambda_q, lambda_k, diff_subln_g, w_o_diff, w_out):
    f = lambda a: np.ascontiguousarray(np.asarray(a, dtype=np.float32))
    x = f(x)
    B, Tn, _ = x.shape
    L = int(np.asarray(norm_g).shape[0])
    wpk = pack_weights(f(w_in), f(w_o_attn), f(w_o_conv), f(w_o_diff), f(w_out))
    cpk = pack_consts(f(norm_g), f(attn_q_norm_g), f(attn_k_norm_g), f(attn_sinks), f(conv_w), f(conv_b),
                      f(conv_norm_g), f(conv_norm_b), f(diff_q_norm_g), f(diff_k_norm_g), f(lambda_q),
                      f(lambda_k), f(diff_subln_g))
    key = (Tn, L)
    if key not in _NC_CACHE:
        _NC_CACHE[key] = build(Tn, L)
    nc = _NC_CACHE[key]
    in_maps = [{"x": x[i], "cpk": cpk, "wpk": wpk} for i in range(B)]
    res = run_bass_kernel_spmd(nc, in_maps, core_ids=list(range(B)))
    return np.stack([np.asarray(r["out"], dtype=np.float32) for r in res.results], axis=0)
```

```python
import contextlib
import math
import types
import numpy as np
import concourse.bass as bass
import concourse.mybir as mybir
from concourse.bass_utils import run_bass_kernel_spmd

F32 = mybir.dt.float32
BF16 = mybir.dt.bfloat16
AF = mybir.ActivationFunctionType
ALU = mybir.AluOpType
AX = mybir.AxisListType

ENGS = ("pe", "act", "dve", "pool", "sp")
SCHEDULE = True
CHECK = True
CONV_PRIO = 2500
NSLOT = 4
ATTACH_WAITS = True
EPS = 1e-6
D = 1024
NG = 24
GSZ = 4096
C_GBC, C_AQG, C_AKG, C_DQG, C_DKG, C_SUBG, C_SINK, C_CONVW, C_CONVB, C_LNG, C_LNB, C_LQ, C_LK, C_SUBGC, NCST = (
    0, 1024, 1088, 1152, 1216, 1280, 1408, 1416, 1540, 1544, 1548, 1552, 1680, 1808, 1812)


class T:
    __slots__ = ("name", "w", "rs")

    def __init__(self, name):
        self.name = name
        self.w = None
        self.rs = []


def _freeze(fn):
    if fn is None or fn.__closure__ is None:
        return fn
    cells = []
    for c in fn.__closure__:
        try:
            cells.append(types.CellType(c.cell_contents))
        except ValueError:
            cells.append(c)
    g = types.FunctionType(fn.__code__, fn.__globals__, fn.__name__, fn.__defaults__, tuple(cells))
    g.__kwdefaults__ = fn.__kwdefaults__
    return g


class Op:
    __slots__ = ("id", "eng", "fn", "chan", "deps", "cost", "lat", "succ", "nun", "fin", "cnt", "semkey", "key", "attach")

    def __init__(self, id_, eng, fn, chan, cost, lat):
        self.id = id_
        self.attach = False
        self.key = id_
        self.eng = eng
        self.fn = fn
        self.chan = chan
        self.deps = {}
        self.cost = cost
        self.lat = lat
        self.succ = []
        self.nun = 0
        self.fin = 0.0
        self.cnt = 0
        self.semkey = None


DEFAULT_COST = {"pe": 0.25, "act": 0.5, "dve": 0.55, "pool": 0.9, "sp": 0.1}


class _MockIns:
    def then_inc(self, *a, **k):
        return self

    def _wait_ge(self, *a, **k):
        return self


def _fsize(ap):
    n = 1
    for d_ in ap.shape[1:]:
        n *= d_
    return n


class _MockEng:
    def __init__(self, eng):
        self.eng = eng
        self.cost = 0.0
        self.ncalls = 0
        self.accum = False

    def __getattr__(self, name):
        def call(*args, **kw):
            eng = self.eng
            self.ncalls += 1
            if kw.get("accum_out") is not None:
                self.accum = True
            if eng == "pe":
                if name == "matmul":
                    rhs = args[2] if len(args) > 2 else kw["rhs"]
                    n = _fsize(rhs)
                    self.cost += 0.02 + n / 2000.0 * (4.0 if rhs.dtype == F32 else 1.0)
                else:
                    self.cost += 0.11
            else:
                out = kw.get("out", args[0] if args else None)
                n = _fsize(out) if out is not None else 64
                if eng == "act":
                    self.cost += 0.2 + n / 1200.0 + (0.1 if kw.get("accum_out") is not None else 0.0)
                elif eng == "dve":
                    self.cost += 0.17 + n / 960.0
                elif eng == "pool":
                    fp32 = out is not None and out.dtype == F32
                    self.cost += 0.2 + n * (0.0032 if fp32 else 0.0012)
                else:
                    self.cost += 0.1
            return _MockIns()
        return call


class Prog:
    def __init__(self, nc):
        self.nc = nc
        self.sems = {}
        self._stack = []
        self.segs = [[]]
        self.nops = 0
        for e in ENGS:
            self._mksem("E_" + e)

    def _mksem(self, key):
        cm = self.nc.semaphore(key)
        h = cm.__enter__()
        self._stack.append(cm)
        self.sems[key] = h
        return h

    def chan(self, name):
        key = "D_" + name
        if key not in self.sems:
            self._mksem(key)
        return key

    def _record(self, o, reads, writes):
        is_dma = o.chan is not None

        def add(d, raw):
            if d is None or d is o:
                return
            need = raw or is_dma or (d.chan is not None) or (d.eng != o.eng) or (o.eng != "pe")
            if need or d not in o.deps:
                o.deps[d] = need or o.deps.get(d, False)
        for t in reads:
            add(t.w, True)
        for t in writes:
            add(t.w, o.eng != "pe")
            for r in t.rs:
                add(r, False)
        for t in reads:
            t.rs.append(o)
        for t in writes:
            t.w = o
            t.rs = []
        self.segs[-1].append(o)

    def op(self, eng, fn, reads=(), writes=(), cost=None, prio=0):
        fn = _freeze(fn)
        att = False
        try:
            m = _MockEng(eng)
            fn(m)
            c = m.cost
            att = ATTACH_WAITS and eng != "pe" and m.ncalls == 1 and not m.accum
        except Exception:
            c = DEFAULT_COST[eng] if cost is None else cost
        o = Op(self.nops, eng, fn, None, c, c + 0.2)
        o.attach = att
        o.key = o.id + prio
        self.nops += 1
        self._record(o, reads, writes)
        return o

    def dma(self, eng, chan, fn, reads=(), writes=(), cost=3.0):
        o = Op(self.nops, eng, _freeze(fn), chan, 0.1 if eng == "sp" else 1.0, cost)
        o.attach = ATTACH_WAITS
        self.nops += 1
        self._record(o, reads, writes)
        return o

    def barrier(self):
        self.segs.append([])

    def _schedule(self, ops):
        import heapq
        inseg = set(o.id for o in ops)
        for o in ops:
            o.succ = []
        for o in ops:
            n = 0
            for d in o.deps:
                if d.id in inseg:
                    d.succ.append(o)
                    n += 1
            o.nun = n
        fut = {e: [] for e in ENGS}
        avail = {e: [] for e in ENGS}
        free = {e: 0.0 for e in ENGS}
        order = {e: [] for e in ENGS}

        def push(o):
            rt = 0.0
            for d in o.deps:
                if d.id in inseg and d.fin > rt:
                    rt = d.fin
            heapq.heappush(fut[o.eng], (rt, o.id, o))
        for o in ops:
            if o.nun == 0:
                push(o)
        left = len(ops)
        while left:
            best = None
            for e in ENGS:
                f, a = fut[e], avail[e]
                while f and f[0][0] <= free[e]:
                    rt, i, o = heapq.heappop(f)
                    heapq.heappush(a, (o.key, i, o))
                if a:
                    st = free[e]
                elif f:
                    st = f[0][0]
                else:
                    continue
                if best is None or st < best[0]:
                    best = (st, e)
            st, e = best
            if avail[e]:
                _k, i, o = heapq.heappop(avail[e])
            else:
                rt, i, o = heapq.heappop(fut[e])
            free[e] = st + o.cost
            o.fin = st + o.lat
            order[e].append(o)
            left -= 1
            for s_ in o.succ:
                s_.nun -= 1
                if s_.nun == 0:
                    push(s_)
        return order, max(free.values())

    def emit(self, schedule=True):
        nc = self.nc
        sems = self.sems
        cnt = {k: 0 for k in sems}
        streams = {e: [] for e in ENGS}
        waited = {e: {} for e in ENGS}
        self.est = 0.0
        for si, seg in enumerate(self.segs):
            if schedule:
                order, est = self._schedule(seg)
                self.est += est
            else:
                order = {e: [o for o in seg if o.eng == e] for e in ENGS}
            for e in ENGS:
                for o in order[e]:
                    if o.chan is None:
                        o.semkey = "E_" + e
                        cnt[o.semkey] += 1
                    else:
                        o.semkey = o.chan
                        cnt[o.semkey] += 16
                    o.cnt = cnt[o.semkey]
            for e in ENGS:
                wd = waited[e]
                for o in order[e]:
                    need = {}
                    for d, ns in o.deps.items():
                        if not ns:
                            continue
                        if need.get(d.semkey, 0) < d.cnt:
                            need[d.semkey] = d.cnt
                    waits = []
                    for k, v in need.items():
                        if wd.get(k, 0) < v:
                            wd[k] = v
                            waits.append((k, v))
                    streams[e].append((waits, o.fn, (o.semkey, 1 if o.chan is None else 16), o))
            for e in ENGS:
                waits = []
                for k, v in cnt.items():
                    if v > 0 and waited[e].get(k, 0) < v:
                        waited[e][k] = v
                        waits.append((k, v))
                streams[e].append((waits, None, None, None))
        self.streams = streams
        if CHECK:
            self.check(streams)

        def replay(e, name):
            for waits, fn, inc, o_ in streams[name]:
                if fn is None:
                    for k, v in waits:
                        e.wait_ge(sems[k], v)
                    continue
                if o_.attach and waits:
                    for k, v in waits[:-1]:
                        e.wait_ge(sems[k], v)
                    ins = fn(e)
                    k, v = waits[-1]
                    ins._wait_ge(sems[k], v)
                else:
                    for k, v in waits:
                        e.wait_ge(sems[k], v)
                    ins = fn(e)
                ins.then_inc(sems[inc[0]], inc[1])

        with nc.Block() as block:
            @block.tensor
            def _(e):
                replay(e, "pe")

            @block.scalar
            def _(e):
                replay(e, "act")

            @block.vector
            def _(e):
                replay(e, "dve")

            @block.gpsimd
            def _(e):
                replay(e, "pool")

            @block.sync
            def _(e):
                replay(e, "sp")

    def check(self, streams):
        val = {k: 0 for k in self.sems}
        ptr = {e: 0 for e in ENGS}
        done = set()
        total = sum(len(v) for v in streams.values())
        n = 0
        while n < total:
            prog = False
            for e in ENGS:
                st = streams[e]
                while ptr[e] < len(st):
                    waits, fn, inc, o = st[ptr[e]]
                    if any(val[k] < v for k, v in waits):
                        break
                    if o is not None:
                        for d in o.deps:
                            assert d.id in done, ("race", e, o.id, d.id, d.eng)
                        done.add(o.id)
                        val[inc[0]] += inc[1]
                        assert val[inc[0]] == o.cnt, ("count mismatch", e, o.id, inc, val[inc[0]], o.cnt)
                    ptr[e] += 1
                    n += 1
                    prog = True
            if not prog:
                info = {e: (ptr[e], len(streams[e]), streams[e][ptr[e]][0] if ptr[e] < len(streams[e]) else None) for e in ENGS}
                raise RuntimeError(f"deadlock: {info} vals={ {k: val[k] for e in ENGS for k, v in (streams[e][ptr[e]][0] if ptr[e] < len(streams[e]) else [])} }")

    def close(self):
        while self._stack:
            self._stack.pop().__exit__(None, None, None)


def build(Tn=4096, L=2, taps=()):
    NB = Tn // 128
    NT = Tn // 512
    nc = bass.Bass("TRN2", target_bir_lowering=False)
    x_d = nc.dram_tensor("x", [Tn, D], F32, kind="ExternalInput").ap()
    cpk_d = nc.dram_tensor("cpk", [L, 128, NCST], F32, kind="ExternalInput").ap()
    wpk_d = nc.dram_tensor("wpk", [L, NG, 128, GSZ], F32, kind="ExternalInput").ap()
    out_d = nc.dram_tensor("out", [Tn, D], F32, kind="ExternalOutput").ap()
    wbf_d = nc.dram_tensor("wbf", [L, NG, 128, GSZ], BF16, kind="Internal").ap()
    x1_d = nc.dram_tensor("x1s", [Tn, D], F32, kind="Internal").ap()
    tap_d = {}
    for name, shape, dt_ in taps:
        tap_d[name] = nc.dram_tensor("tap_" + name, list(shape), dt_, kind="ExternalOutput").ap()

    P = Prog(nc)
    es = contextlib.ExitStack()

    def sb(name, shape, dt, stack=es):
        return stack.enter_context(nc.sbuf_tensor(name, list(shape), dt))

    t_wbf = [[T(f"wbf{l}_{g}") for g in range(NG)] for l in range(L)]

    ident = sb("ident", [128, 128], BF16)
    ones_f = sb("ones_f", [128, 128], F32)
    eps_t = sb("eps_t", [128, 1], F32)
    cst = sb("cst", [128, NCST], F32)
    esink = sb("esink", [128, 8], F32)
    lamt = sb("lamt", [128, 8], F32)
    xld = [sb(f"xld{i}", [128, D], F32) for i in range(2)]
    xr = [sb(f"xr{i}", [128, D], F32) for i in range(2)]
    t_xr = [T("xr0"), T("xr1")]
    hbf0_ = sb("hbf0", [128, D], BF16)
    hbf = [hbf0_, hbf0_]
    nrm = [sb(f"nrm{i}", [128, 4], F32) for i in range(2)]
    hT = sb("hT", [128, 8, 512], BF16)
    wsl = [sb(f"wsl{i}", [128, GSZ], BF16) for i in range(NSLOT)]
    qk_f = sb("qk_f", [128, 512], F32)
    qk_sq = sb("qk_sq", [128, 512], F32)
    qk_ss = sb("qk_ss", [128, 8], F32)
    qk_n = sb("qk_n", [128, 512], BF16)
    mT = sb("mT", [128, 8, 512], BF16)
    aqT = mT[:, 0:4, :]
    akT2 = sb("akT2", [128, 2, 640], BF16)
    av1 = sb("av1", [128, 5, 2, 65], BF16)
    sag = sb("sag", [128, 4, 512], BF16)
    scg = sb("scg", [128, 4, 512], BF16)
    cqT = mT[:, 4:8, :]
    ckT = sb("ckT", [128, 4, Tn], BF16)
    cv1_flat = sb("cv1", [128, NB * 516], BF16)
    cv1 = cv1_flat[:].rearrange("p (a b c) -> p a b c", b=4, c=129)
    u_ext = sb("u_ext", [128, 4, 542], F32)
    sigb = sb("sigb", [128, 512], F32)
    sbg = sb("sbg", [128, 4, 512], BF16)
    vcv = sb("vcv", [128, 4, 512], F32)
    mean = sigb
    rstd = sb("rstd", [128, 512], F32)
    tmpf = sb("tmpf", [128, 512], F32)
    PTc = [[sb(f"PTc{i}{c}", [128, 512], BF16) for c in range(2)] for i in range(2)]
    PTa = PTc[1]
    den = sb("den", [128, 16], F32)
    ya_f = tmpf
    ya_bf = sb("ya_bf", [128, 512], BF16)
    yaT = sb("yaT", [128, 4, 512], BF16)
    ybT = sbg
    ycT = sb("ycT", [128, 4, 512], BF16)
    dsm = sb("dsm", [128, 8], F32)
    d_t = sb("d_t", [128, 128], F32)
    ones_b = sb("ones_b", [128, 128], BF16)
    gt = sb("gt", [128, 3, 512], F32)
    e_a = gt[:, 0, :]
    e_b = gt[:, 1, :]
    mt0 = qk_f
    mt1 = qk_sq

    pb = [es.enter_context(nc.psum_tensor(f"pb{i}", [128, 512], F32)) for i in range(8)]
    t_pb = [T(f"pb{i}") for i in range(8)]

    def TT(*names):
        return [T(n) for n in names]

    (t_ident, t_ones, t_cst, t_der, t_hT, t_qkf, t_qksq, t_qkss, t_qkn, t_aqT, t_sag, t_scg, t_cqT,
     t_sigb, t_sbg, t_mean, t_rstd, t_tmpf, t_den, t_yaf, t_yabf, t_yaT, t_ybT, t_ycT, t_dsm, t_dt, t_dd,
     t_dj, t_ycbf, t_gt, t_mT, t_mt0, t_mt1, t_zrow, t_ones_cols) = TT(
        "ident", "ones", "cst", "der", "hT", "qkf", "qksq", "qkss", "qkn", "aqT", "sag", "scg", "cqT",
        "sigb", "sbg", "mean", "rstd", "tmpf", "den", "yaf", "yabf", "yaT", "ybT", "ycT", "dsm", "dt", "dd",
        "dj", "ycbf", "gt", "mT", "mt0", "mt1", "zrow", "ones_cols")
    t_mt0, t_mt1 = t_qkf, t_qksq
    t_cgate = T("cgate")
    t_ea = t_eb = t_gt
    qkbufs = [(qk_f, qk_sq, qk_ss, qk_n, t_qkf, t_qksq, t_qkss, t_qkn),
              (sb("qk_f2", [128, 512], F32), sb("qk_sq2", [128, 512], F32), sb("qk_ss2", [128, 8], F32),
               sb("qk_n2", [128, 512], BF16), T("qkf2"), T("qksq2"), T("qkss2"), T("qkn2"))]
    qkpar = [0]
    t_ybT = t_sbg
    t_hTb = [T(f"hT{b}") for b in range(4)]
    t_mean = t_sigb
    t_dsmq = [T(f"dsmq{i}") for i in range(4)]
    t_ddq = [T(f"ddq{i}") for i in range(4)]
    t_ss4 = T("ss4")
    t_yaf = t_tmpf
    t_xld = TT("xld0", "xld1")
    t_hbf = [T("hbf0")] * 2
    t_nrm = TT("nrm0", "nrm1")
    t_wsl = [T(f"wsl{i}") for i in range(NSLOT)]
    t_akT = [T(f"akT{i}") for i in range(5)]
    t_av1 = [T(f"av1{i}") for i in range(5)]
    t_ckT = [T(f"ckT{b}") for b in range(NB)]
    t_cv1 = [T(f"cv1{b}") for b in range(NB)]
    t_uext = [T(f"uext{c}") for c in range(4)]
    t_vcv = [T(f"vcv{c}") for c in range(4)]
    t_PTc = [TT("PTc00", "PTc01"), TT("PTc10", "PTc11")]
    t_PTa = t_PTc[1]
    t_x1 = [T(f"x1_{b}") for b in range(NB)]
    t_out = T("out")
    t_tap = T("tap")

    tapped = set()

    def tap(name, src_ap, reads):
        if name in tap_d and name not in tapped:
            tapped.add(name)
            P.dma("sp", P.chan("tap_" + name), lambda e: e.dma_start(out=tap_d[name], in_=src_ap),
                  reads=reads, writes=[t_tap])

    P.op("pool", lambda e: e.memset(ident[:], 0.0), writes=[t_ident])
    P.op("pool", lambda e: e.affine_select(out=ident[:], in_=ident[:], compare_op=ALU.not_equal, fill=1.0,
                                            base=0, pattern=[[-1, 128]], channel_multiplier=1),
         reads=[t_ident], writes=[t_ident])
    P.op("pool", lambda e: e.memset(ones_f[:], 1.0), writes=[t_ones])
    P.op("pool", lambda e: e.memset(ones_b[:], 1.0), writes=[t_ones])
    P.op("pool", lambda e: e.memset(eps_t[:], EPS), writes=[t_ones])
    P.op("pool", lambda e: e.memset(av1[:, :, :, 64:65], 1.0), writes=t_av1)

    wctr = [0]

    if NB >= 28:
        stage = [cv1_flat[:, (4 + 8 * i) * 516:(4 + 8 * i) * 516 + 4096].bitcast(F32) for i in range(3)]
        t_stage = [[t_cv1[b] for b in range(4 + 8 * i, 12 + 8 * i)] for i in range(3)]
    else:
        stage = [sb(f"stage{i}", [128, 2048], F32)[:] for i in range(3)]
        t_stage = [[T(f"stage{i}")] for i in range(3)]
    sctr = [0]

    def wload(l, g, cast):
        i = wctr[0] % NSLOT
        wctr[0] += 1
        if not cast:
            src = wbf_d[l, g, :, :]
            P.dma("sp", P.chan(f"wsl{i}"), lambda e: e.dma_start(out=wsl[i][:], in_=src),
                  reads=[t_wbf[l][g]], writes=[t_wsl[i]], cost=5.0)
            return i
        for hf in range(2):
            si = sctr[0] % 3
            sctr[0] += 1
            src = wpk_d[l, g, :, hf * 2048:(hf + 1) * 2048]
            P.dma("sp", P.chan(f"stage{si}"), lambda e: e.dma_start(out=stage[si], in_=src),
                  writes=t_stage[si], cost=5.0)
            dst = wsl[i][:, hf * 2048:(hf + 1) * 2048]
            if hf == 0:
                P.op("act", lambda e: e.activation(out=dst, in_=stage[si], func=AF.Copy),
                     reads=t_stage[si], writes=[t_wsl[i]], cost=1.9)
            else:
                P.op("pool", lambda e: e.tensor_copy(out=dst, in_=stage[si]),
                     reads=t_stage[si], writes=[t_wsl[i]], cost=7.0)
        dstd = wbf_d[l, g, :, :]
        P.dma("sp", P.chan(f"wst{i}"), lambda e: e.dma_start(out=dstd, in_=wsl[i][:]),
              reads=[t_wsl[i]], writes=[t_wbf[l][g]], cost=4.0)
        return i

    bankctr = [0]

    def nextbank(lo, hi):
        b = lo + bankctr[0] % (hi - lo)
        bankctr[0] += 1
        return b

    def mm_group(out_ap, pairs, t_out_, reads, skip=False, extra_writes=()):
        n = len(pairs)

        def fn(e):
            ins = None
            for i, (lt, rh) in enumerate(pairs):
                ins = e.matmul(out_ap, lt, rh, start=(i == 0), stop=(i == n - 1))
            return ins
        ncol = 1
        for d_ in pairs[0][1].shape[1:]:
            ncol *= d_
        fp32 = pairs[0][1].dtype == F32
        P.op("pe", fn, reads=reads, writes=[t_out_] + list(extra_writes), cost=n * (0.06 + ncol / 2400.0 * (4 if fp32 else 1)))

    def pow_rstd(eng, out_ap, in_ap, scale, reads, writes):
        P.op("act", lambda e: e.activation(out=out_ap, in_=in_ap, func=AF.Ln, scale=scale, bias=eps_t[:, 0:1]),
             reads=list(reads) + [t_ones], writes=writes)
        P.op("act", lambda e: e.activation(out=out_ap, in_=out_ap, func=AF.Exp, scale=-0.5), reads=writes, writes=writes)

    for l in range(L):
        last = (l == L - 1)
        first = (l == 0)
        lam_init = 0.8 - 0.6 * math.exp(-0.3 * l)
        xin_d = x_d if first else x1_d
        xout_d = out_d if last else x1_d

        def xin_reads(b):
            return [] if first else [t_x1[b]]

        def xout_writes(b):
            return [t_out] if last else [t_x1[b]]

        P.dma("sp", P.chan("cst"), lambda e, l=l: e.dma_start(out=cst[:], in_=cpk_d[l, :, :]), writes=[t_cst])
        P.op("act", lambda e: e.activation(out=esink[:], in_=cst[:, C_SINK:C_SINK + 8], func=AF.Exp),
             reads=[t_cst], writes=[t_der])
        P.op("dve", lambda e: e.tensor_tensor(out=d_t[:, 0:128], in0=cst[:, C_LQ:C_LQ + 128],
                                              in1=cst[:, C_LK:C_LK + 128], op=ALU.mult),
             reads=[t_cst], writes=[t_dt])
        P.op("dve", lambda e: e.reduce_sum(out=lamt[:, 0:2], in_=d_t[:, 0:128].rearrange("p (a b) -> p a b", b=64),
                                           axis=AX.X), reads=[t_dt], writes=[t_der])
        P.op("act", lambda e: e.activation(out=lamt[:, 2:4], in_=lamt[:, 0:2], func=AF.Exp),
             reads=[t_der], writes=[t_der])
        P.op("dve", lambda e: e.tensor_tensor(out=lamt[:, 4:5], in0=lamt[:, 2:3], in1=lamt[:, 3:4], op=ALU.subtract),
             reads=[t_der], writes=[t_der])
        P.op("dve", lambda e, li=lam_init: e.tensor_scalar(out=lamt[:, 5:6], in0=lamt[:, 4:5], scalar1=li,
                                                           scalar2=-1.0, op0=ALU.add, op1=ALU.mult),
             reads=[t_der], writes=[t_der])
        P.op("dve", lambda e, li=lam_init: e.tensor_scalar(out=lamt[:, 6:7], in0=cst[:, C_SUBGC:C_SUBGC + 1],
                                                           scalar1=(1.0 - li), scalar2=None, op0=ALU.mult),
             reads=[t_cst], writes=[t_der])
        for c in range(4):
            P.op("pool", lambda e, c=c: e.memset(u_ext[:, c, 0:30], 0.0), writes=[t_uext[c]])

        wctr_l = []
        for ti in range(NT):
            seq = list(range(NG))
            slot_of = {}
            pend = [(l, g) for g in seq]
            issued = [0]

            def ensure(k):
                while issued[0] <= min(k, NG - 1):
                    g = seq[issued[0]]
                    slot_of[g] = wload(l, g, ti == 0)
                    issued[0] += 1

            ensure(NSLOT - 2)
            for b in range(4):
                blk = ti * 4 + b
                xi = blk % 2
                rows = slice(blk * 128, (blk + 1) * 128)
                P.dma("pool", P.chan(f"xld{xi}"), lambda e, xi=xi, rows=rows: e.dma_start(out=xld[xi][:], in_=xin_d[rows, :]),
                      reads=xin_reads(blk), writes=[t_xld[xi]])
                P.op("act", lambda e, xi=xi: e.activation(out=hbf[xi][:], in_=xld[xi][:], func=AF.Square,
                                                          accum_out=nrm[xi][:, 0:1]),
                     reads=[t_xld[xi]], writes=[t_hbf[xi], t_nrm[xi]])
                pow_rstd("dve", nrm[xi][:, 1:2], nrm[xi][:, 0:1], 1.0 / D, [t_nrm[xi]], [t_nrm[xi]])
                P.op("dve", lambda e, xi=xi: e.scalar_tensor_tensor(out=hbf[xi][:], in0=xld[xi][:], scalar=nrm[xi][:, 1:2],
                                                                    in1=cst[:, C_GBC:C_GBC + D], op0=ALU.mult, op1=ALU.mult),
                     reads=[t_xld[xi], t_nrm[xi], t_cst], writes=[t_hbf[xi]])
                bk = b % 2
                pT = pb[bk][:].bitcast(BF16)

                def ftr(e, xi=xi, pT=pT):
                    ins = None
                    for k in range(8):
                        ins = e.transpose(pT[:, k * 128:(k + 1) * 128], hbf[xi][:, k * 128:(k + 1) * 128], ident[:])
                    return ins
                P.op("pe", ftr, reads=[t_hbf[xi], t_ident], writes=[t_pb[bk]])
                P.op("act", lambda e, b=b, pT=pT: e.activation(out=hT[:, :, b * 128:(b + 1) * 128],
                                                               in_=pT.rearrange("p (k t) -> p k t", t=128), func=AF.Copy),
                     reads=[t_pb[bk]], writes=[t_hTb[b]], cost=0.9)

            tap("hT", hT[:], t_hTb)
            def tm_group(gk, ncols, post):
                ensure(gk + NSLOT - 1)
                s = slot_of[seq[gk]]
                wv = wsl[s][:].rearrange("p (k c) -> p k c", c=512)
                for b in range(4):
                    bk = nextbank(2, 8)
                    mm_group(pb[bk][:, 0:ncols],
                             [(hT[:, k, b * 128:(b + 1) * 128], wv[:, k, 0:ncols]) for k in range(8)],
                             t_pb[bk], [t_hTb[b], t_wsl[s]])
                    post(b, bk)

            def qk_norm(bk, ncols, gcol, nh):
                nonlocal qk_f, qk_sq, qk_ss, qk_n, t_qkf, t_qksq, t_qkss, t_qkn
                (qk_f, qk_sq, qk_ss, qk_n, t_qkf, t_qksq, t_qkss, t_qkn) = qkbufs[qkpar[0] % 2]
                qkpar[0] += 1
                P.op("act", lambda e: e.activation(out=qk_f[:, 0:ncols], in_=pb[bk][:, 0:ncols], func=AF.Copy),
                     reads=[t_pb[bk]], writes=[t_qkf])
                P.op("act", lambda e: e.activation(out=qk_sq[:, 0:ncols], in_=pb[bk][:, 0:ncols], func=AF.Square),
                     reads=[t_pb[bk]], writes=[t_qksq])
                P.op("dve", lambda e: e.reduce_sum(out=qk_ss[:, 0:nh],
                                                   in_=qk_sq[:, 0:ncols].rearrange("p (h d) -> p h d", d=64), axis=AX.X),
                     reads=[t_qksq], writes=[t_qkss])
                pow_rstd("dve", qk_ss[:, 0:nh], qk_ss[:, 0:nh], 1.0 / 64, [t_qkss], [t_qkss])
                P.op("dve", lambda e: e.tensor_tensor(
                    out=qk_sq[:, 0:ncols].rearrange("p (h d) -> p h d", d=64),
                    in0=qk_f[:, 0:ncols].rearrange("p (h d) -> p h d", d=64),
                    in1=qk_ss[:, 0:nh].unsqueeze(2).to_broadcast([128, nh, 64]), op=ALU.mult),
                    reads=[t_qkf, t_qkss], writes=[t_qksq])

            def post_aq(b, bk):
                qk_norm(bk, 512, C_AQG, 8)
                P.op("dve", lambda e: e.tensor_tensor(
                    out=qk_n[:].rearrange("p (h d) -> p h d", d=64),
                    in0=qk_sq[:].rearrange("p (h d) -> p h d", d=64),
                    in1=cst[:, C_AQG:C_AQG + 64].unsqueeze(1).to_broadcast([128, 8, 64]), op=ALU.mult),
                    reads=[t_qksq, t_cst], writes=[t_qkn])
                tb = 0 if b % 2 == 0 else 1
                pT = pb[tb][:].bitcast(BF16)

                def ftr(e):
                    ins = None
                    for c in range(4):
                        ins = e.transpose(pT[:, c * 128:(c + 1) * 128], qk_n[:, c * 128:(c + 1) * 128], ident[:])
                    return ins
                P.op("pe", ftr, reads=[t_qkn, t_ident], writes=[t_pb[tb]])
                P.op("act", lambda e: e.activation(out=aqT[:, :, b * 128:(b + 1) * 128],
                                                   in_=pT[:, 0:512].rearrange("p (c t) -> p c t", t=128), func=AF.Copy),
                     reads=[t_pb[tb]], writes=[t_aqT])
            tm_group(0, 512, post_aq)

            def post_akv(b, bk):
                qk_norm(bk, 128, C_AKG, 2)
                for dup in range(2):
                    P.op("dve", lambda e, dup=dup: e.tensor_tensor(
                        out=qk_n[:, 0:256].rearrange("p (h u d) -> p h u d", u=2, d=64)[:, :, dup, :],
                        in0=qk_sq[:, 0:128].rearrange("p (h d) -> p h d", d=64),
                        in1=cst[:, C_AKG:C_AKG + 64].unsqueeze(1).to_broadcast([128, 2, 64]), op=ALU.mult),
                        reads=[t_qksq, t_cst], writes=[t_qkn])
                tb = 0 if b % 2 == 0 else 1
                pT = pb[tb][:].bitcast(BF16)

                def ftr(e):
                    ins = None
                    for c in range(2):
                        ins = e.transpose(pT[:, c * 128:(c + 1) * 128], qk_n[:, c * 128:(c + 1) * 128], ident[:])
                    return ins
                P.op("pe", ftr, reads=[t_qkn, t_ident], writes=[t_pb[tb]])
                P.op("act", lambda e: e.activation(out=akT2[:, :, (b + 1) * 128:(b + 2) * 128],
                                                   in_=pT[:, 0:256].rearrange("p (c t) -> p c t", t=128), func=AF.Copy),
                     reads=[t_pb[tb]], writes=[t_akT[b + 1]])
                P.op("act", lambda e: e.activation(out=av1[:, b + 1, :, 0:64],
                                                   in_=pb[bk][:, 128:256].rearrange("p (j d) -> p j d", d=64), func=AF.Copy),
                     reads=[t_pb[bk]], writes=[t_av1[b + 1]])
            tm_group(1, 256, post_akv)

            def post_ag(b, bk):
                P.op("act", lambda e: e.activation(out=sag[:, b, :], in_=pb[bk][:, :], func=AF.Silu),
                     reads=[t_pb[bk]], writes=[t_sag])
            tm_group(2, 512, post_ag)

            def fm_chunk(s, col0, bk):
                wv = wsl[s][:].rearrange("p (k c) -> p k c", c=512)
                mm_group(pb[bk][:, :], [(wv[:, k, col0:col0 + 128], hT[:, k, :]) for k in range(8)],
                         t_pb[bk], t_hTb + [t_wsl[s]])

            for gi in range(2):
                ensure(3 + gi + NSLOT - 1)
                s = slot_of[seq[3 + gi]]
                for cc in range(2):
                    c = gi * 2 + cc
                    ba = nextbank(2, 8)
                    fm_chunk(s, cc * 256, ba)
                    bb = nextbank(2, 8)
                    fm_chunk(s, cc * 256 + 128, bb)
                    P.op("act", lambda e, bb=bb: e.activation(out=sigb[:], in_=pb[bb][:, :], func=AF.Sigmoid),
                         reads=[t_pb[bb]], writes=[t_sigb])
                    P.op("dve", lambda e, c=c, ba=ba: e.tensor_tensor(out=u_ext[:, c, 30:542], in0=pb[ba][:, :], in1=sigb[:],
                                                                      op=ALU.mult),
                         reads=[t_pb[ba], t_sigb], writes=[t_uext[c]])
            ensure(5 + NSLOT - 1)
            s = slot_of[seq[5]]
            for c in range(4):
                bk = nextbank(2, 8)
                fm_chunk(s, c * 128, bk)
                P.op("act", lambda e, c=c, bk=bk: e.activation(out=sbg[:, c, :], in_=pb[bk][:, :], func=AF.Silu),
                     reads=[t_pb[bk]], writes=[t_sbg])

            def post_cq(b, bk):
                qk_norm(bk, 512, C_DQG, 8)
                P.op("dve", lambda e: e.tensor_tensor(
                    out=qk_n[:].rearrange("p (h d) -> p h d", d=64),
                    in0=qk_sq[:].rearrange("p (h d) -> p h d", d=64),
                    in1=cst[:, C_DQG:C_DQG + 64].unsqueeze(1).to_broadcast([128, 8, 64]), op=ALU.mult),
                    reads=[t_qksq, t_cst], writes=[t_qkn])
                tb = 0 if b % 2 == 0 else 1
                pT = pb[tb][:].bitcast(BF16)

                def ftr(e):
                    ins = None
                    for c in range(4):
                        ins = e.transpose(pT[:, c * 128:(c + 1) * 128], qk_n[:, c * 128:(c + 1) * 128], ident[:])
                    return ins
                P.op("pe", ftr, reads=[t_qkn, t_ident], writes=[t_pb[tb]])
                P.op("act", lambda e: e.activation(out=cqT[:, :, b * 128:(b + 1) * 128],
                                                   in_=pT[:, 0:512].rearrange("p (c t) -> p c t", t=128), func=AF.Copy),
                     reads=[t_pb[tb]], writes=[t_cqT])
            tm_group(6, 512, post_cq)

            def post_ck(b, bk):
                blk = ti * 4 + b
                qk_norm(bk, 512, C_DKG, 8)
                P.op("dve", lambda e: e.tensor_tensor(
                    out=qk_n[:].rearrange("p (h d) -> p h d", d=64),
                    in0=qk_sq[:].rearrange("p (h d) -> p h d", d=64),
                    in1=cst[:, C_DKG:C_DKG + 64].unsqueeze(1).to_broadcast([128, 8, 64]), op=ALU.mult),
                    reads=[t_qksq, t_cst], writes=[t_qkn])
                tb = 0 if b % 2 == 0 else 1
                pT = pb[tb][:].bitcast(BF16)

                def ftr(e):
                    ins = None
                    for c in range(4):
                        ins = e.transpose(pT[:, c * 128:(c + 1) * 128], qk_n[:, c * 128:(c + 1) * 128], ident[:])
                    return ins
                P.op("pe", ftr, reads=[t_qkn, t_ident], writes=[t_pb[tb]])
                P.op("act", lambda e: e.activation(out=ckT[:, :, blk * 128:(blk + 1) * 128],
                                                   in_=pT[:, 0:512].rearrange("p (c t) -> p c t", t=128), func=AF.Copy),
                     reads=[t_pb[tb]], writes=[t_ckT[blk]])
            tm_group(7, 512, post_ck)

            def post_cv(b, bk):
                blk = ti * 4 + b
                P.op("act", lambda e: e.activation(out=cv1[:, blk, :, 0:128],
                                                   in_=pb[bk][:, :].rearrange("p (h d) -> p h d", d=128), func=AF.Copy),
                     reads=[t_pb[bk]], writes=[t_cv1[blk]])
                P.op("pool", lambda e: e.memset(cv1[:, blk, :, 128:129], 1.0), writes=[t_cv1[blk]], cost=0.2)
            tm_group(8, 512, post_cv)

            ensure(9 + NSLOT - 1)
            s = slot_of[seq[9]]
            for c in range(4):
                bk = nextbank(2, 8)
                fm_chunk(s, c * 128, bk)
                P.op("act", lambda e, c=c, bk=bk: e.activation(out=scg[:, c, :], in_=pb[bk][:, :], func=AF.Silu),
                     reads=[t_pb[bk]], writes=[t_scg])
            ensure(10 + 1)

            tap("aqT", aqT[:], [t_aqT])
            tap("akT2", akT2[:], t_akT)
            tap("av1", av1[:], t_av1)
            tap("sag", sag[:], [t_sag])
            tap("scg", scg[:], [t_scg])
            tap("cqT", cqT[:], [t_cqT])
            tap("ckT", ckT[:, :, 0:512], t_ckT)
            tap("cv1", cv1[:, 0:4], t_cv1)
            tap("u_ext", u_ext[:], t_uext)
            tap("sbg", sbg[:], [t_sbg])
            if ti == 0:
                pass
            for b in range(4):
                blk = ti * 4 + b
                sbs = ([] if blk == 0 else [0]) + [1]
                O = [pb[4], pb[5]]
                for j in range(2):
                    for hf in range(2):
                        bk = hf
                        for sbi in sbs:
                            slot = b + sbi
                            mm_group(pb[bk][:, sbi * 256:(sbi + 1) * 256].rearrange("p (c t) -> p c t", t=128),
                                     [(akT2[hf * 64:(hf + 1) * 64, j, slot * 128:(slot + 1) * 128],
                                       aqT[hf * 64:(hf + 1) * 64, 2 * j:2 * j + 2, b * 128:(b + 1) * 128])],
                                     t_pb[bk], [t_akT[slot], t_aqT])
                        lo = sbs[0] * 256
                        P.op("act", lambda e, hf=hf, bk=bk, lo=lo: e.activation(out=PTa[hf][:, lo:512], in_=pb[bk][:, lo:512],
                                                                                func=AF.Exp, scale=0.125),
                             reads=[t_pb[bk]], writes=[t_PTa[hf]])
                        if 0 in sbs:
                            P.op("pool", lambda e, hf=hf: e.affine_select(
                                out=PTa[hf][:, 0:256], in_=PTa[hf][:, 0:256], compare_op=ALU.is_ge, fill=0.0,
                                base=-1, pattern=[[0, 2], [-1, 128]], channel_multiplier=1),
                                reads=[t_PTa[hf]], writes=[t_PTa[hf]])
                        P.op("pool", lambda e, hf=hf: e.affine_select(
                            out=PTa[hf][:, 256:512], in_=PTa[hf][:, 256:512], compare_op=ALU.is_ge, fill=0.0,
                            base=0, pattern=[[0, 2], [1, 128]], channel_multiplier=-1),
                            reads=[t_PTa[hf]], writes=[t_PTa[hf]])
                    for hf in range(2):
                        for cc in range(2):
                            hl = 2 * cc + hf
                            pairs = []
                            rds = [t_PTa[hf]]
                            for sbi in sbs:
                                slot = b + sbi
                                pairs.append((PTa[hf][:, sbi * 256 + cc * 128: sbi * 256 + (cc + 1) * 128],
                                              av1[:, slot, j, :]))
                                rds.append(t_av1[slot])
                            mm_group(pb[4 + j][:, hl * 65:(hl + 1) * 65], pairs, t_pb[4 + j], rds)
                for j in range(2):
                    Ov = pb[4 + j][:, 0:260].rearrange("p (h e) -> p h e", e=65)
                    P.op("dve", lambda e, j=j, Ov=Ov: e.tensor_tensor(
                        out=den[:, j * 4:(j + 1) * 4].unsqueeze(2), in0=Ov[:, :, 64:65],
                        in1=esink[:, j * 4:(j + 1) * 4].unsqueeze(2), op=ALU.add),
                        reads=[t_pb[4 + j], t_der], writes=[t_den])
                    P.op("dve", lambda e, j=j: e.reciprocal(out=den[:, 8 + j * 4:8 + (j + 1) * 4], in_=den[:, j * 4:(j + 1) * 4]),
                         reads=[t_den], writes=[t_den])
                    P.op("dve", lambda e, j=j, Ov=Ov: e.tensor_tensor(
                        out=ya_f[:, j * 256:(j + 1) * 256].rearrange("p (h d) -> p h d", d=64), in0=Ov[:, :, 0:64],
                        in1=den[:, 8 + j * 4:8 + (j + 1) * 4].unsqueeze(2).to_broadcast([128, 4, 64]), op=ALU.mult),
                        reads=[t_pb[4 + j], t_den], writes=[t_yaf])
                P.op("dve", lambda e, b=b: e.tensor_tensor(out=ya_bf[:], in0=ya_f[:], in1=sag[:, b, :], op=ALU.mult),
                     reads=[t_yaf, t_sag], writes=[t_yabf])
                pT = pb[6][:].bitcast(BF16)

                def ftr(e, pT=pT):
                    ins = None
                    for c in range(4):
                        ins = e.transpose(pT[:, c * 128:(c + 1) * 128], ya_bf[:, c * 128:(c + 1) * 128], ident[:])
                    return ins
                P.op("pe", ftr, reads=[t_yabf, t_ident], writes=[t_pb[6]])
                P.op("act", lambda e, b=b, pT=pT: e.activation(out=yaT[:, :, b * 128:(b + 1) * 128],
                                                               in_=pT[:, 0:512].rearrange("p (c t) -> p c t", t=128), func=AF.Copy),
                     reads=[t_pb[6]], writes=[t_yaT])
            P.op("pool", lambda e: e.tensor_copy(out=akT2[:, :, 0:128], in_=akT2[:, :, 512:640]),
                 reads=[t_akT[4]], writes=[t_akT[0]])
            P.op("pool", lambda e: e.tensor_copy(out=av1[:, 0, :, 0:64], in_=av1[:, 4, :, 0:64]),
                 reads=[t_av1[4]], writes=[t_av1[0]])

            tap("yaT", yaT[:], [t_yaT])
            nsb = ti * 4 + 4
            for h in range(4):
                for jb in range(nsb):
                    pi = jb % 2
                    qb0 = max(0, jb - ti * 4)
                    q0 = qb0 * 128
                    def fsc(e, pi=pi, q0=q0, jb=jb, h=h):
                        ins = None
                        for c in range(2):
                            ins = e.matmul(pb[pi * 2 + c][:, q0:512], ckT[c * 64:(c + 1) * 64, h, jb * 128:(jb + 1) * 128],
                                           cqT[c * 64:(c + 1) * 64, h, q0:512], start=True, stop=True)
                        return ins
                    P.op("pe", fsc, reads=[t_ckT[jb], t_cqT],
                         writes=[t_pb[pi * 2], t_pb[pi * 2 + 1]] + ([t_cgate] if (h == 0 and jb == 0) else []))
                    for c in range(2):
                        bk = pi * 2 + c
                        P.op("act", lambda e, pi=pi, c=c, bk=bk, q0=q0: e.activation(
                            out=PTc[pi][c][:, q0:512], in_=pb[bk][:, q0:512], func=AF.Exp, scale=0.125),
                            reads=[t_pb[bk]], writes=[t_PTc[pi][c]])
                        if jb >= ti * 4:
                            P.op("pool", lambda e, pi=pi, c=c, q0=q0: e.affine_select(
                                out=PTc[pi][c][:, q0:q0 + 128], in_=PTc[pi][c][:, q0:q0 + 128], compare_op=ALU.is_ge,
                                fill=0.0, base=0, pattern=[[1, 128]], channel_multiplier=-1),
                                reads=[t_PTc[pi][c]], writes=[t_PTc[pi][c]])

                    def fpv(e, pi=pi, q0=q0, jb=jb, h=h, nsb=nsb):
                        ins = None
                        for c in range(2):
                            e.matmul(pb[4 + c][:, q0:512], cv1[:, jb, h, 0:128], PTc[pi][c][:, q0:512],
                                     start=(jb == 0), stop=(jb == nsb - 1))
                        for c in range(2):
                            ins = e.matmul(pb[6 + c][:, q0:512], ones_b[:], PTc[pi][c][:, q0:512],
                                           start=(jb == 0), stop=(jb == nsb - 1))
                        return ins
                    P.op("pe", fpv, reads=[t_PTc[pi][0], t_PTc[pi][1], t_cv1[jb], t_ones],
                         writes=[t_pb[4], t_pb[5], t_pb[6], t_pb[7]])
                P.op("dve", lambda e: e.reciprocal(out=e_a[:], in_=pb[6][:, :]), reads=[t_pb[6]], writes=[t_ea])
                P.op("dve", lambda e: e.reciprocal(out=e_b[:], in_=pb[7][:, :]), reads=[t_pb[7]], writes=[t_eb])
                P.op("dve", lambda e: e.tensor_tensor(out=e_a[:], in0=pb[4][:, :], in1=e_a[:], op=ALU.mult),
                     reads=[t_pb[4], t_ea], writes=[t_ea])
                P.op("dve", lambda e: e.tensor_tensor(out=e_b[:], in0=pb[5][:, :], in1=e_b[:], op=ALU.mult),
                     reads=[t_pb[5], t_eb], writes=[t_eb])
                P.op("dve", lambda e: e.scalar_tensor_tensor(out=e_a[:], in0=e_b[:], scalar=lamt[:, 5:6], in1=e_a[:],
                                                             op0=ALU.mult, op1=ALU.add),
                     reads=[t_ea, t_eb, t_der], writes=[t_ea], cost=0.65)
                P.op("act", lambda e: e.activation(out=e_b[:], in_=e_a[:], func=AF.Square), reads=[t_ea], writes=[t_eb])
                mm_group(pb[3][:, :], [(ones_f[:], e_b[:])], t_pb[3], [t_ones, t_eb])
                pow_rstd("dve", e_b[:], pb[3][:, :], 1.0 / 128, [t_pb[3]], [t_eb])
                P.op("dve", lambda e: e.tensor_tensor(out=e_a[:], in0=e_a[:], in1=e_b[:], op=ALU.mult),
                     reads=[t_ea, t_eb], writes=[t_ea])
                P.op("dve", lambda e, h=h: e.scalar_tensor_tensor(out=ycT[:, h, :], in0=e_a[:], scalar=lamt[:, 6:7], in1=scg[:, h, :],
                                                                  op0=ALU.mult, op1=ALU.mult),
                     reads=[t_ea, t_der, t_scg], writes=[t_ycT], cost=0.65)

            tap("ycT", ycT[:], [t_ycT])
            for c in range(4):
                eng = "dve"
                P.op(eng, lambda e, c=c: e.tensor_scalar(out=vcv[:, c, :], in0=u_ext[:, c, 0:512],
                                                         scalar1=cst[:, C_CONVW + c * 31:C_CONVW + c * 31 + 1],
                                                         scalar2=cst[:, C_CONVB + c:C_CONVB + c + 1], op0=ALU.mult, op1=ALU.add),
                     reads=[t_uext[c], t_cst, t_cgate], writes=[t_vcv[c]], prio=CONV_PRIO)
                for jj in range(1, 31):
                    P.op(eng, lambda e, c=c, jj=jj: e.scalar_tensor_tensor(
                        out=vcv[:, c, :], in0=u_ext[:, c, jj:jj + 512],
                        scalar=cst[:, C_CONVW + c * 31 + jj:C_CONVW + c * 31 + jj + 1], in1=vcv[:, c, :],
                        op0=ALU.mult, op1=ALU.add), reads=[t_uext[c], t_cst, t_vcv[c]], writes=[t_vcv[c]], prio=CONV_PRIO, cost=0.65)
                P.op(eng, lambda e, c=c: e.tensor_copy(out=u_ext[:, c, 0:30], in_=u_ext[:, c, 512:542]),
                     reads=[t_uext[c]], writes=[t_uext[c]])
            mm_group(pb[0][:, :], [(ones_f[:], vcv[:, c, :]) for c in range(4)], t_pb[0], [t_ones] + t_vcv)
            P.op("dve", lambda e: e.tensor_scalar(out=mean[:], in0=pb[0][:, :], scalar1=1.0 / 512, scalar2=None, op0=ALU.mult),
                 reads=[t_pb[0]], writes=[t_mean])
            for c in range(4):
                P.op("dve", lambda e, c=c: e.tensor_tensor(out=vcv[:, c, :], in0=vcv[:, c, :], in1=mean[:], op=ALU.subtract),
                     reads=[t_vcv[c], t_mean], writes=[t_vcv[c]])
            for c in range(4):
                P.op("act", lambda e, c=c: e.activation(out=(tmpf if c % 2 == 0 else sigb)[:], in_=vcv[:, c, :], func=AF.Square),
                     reads=[t_vcv[c]], writes=[t_tmpf if c % 2 == 0 else t_sigb])
                P.op("pe", lambda e, c=c: e.matmul(pb[1][:, :], ones_f[:], (tmpf if c % 2 == 0 else sigb)[:],
                                                   start=(c == 0), stop=(c == 3)),
                     reads=[t_ones, t_tmpf if c % 2 == 0 else t_sigb], writes=[t_pb[1]])
            pow_rstd("dve", rstd[:], pb[1][:, :], 1.0 / 512, [t_pb[1]], [t_rstd])
            for c in range(4):
                P.op("dve", lambda e, c=c: e.tensor_tensor(out=vcv[:, c, :], in0=vcv[:, c, :], in1=rstd[:], op=ALU.mult),
                     reads=[t_vcv[c], t_rstd], writes=[t_vcv[c]])
                P.op("act", lambda e, c=c: e.activation(out=vcv[:, c, :], in_=vcv[:, c, :], func=AF.Silu,
                                                        scale=cst[:, C_LNG + c:C_LNG + c + 1], bias=cst[:, C_LNB + c:C_LNB + c + 1]),
                     reads=[t_vcv[c], t_cst], writes=[t_vcv[c]])
                P.op("dve", lambda e, c=c: e.tensor_tensor(out=ybT[:, c, :], in0=vcv[:, c, :], in1=sbg[:, c, :], op=ALU.mult),
                     reads=[t_vcv[c], t_sbg], writes=[t_ybT])

            tap("ybT", ybT[:], [t_ybT])
            yTs = [(yaT, t_yaT), (ybT, t_ybT), (ycT, t_ycT)]
            for mp in range(4):
                gk_wop = 10 + mp * 3
                ensure(gk_wop + NSLOT - 1)
                s_w = slot_of[seq[gk_wop]]
                wop = wsl[s_w][:, 0:3072].rearrange("p (m i k c) -> p m i k c", m=2, i=3, k=4)
                for mm in range(2):
                    m = mp * 2 + mm
                    gk_mg = gk_wop + 1 + mm
                    s_g = slot_of[seq[gk_mg]]
                    mgv = wsl[s_g][:, 0:3072].rearrange("p (k c) -> p k c", c=384)
                    for i in range(3):
                        mm_group(pb[i][:, :], [(mgv[:, k, i * 128:(i + 1) * 128], hT[:, k, :]) for k in range(8)],
                                 t_pb[i], t_hTb + [t_wsl[s_g]])
                        P.op("act", lambda e, i=i: e.activation(out=gt[:, i, :], in_=pb[i][:, :], func=AF.Sigmoid),
                             reads=[t_pb[i]], writes=[t_gt])
                    for i in range(3):
                        yT_, ty_ = yTs[i]
                        mm_group(pb[3 + i][:, :], [(wop[:, mm, i, k, :], yT_[:, k, :]) for k in range(4)],
                                 t_pb[3 + i], [ty_, t_wsl[s_w]])
                    P.op("dve", lambda e: e.tensor_tensor(out=mt0[:], in0=pb[3][:, :], in1=gt[:, 0, :], op=ALU.mult),
                         reads=[t_pb[3], t_gt], writes=[t_mt0])
                    P.op("dve", lambda e: e.tensor_tensor(out=mt1[:], in0=pb[4][:, :], in1=gt[:, 1, :], op=ALU.mult),
                         reads=[t_pb[4], t_gt], writes=[t_mt1])
                    P.op("dve", lambda e: e.tensor_tensor(out=mt0[:], in0=mt0[:], in1=mt1[:], op=ALU.add),
                         reads=[t_mt0, t_mt1], writes=[t_mt0])
                    P.op("dve", lambda e: e.tensor_tensor(out=mt1[:], in0=pb[5][:, :], in1=gt[:, 2, :], op=ALU.mult),
                         reads=[t_pb[5], t_gt], writes=[t_mt1])
                    P.op("dve", lambda e, m=m: e.tensor_tensor(out=mT[:, m, :], in0=mt0[:], in1=mt1[:], op=ALU.add),
                         reads=[t_mt0, t_mt1], writes=[t_aqT if m < 4 else t_cqT])

            tap("mT", mT[:], [t_aqT, t_cqT])
            ensure(23)
            s0 = slot_of[seq[22]]
            s1 = slot_of[seq[23]]
            for b in range(4):
                blk = ti * 4 + b
                xi = blk % 2
                rows = slice(blk * 128, (blk + 1) * 128)
                P.dma("pool", P.chan(f"xr{xi}"), lambda e, xi=xi, rows=rows: e.dma_start(out=xr[xi][:], in_=xin_d[rows, :]),
                      reads=xin_reads(blk), writes=[t_xr[xi]])
                for n, s in enumerate((s0, s1)):
                    bk = 6 + n
                    wv = wsl[s][:].rearrange("p (k c) -> p k c", c=512)
                    mm_group(pb[bk][:, :], [(mT[:, k, b * 128:(b + 1) * 128], wv[:, k, :]) for k in range(8)],
                             t_pb[bk], [t_aqT, t_cqT, t_wsl[s]])
                    P.op("dve", lambda e, xi=xi, n=n, bk=bk: e.tensor_tensor(out=xr[xi][:, n * 512:(n + 1) * 512],
                                                                             in0=pb[bk][:, :], in1=xr[xi][:, n * 512:(n + 1) * 512], op=ALU.add),
                         reads=[t_pb[bk], t_xr[xi]], writes=[t_xr[xi]])
                P.dma("pool", P.chan(f"xst{xi}"), lambda e, xi=xi, rows=rows: e.dma_start(out=xout_d[rows, :], in_=xr[xi][:]),
                      reads=[t_xr[xi]], writes=xout_writes(blk))

    P.emit(schedule=SCHEDULE)
    P.close()
    es.close()
    return nc


def pack_weights(w_in, w_o_attn, w_o_conv, w_o_diff, w_out):
    L = w_in.shape[0]
    wpk = np.zeros((L, NG, 128, GSZ), np.float32)
    for l in range(L):
        wk = w_in[l].reshape(8, 128, 7936).transpose(1, 0, 2)

        def put(g, cols):
            blk = wk[:, :, cols]
            n = blk.shape[2]
            wpk[l, g].reshape(128, 8, 512)[:, :, :n] = blk
        ar = np.arange
        put(0, ar(0, 512))
        put(1, ar(512, 768))
        put(2, ar(768, 1280))
        for gi in range(2):
            cols = []
            for cc in range(2):
                c = gi * 2 + cc
                cols += list(range(1280 + c * 128, 1280 + (c + 1) * 128))
                cols += list(range(1792 + c * 128, 1792 + (c + 1) * 128))
            put(3 + gi, np.array(cols))
        put(5, ar(2304, 2816))
        put(6, ar(2816, 3328))
        put(7, ar(3328, 3840))
        put(8, ar(3840, 4352))
        put(9, ar(4352, 4864))
        wos = [w.reshape(4, 128, 1024).transpose(1, 0, 2) for w in (w_o_attn[l], w_o_conv[l], w_o_diff[l])]
        for mp in range(4):
            g = 10 + mp * 3
            v = wpk[l, g][:, 0:3072].reshape(128, 2, 3, 4, 128)
            for mm in range(2):
                m = mp * 2 + mm
                for i in range(3):
                    v[:, mm, i, :, :] = wos[i][:, :, m * 128:(m + 1) * 128]
                cols = np.concatenate([ar(4864 + i * 1024 + m * 128, 4864 + i * 1024 + (m + 1) * 128) for i in range(3)])
                wpk[l, g + 1 + mm][:, 0:3072].reshape(128, 8, 384)[:] = wk[:, :, cols]
        wo = w_out[l].reshape(8, 128, 1024).transpose(1, 0, 2)
        for n in range(2):
            wpk[l, 22 + n].reshape(128, 8, 512)[:] = wo[:, :, n * 512:(n + 1) * 512]
    return wpk


def pack_consts(norm_g, attn_q_norm_g, attn_k_norm_g, attn_sinks, conv_w, conv_b, conv_norm_g, conv_norm_b,
                diff_q_norm_g, diff_k_norm_g, lambda_q, lambda_k, diff_subln_g):
    L = norm_g.shape[0]
    c = np.zeros((L, 128, NCST), np.float32)
    for l in range(L):
        c[l, :, C_GBC:C_GBC + 1024] = norm_g[l][None, :]
        c[l, :, C_AQG:C_AQG + 64] = attn_q_norm_g[l][None, :]
        c[l, :, C_AKG:C_AKG + 64] = attn_k_norm_g[l][None, :]
        c[l, :, C_DQG:C_DQG + 64] = diff_q_norm_g[l][None, :]
        c[l, :, C_DKG:C_DKG + 64] = diff_k_norm_g[l][None, :]
        c[l, :, C_SUBG:C_SUBG + 128] = diff_subln_g[l][None, :]
        c[l, :, C_SINK:C_SINK + 8] = attn_sinks[l][None, :]
        c[l, :, C_CONVW:C_CONVW + 124] = conv_w[l].reshape(31, 4, 128).transpose(2, 1, 0).reshape(128, 124)
        c[l, :, C_CONVB:C_CONVB + 4] = conv_b[l].reshape(4, 128).T
        c[l, :, C_LNG:C_LNG + 4] = conv_norm_g[l].reshape(4, 128).T
        c[l, :, C_LNB:C_LNB + 4] = conv_norm_b[l].reshape(4, 128).T
        c[l, :, C_LQ:C_LQ + 128] = lambda_q[l].reshape(1, 128)
        c[l, :, C_LK:C_LK + 128] = lambda_k[l].reshape(1, 128)
        c[l, :, C_SUBGC] = diff_subln_g[l]
    return c


_NC_CACHE = {}


def kernel(x, norm_g, w_in, attn_q_norm_g, attn_k_norm_g, attn_sinks, w_o_attn,
           conv_w, conv_b, conv_norm_g, conv_norm_b, w_o_conv,
           diff_q_norm_g, diff_k_norm_g, lambda_q, lambda_k, diff_subln_g, w_o_diff, w_out):
    f = lambda a: np.ascontiguousarray(np.asarray(a, dtype=np.float32))
    x = f(x)
    B, Tn, _ = x.shape
    L = int(np.asarray(norm_g).shape[0])
    wpk = pack_weights(f(w_in), f(w_o_attn), f(w_o_conv), f(w_o_diff), f(w_out))
    cpk = pack_consts(f(norm_g), f(attn_q_norm_g), f(attn_k_norm_g), f(attn_sinks), f(conv_w), f(conv_b),
                      f(conv_norm_g), f(conv_norm_b), f(diff_q_norm_g), f(diff_k_norm_g), f(lambda_q),
                      f(lambda_k), f(diff_subln_g))
    key = (Tn, L)
    if key not in _NC_CACHE:
        _NC_CACHE[key] = build(Tn, L)
    nc = _NC_CACHE[key]
    in_maps = [{"x": x[i], "cpk": cpk, "wpk": wpk} for i in range(B)]
    res = run_bass_kernel_spmd(nc, in_maps, core_ids=list(range(B)))
    return np.stack([np.asarray(r["out"], dtype=np.float32) for r in res.results], axis=0)
```

```python
import contextlib
import math
import types
import numpy as np
import concourse.bass as bass
import concourse.mybir as mybir
from concourse.bass_utils import run_bass_kernel_spmd

F32 = mybir.dt.float32
BF16 = mybir.dt.bfloat16
AF = mybir.ActivationFunctionType
ALU = mybir.AluOpType
AX = mybir.AxisListType

ENGS = ("pe", "act", "dve", "pool", "sp")
SCHEDULE = True
CHECK = True
CONV_PRIO = 2500
NSLOT = 4
ATTACH_WAITS = True
EPS = 1e-6
D = 1024
NG = 24
GSZ = 4096
C_GBC, C_AQG, C_AKG, C_DQG, C_DKG, C_SUBG, C_SINK, C_CONVW, C_CONVB, C_LNG, C_LNB, C_LQ, C_LK, C_SUBGC, NCST = (
    0, 1024, 1088, 1152, 1216, 1280, 1408, 1416, 1540, 1544, 1548, 1552, 1680, 1808, 1812)


class T:
    __slots__ = ("name", "w", "rs")

    def __init__(self, name):
        self.name = name
        self.w = None
        self.rs = []


def _freeze(fn):
    if fn is None or fn.__closure__ is None:
        return fn
    cells = []
    for c in fn.__closure__:
        try:
            cells.append(types.CellType(c.cell_contents))
        except ValueError:
            cells.append(c)
    g = types.FunctionType(fn.__code__, fn.__globals__, fn.__name__, fn.__defaults__, tuple(cells))
    g.__kwdefaults__ = fn.__kwdefaults__
    return g


class Op:
    __slots__ = ("id", "eng", "fn", "chan", "deps", "cost", "lat", "succ", "nun", "fin", "cnt", "semkey", "key", "attach")

    def __init__(self, id_, eng, fn, chan, cost, lat):
        self.id = id_
        self.attach = False
        self.key = id_
        self.eng = eng
        self.fn = fn
        self.chan = chan
        self.deps = {}
        self.cost = cost
        self.lat = lat
        self.succ = []
        self.nun = 0
        self.fin = 0.0
        self.cnt = 0
        self.semkey = None


DEFAULT_COST = {"pe": 0.25, "act": 0.5, "dve": 0.55, "pool": 0.9, "sp": 0.1}


class _MockIns:
    def then_inc(self, *a, **k):
        return self

    def _wait_ge(self, *a, **k):
        return self


def _fsize(ap):
    n = 1
    for d_ in ap.shape[1:]:
        n *= d_
    return n


class _MockEng:
    def __init__(self, eng):
        self.eng = eng
        self.cost = 0.0
        self.ncalls = 0
        self.accum = False

    def __getattr__(self, name):
        def call(*args, **kw):
            eng = self.eng
            self.ncalls += 1
            if kw.get("accum_out") is not None:
                self.accum = True
            if eng == "pe":
                if name == "matmul":
                    rhs = args[2] if len(args) > 2 else kw["rhs"]
                    n = _fsize(rhs)
                    self.cost += 0.02 + n / 2000.0 * (4.0 if rhs.dtype == F32 else 1.0)
                else:
                    self.cost += 0.11
            else:
                out = kw.get("out", args[0] if args else None)
                n = _fsize(out) if out is not None else 64
                if eng == "act":
                    self.cost += 0.2 + n / 1200.0 + (0.1 if kw.get("accum_out") is not None else 0.0)
                elif eng == "dve":
                    self.cost += 0.17 + n / 960.0
                elif eng == "pool":
                    fp32 = out is not None and out.dtype == F32
                    self.cost += 0.2 + n * (0.0032 if fp32 else 0.0012)
                else:
                    self.cost += 0.1
            return _MockIns()
        return call


class Prog:
    def __init__(self, nc):
        self.nc = nc
        self.sems = {}
        self._stack = []
        self.segs = [[]]
        self.nops = 0
        for e in ENGS:
            self._mksem("E_" + e)

    def _mksem(self, key):
        cm = self.nc.semaphore(key)
        h = cm.__enter__()
        self._stack.append(cm)
        self.sems[key] = h
        return h

    def chan(self, name):
        key = "D_" + name
        if key not in self.sems:
            self._mksem(key)
        return key

    def _record(self, o, reads, writes):
        is_dma = o.chan is not None

        def add(d, raw):
            if d is None or d is o:
                return
            need = raw or is_dma or (d.chan is not None) or (d.eng != o.eng) or (o.eng != "pe")
            if need or d not in o.deps:
                o.deps[d] = need or o.deps.get(d, False)
        for t in reads:
            add(t.w, True)
        for t in writes:
            add(t.w, o.eng != "pe")
            for r in t.rs:
                add(r, False)
        for t in reads:
            t.rs.append(o)
        for t in writes:
            t.w = o
            t.rs = []
        self.segs[-1].append(o)

    def op(self, eng, fn, reads=(), writes=(), cost=None, prio=0):
        fn = _freeze(fn)
        att = False
        try:
            m = _MockEng(eng)
            fn(m)
            c = m.cost
            att = ATTACH_WAITS and eng != "pe" and m.ncalls == 1 and not m.accum
        except Exception:
            c = DEFAULT_COST[eng] if cost is None else cost
        o = Op(self.nops, eng, fn, None, c, c + 0.2)
        o.attach = att
        o.key = o.id + prio
        self.nops += 1
        self._record(o, reads, writes)
        return o

    def dma(self, eng, chan, fn, reads=(), writes=(), cost=3.0):
        o = Op(self.nops, eng, _freeze(fn), chan, 0.1 if eng == "sp" else 1.0, cost)
        o.attach = ATTACH_WAITS
        self.nops += 1
        self._record(o, reads, writes)
        return o

    def barrier(self):
        self.segs.append([])

    def _schedule(self, ops):
        import heapq
        inseg = set(o.id for o in ops)
        for o in ops:
            o.succ = []
        for o in ops:
            n = 0
            for d in o.deps:
                if d.id in inseg:
                    d.succ.append(o)
                    n += 1
            o.nun = n
        fut = {e: [] for e in ENGS}
        avail = {e: [] for e in ENGS}
        free = {e: 0.0 for e in ENGS}
        order = {e: [] for e in ENGS}

        def push(o):
            rt = 0.0
            for d in o.deps:
                if d.id in inseg and d.fin > rt:
                    rt = d.fin
            heapq.heappush(fut[o.eng], (rt, o.id, o))
        for o in ops:
            if o.nun == 0:
                push(o)
        left = len(ops)
        while left:
            best = None
            for e in ENGS:
                f, a = fut[e], avail[e]
                while f and f[0][0] <= free[e]:
                    rt, i, o = heapq.heappop(f)
                    heapq.heappush(a, (o.key, i, o))
                if a:
                    st = free[e]
                elif f:
                    st = f[0][0]
                else:
                    continue
                if best is None or st < best[0]:
                    best = (st, e)
            st, e = best
            if avail[e]:
                _k, i, o = heapq.heappop(avail[e])
            else:
                rt, i, o = heapq.heappop(fut[e])
            free[e] = st + o.cost
            o.fin = st + o.lat
            order[e].append(o)
            left -= 1
            for s_ in o.succ:
                s_.nun -= 1
                if s_.nun == 0:
                    push(s_)
        return order, max(free.values())

    def emit(self, schedule=True):
        nc = self.nc
        sems = self.sems
        cnt = {k: 0 for k in sems}
        streams = {e: [] for e in ENGS}
        waited = {e: {} for e in ENGS}
        self.est = 0.0
        for si, seg in enumerate(self.segs):
            if schedule:
                order, est = self._schedule(seg)
                self.est += est
            else:
                order = {e: [o for o in seg if o.eng == e] for e in ENGS}
            for e in ENGS:
                for o in order[e]:
                    if o.chan is None:
                        o.semkey = "E_" + e
                        cnt[o.semkey] += 1
                    else:
                        o.semkey = o.chan
                        cnt[o.semkey] += 16
                    o.cnt = cnt[o.semkey]
            for e in ENGS:
                wd = waited[e]
                for o in order[e]:
                    need = {}
                    for d, ns in o.deps.items():
                        if not ns:
                            continue
                        if need.get(d.semkey, 0) < d.cnt:
                            need[d.semkey] = d.cnt
                    waits = []
                    for k, v in need.items():
                        if wd.get(k, 0) < v:
                            wd[k] = v
                            waits.append((k, v))
                    streams[e].append((waits, o.fn, (o.semkey, 1 if o.chan is None else 16), o))
            for e in ENGS:
                waits = []
                for k, v in cnt.items():
                    if v > 0 and waited[e].get(k, 0) < v:
                        waited[e][k] = v
                        waits.append((k, v))
                streams[e].append((waits, None, None, None))
        self.streams = streams
        if CHECK:
            self.check(streams)

        def replay(e, name):
            for waits, fn, inc, o_ in streams[name]:
                if fn is None:
                    for k, v in waits:
                        e.wait_ge(sems[k], v)
                    continue
                if o_.attach and waits:
                    for k, v in waits[:-1]:
                        e.wait_ge(sems[k], v)
                    ins = fn(e)
                    k, v = waits[-1]
                    ins._wait_ge(sems[k], v)
                else:
                    for k, v in waits:
                        e.wait_ge(sems[k], v)
                    ins = fn(e)
                ins.then_inc(sems[inc[0]], inc[1])

        with nc.Block() as block:
            @block.tensor
            def _(e):
                replay(e, "pe")

            @block.scalar
            def _(e):
                replay(e, "act")

            @block.vector
            def _(e):
                replay(e, "dve")

            @block.gpsimd
            def _(e):
                replay(e, "pool")

            @block.sync
            def _(e):
                replay(e, "sp")

    def check(self, streams):
        val = {k: 0 for k in self.sems}
        ptr = {e: 0 for e in ENGS}
        done = set()
        total = sum(len(v) for v in streams.values())
        n = 0
        while n < total:
            prog = False
            for e in ENGS:
                st = streams[e]
                while ptr[e] < len(st):
                    waits, fn, inc, o = st[ptr[e]]
                    if any(val[k] < v for k, v in waits):
                        break
                    if o is not None:
                        for d in o.deps:
                            assert d.id in done, ("race", e, o.id, d.id, d.eng)
                        done.add(o.id)
                        val[inc[0]] += inc[1]
                        assert val[inc[0]] == o.cnt, ("count mismatch", e, o.id, inc, val[inc[0]], o.cnt)
                    ptr[e] += 1
                    n += 1
                    prog = True
            if not prog:
                info = {e: (ptr[e], len(streams[e]), streams[e][ptr[e]][0] if ptr[e] < len(streams[e]) else None) for e in ENGS}
                raise RuntimeError(f"deadlock: {info} vals={ {k: val[k] for e in ENGS for k, v in (streams[e][ptr[e]][0] if ptr[e] < len(streams[e]) else [])} }")

    def close(self):
        while self._stack:
            self._stack.pop().__exit__(None, None, None)


def build(Tn=4096, L=2, taps=()):
    NB = Tn // 128
    NT = Tn // 512
    nc = bass.Bass("TRN2", target_bir_lowering=False)
    x_d = nc.dram_tensor("x", [Tn, D], F32, kind="ExternalInput").ap()
    cpk_d = nc.dram_tensor("cpk", [L, 128, NCST], F32, kind="ExternalInput").ap()
    wpk_d = nc.dram_tensor("wpk", [L, NG, 128, GSZ], F32, kind="ExternalInput").ap()
    out_d = nc.dram_tensor("out", [Tn, D], F32, kind="ExternalOutput").ap()
    wbf_d = nc.dram_tensor("wbf", [L, NG, 128, GSZ], BF16, kind="Internal").ap()
    x1_d = nc.dram_tensor("x1s", [Tn, D], F32, kind="Internal").ap()
    tap_d = {}
    for name, shape, dt_ in taps:
        tap_d[name] = nc.dram_tensor("tap_" + name, list(shape), dt_, kind="ExternalOutput").ap()

    P = Prog(nc)
    es = contextlib.ExitStack()

    def sb(name, shape, dt, stack=es):
        return stack.enter_context(nc.sbuf_tensor(name, list(shape), dt))

    t_wbf = [[T(f"wbf{l}_{g}") for g in range(NG)] for l in range(L)]

    ident = sb("ident", [128, 128], BF16)
    ones_f = sb("ones_f", [128, 128], F32)
    eps_t = sb("eps_t", [128, 1], F32)
    cst = sb("cst", [128, NCST], F32)
    esink = sb("esink", [128, 8], F32)
    lamt = sb("lamt", [128, 8], F32)
    xld = [sb(f"xld{i}", [128, D], F32) for i in range(2)]
    xr = [sb(f"xr{i}", [128, D], F32) for i in range(2)]
    t_xr = [T("xr0"), T("xr1")]
    hbf0_ = sb("hbf0", [128, D], BF16)
    hbf = [hbf0_, hbf0_]
    nrm = [sb(f"nrm{i}", [128, 4], F32) for i in range(2)]
    hT = sb("hT", [128, 8, 512], BF16)
    wsl = [sb(f"wsl{i}", [128, GSZ], BF16) for i in range(NSLOT)]
    qk_f = sb("qk_f", [128, 512], F32)
    qk_sq = sb("qk_sq", [128, 512], F32)
    qk_ss = sb("qk_ss", [128, 8], F32)
    qk_n = sb("qk_n", [128, 512], BF16)
    mT = sb("mT", [128, 8, 512], BF16)
    aqT = mT[:, 0:4, :]
    akT2 = sb("akT2", [128, 2, 640], BF16)
    av1 = sb("av1", [128, 5, 2, 65], BF16)
    sag = sb("sag", [128, 4, 512], BF16)
    scg = sb("scg", [128, 4, 512], BF16)
    cqT = mT[:, 4:8, :]
    ckT = sb("ckT", [128, 4, Tn], BF16)
    cv1_flat = sb("cv1", [128, NB * 516], BF16)
    cv1 = cv1_flat[:].rearrange("p (a b c) -> p a b c", b=4, c=129)
    u_ext = sb("u_ext", [128, 4, 542], F32)
    sigb = sb("sigb", [128, 512], F32)
    sbg = sb("sbg", [128, 4, 512], BF16)
    vcv = sb("vcv", [128, 4, 512], F32)
    mean = sigb
    rstd = sb("rstd", [128, 512], F32)
    tmpf = sb("tmpf", [128, 512], F32)
    PTc = [[sb(f"PTc{i}{c}", [128, 512], BF16) for c in range(2)] for i in range(2)]
    PTa = PTc[1]
    den = sb("den", [128, 16], F32)
    ya_f = tmpf
    ya_bf = sb("ya_bf", [128, 512], BF16)
    yaT = sb("yaT", [128, 4, 512], BF16)
    ybT = sbg
    ycT = sb("ycT", [128, 4, 512], BF16)
    dsm = sb("dsm", [128, 8], F32)
    d_t = sb("d_t", [128, 128], F32)
    ones_b = sb("ones_b", [128, 128], BF16)
    gt = sb("gt", [128, 3, 512], F32)
    e_a = gt[:, 0, :]
    e_b = gt[:, 1, :]
    mt0 = qk_f
    mt1 = qk_sq

    pb = [es.enter_context(nc.psum_tensor(f"pb{i}", [128, 512], F32)) for i in range(8)]
    t_pb = [T(f"pb{i}") for i in range(8)]

    def TT(*names):
        return [T(n) for n in names]

    (t_ident, t_ones, t_cst, t_der, t_hT, t_qkf, t_qksq, t_qkss, t_qkn, t_aqT, t_sag, t_scg, t_cqT,
     t_sigb, t_sbg, t_mean, t_rstd, t_tmpf, t_den, t_yaf, t_yabf, t_yaT, t_ybT, t_ycT, t_dsm, t_dt, t_dd,
     t_dj, t_ycbf, t_gt, t_mT, t_mt0, t_mt1, t_zrow, t_ones_cols) = TT(
        "ident", "ones", "cst", "der", "hT", "qkf", "qksq", "qkss", "qkn", "aqT", "sag", "scg", "cqT",
        "sigb", "sbg", "mean", "rstd", "tmpf", "den", "yaf", "yabf", "yaT", "ybT", "ycT", "dsm", "dt", "dd",
        "dj", "ycbf", "gt", "mT", "mt0", "mt1", "zrow", "ones_cols")
    t_mt0, t_mt1 = t_qkf, t_qksq
    t_cgate = T("cgate")
    t_ea = t_eb = t_gt
    qkbufs = [(qk_f, qk_sq, qk_ss, qk_n, t_qkf, t_qksq, t_qkss, t_qkn),
              (sb("qk_f2", [128, 512], F32), sb("qk_sq2", [128, 512], F32), sb("qk_ss2", [128, 8], F32),
               sb("qk_n2", [128, 512], BF16), T("qkf2"), T("qksq2"), T("qkss2"), T("qkn2"))]
    qkpar = [0]
    t_ybT = t_sbg
    t_hTb = [T(f"hT{b}") for b in range(4)]
    t_mean = t_sigb
    t_dsmq = [T(f"dsmq{i}") for i in range(4)]
    t_ddq = [T(f"ddq{i}") for i in range(4)]
    t_ss4 = T("ss4")
    t_yaf = t_tmpf
    t_xld = TT("xld0", "xld1")
    t_hbf = [T("hbf0")] * 2
    t_nrm = TT("nrm0", "nrm1")
    t_wsl = [T(f"wsl{i}") for i in range(NSLOT)]
    t_akT = [T(f"akT{i}") for i in range(5)]
    t_av1 = [T(f"av1{i}") for i in range(5)]
    t_ckT = [T(f"ckT{b}") for b in range(NB)]
    t_cv1 = [T(f"cv1{b}") for b in range(NB)]
    t_uext = [T(f"uext{c}") for c in range(4)]
    t_vcv = [T(f"vcv{c}") for c in range(4)]
    t_PTc = [TT("PTc00", "PTc01"), TT("PTc10", "PTc11")]
    t_PTa = t_PTc[1]
    t_x1 = [T(f"x1_{b}") for b in range(NB)]
    t_out = T("out")
    t_tap = T("tap")

    tapped = set()

    def tap(name, src_ap, reads):
        if name in tap_d and name not in tapped:
            tapped.add(name)
            P.dma("sp", P.chan("tap_" + name), lambda e: e.dma_start(out=tap_d[name], in_=src_ap),
                  reads=reads, writes=[t_tap])

    P.op("pool", lambda e: e.memset(ident[:], 0.0), writes=[t_ident])
    P.op("pool", lambda e: e.affine_select(out=ident[:], in_=ident[:], compare_op=ALU.not_equal, fill=1.0,
                                            base=0, pattern=[[-1, 128]], channel_multiplier=1),
         reads=[t_ident], writes=[t_ident])
    P.op("pool", lambda e: e.memset(ones_f[:], 1.0), writes=[t_ones])
    P.op("pool", lambda e: e.memset(ones_b[:], 1.0), writes=[t_ones])
    P.op("pool", lambda e: e.memset(eps_t[:], EPS), writes=[t_ones])
    P.op("pool", lambda e: e.memset(av1[:, :, :, 64:65], 1.0), writes=t_av1)

    wctr = [0]

    if NB >= 28:
        stage = [cv1_flat[:, (4 + 8 * i) * 516:(4 + 8 * i) * 516 + 4096].bitcast(F32) for i in range(3)]
        t_stage = [[t_cv1[b] for b in range(4 + 8 * i, 12 + 8 * i)] for i in range(3)]
    else:
        stage = [sb(f"stage{i}", [128, 2048], F32)[:] for i in range(3)]
        t_stage = [[T(f"stage{i}")] for i in range(3)]
    sctr = [0]

    def wload(l, g, cast):
        i = wctr[0] % NSLOT
        wctr[0] += 1
        if not cast:
            src = wbf_d[l, g, :, :]
            P.dma("sp", P.chan(f"wsl{i}"), lambda e: e.dma_start(out=wsl[i][:], in_=src),
                  reads=[t_wbf[l][g]], writes=[t_wsl[i]], cost=5.0)
            return i
        for hf in range(2):
            si = sctr[0] % 3
            sctr[0] += 1
            src = wpk_d[l, g, :, hf * 2048:(hf + 1) * 2048]
            P.dma("sp", P.chan(f"stage{si}"), lambda e: e.dma_start(out=stage[si], in_=src),
                  writes=t_stage[si], cost=5.0)
            dst = wsl[i][:, hf * 2048:(hf + 1) * 2048]
            if hf == 0 or g % 2 == 1:
                P.op("act", lambda e: e.activation(out=dst, in_=stage[si], func=AF.Copy),
                     reads=t_stage[si], writes=[t_wsl[i]], cost=1.9)
            else:
                P.op("pool", lambda e: e.tensor_copy(out=dst, in_=stage[si]),
                     reads=t_stage[si], writes=[t_wsl[i]], cost=7.0)
        dstd = wbf_d[l, g, :, :]
        P.dma("sp", P.chan(f"wst{i}"), lambda e: e.dma_start(out=dstd, in_=wsl[i][:]),
              reads=[t_wsl[i]], writes=[t_wbf[l][g]], cost=4.0)
        return i

    bankctr = [0]

    def nextbank(lo, hi):
        b = lo + bankctr[0] % (hi - lo)
        bankctr[0] += 1
        return b

    def mm_group(out_ap, pairs, t_out_, reads, skip=False, extra_writes=()):
        n = len(pairs)

        def fn(e):
            ins = None
            for i, (lt, rh) in enumerate(pairs):
                ins = e.matmul(out_ap, lt, rh, start=(i == 0), stop=(i == n - 1))
            return ins
        ncol = 1
        for d_ in pairs[0][1].shape[1:]:
            ncol *= d_
        fp32 = pairs[0][1].dtype == F32
        P.op("pe", fn, reads=reads, writes=[t_out_] + list(extra_writes), cost=n * (0.06 + ncol / 2400.0 * (4 if fp32 else 1)))

    def pow_rstd(eng, out_ap, in_ap, scale, reads, writes):
        P.op("act", lambda e: e.activation(out=out_ap, in_=in_ap, func=AF.Ln, scale=scale, bias=eps_t[:, 0:1]),
             reads=list(reads) + [t_ones], writes=writes)
        P.op("act", lambda e: e.activation(out=out_ap, in_=out_ap, func=AF.Exp, scale=-0.5), reads=writes, writes=writes)

    for l in range(L):
        last = (l == L - 1)
        first = (l == 0)
        lam_init = 0.8 - 0.6 * math.exp(-0.3 * l)
        xin_d = x_d if first else x1_d
        xout_d = out_d if last else x1_d

        def xin_reads(b):
            return [] if first else [t_x1[b]]

        def xout_writes(b):
            return [t_out] if last else [t_x1[b]]

        P.dma("sp", P.chan("cst"), lambda e, l=l: e.dma_start(out=cst[:], in_=cpk_d[l, :, :]), writes=[t_cst])
        P.op("act", lambda e: e.activation(out=esink[:], in_=cst[:, C_SINK:C_SINK + 8], func=AF.Exp),
             reads=[t_cst], writes=[t_der])
        P.op("dve", lambda e: e.tensor_tensor(out=d_t[:, 0:128], in0=cst[:, C_LQ:C_LQ + 128],
                                              in1=cst[:, C_LK:C_LK + 128], op=ALU.mult),
             reads=[t_cst], writes=[t_dt])
        P.op("dve", lambda e: e.reduce_sum(out=lamt[:, 0:2], in_=d_t[:, 0:128].rearrange("p (a b) -> p a b", b=64),
                                           axis=AX.X), reads=[t_dt], writes=[t_der])
        P.op("act", lambda e: e.activation(out=lamt[:, 2:4], in_=lamt[:, 0:2], func=AF.Exp),
             reads=[t_der], writes=[t_der])
        P.op("dve", lambda e: e.tensor_tensor(out=lamt[:, 4:5], in0=lamt[:, 2:3], in1=lamt[:, 3:4], op=ALU.subtract),
             reads=[t_der], writes=[t_der])
        P.op("dve", lambda e, li=lam_init: e.tensor_scalar(out=lamt[:, 5:6], in0=lamt[:, 4:5], scalar1=li,
                                                           scalar2=-1.0, op0=ALU.add, op1=ALU.mult),
             reads=[t_der], writes=[t_der])
        P.op("dve", lambda e, li=lam_init: e.tensor_scalar(out=lamt[:, 6:7], in0=cst[:, C_SUBGC:C_SUBGC + 1],
                                                           scalar1=(1.0 - li), scalar2=None, op0=ALU.mult),
             reads=[t_cst], writes=[t_der])
        for c in range(4):
            P.op("pool", lambda e, c=c: e.memset(u_ext[:, c, 0:30], 0.0), writes=[t_uext[c]])

        wctr_l = []
        for ti in range(NT):
            seq = list(range(NG))
            slot_of = {}
            pend = [(l, g) for g in seq]
            issued = [0]

            def ensure(k):
                while issued[0] <= min(k, NG - 1):
                    g = seq[issued[0]]
                    slot_of[g] = wload(l, g, ti == 0)
                    issued[0] += 1

            ensure(NSLOT - 2)
            for b in range(4):
                blk = ti * 4 + b
                xi = blk % 2
                rows = slice(blk * 128, (blk + 1) * 128)
                P.dma("pool", P.chan(f"xld{xi}"), lambda e, xi=xi, rows=rows: e.dma_start(out=xld[xi][:], in_=xin_d[rows, :]),
                      reads=xin_reads(blk), writes=[t_xld[xi]])
                P.op("act", lambda e, xi=xi: e.activation(out=hbf[xi][:], in_=xld[xi][:], func=AF.Square,
                                                          accum_out=nrm[xi][:, 0:1]),
                     reads=[t_xld[xi]], writes=[t_hbf[xi], t_nrm[xi]])
                pow_rstd("dve", nrm[xi][:, 1:2], nrm[xi][:, 0:1], 1.0 / D, [t_nrm[xi]], [t_nrm[xi]])
                P.op("dve", lambda e, xi=xi: e.scalar_tensor_tensor(out=hbf[xi][:], in0=xld[xi][:], scalar=nrm[xi][:, 1:2],
                                                                    in1=cst[:, C_GBC:C_GBC + D], op0=ALU.mult, op1=ALU.mult),
                     reads=[t_xld[xi], t_nrm[xi], t_cst], writes=[t_hbf[xi]])
                bk = b % 2
                pT = pb[bk][:].bitcast(BF16)

                def ftr(e, xi=xi, pT=pT):
                    ins = None
                    for k in range(8):
                        ins = e.transpose(pT[:, k * 128:(k + 1) * 128], hbf[xi][:, k * 128:(k + 1) * 128], ident[:])
                    return ins
                P.op("pe", ftr, reads=[t_hbf[xi], t_ident], writes=[t_pb[bk]])
                P.op("act", lambda e, b=b, pT=pT: e.activation(out=hT[:, :, b * 128:(b + 1) * 128],
                                                               in_=pT.rearrange("p (k t) -> p k t", t=128), func=AF.Copy),
                     reads=[t_pb[bk]], writes=[t_hTb[b]], cost=0.9)

            tap("hT", hT[:], t_hTb)
            def tm_group(gk, ncols, post):
                ensure(gk + NSLOT - 1)
                s = slot_of[seq[gk]]
                wv = wsl[s][:].rearrange("p (k c) -> p k c", c=512)
                for b in range(4):
                    bk = nextbank(2, 8)
                    mm_group(pb[bk][:, 0:ncols],
                             [(hT[:, k, b * 128:(b + 1) * 128], wv[:, k, 0:ncols]) for k in range(8)],
                             t_pb[bk], [t_hTb[b], t_wsl[s]])
                    post(b, bk)

            def qk_norm(bk, ncols, gcol, nh):
                nonlocal qk_f, qk_sq, qk_ss, qk_n, t_qkf, t_qksq, t_qkss, t_qkn
                (qk_f, qk_sq, qk_ss, qk_n, t_qkf, t_qksq, t_qkss, t_qkn) = qkbufs[qkpar[0] % 2]
                qkpar[0] += 1
                P.op("act", lambda e: e.activation(out=qk_f[:, 0:ncols], in_=pb[bk][:, 0:ncols], func=AF.Copy),
                     reads=[t_pb[bk]], writes=[t_qkf])
                P.op("act", lambda e: e.activation(out=qk_sq[:, 0:ncols], in_=pb[bk][:, 0:ncols], func=AF.Square),
                     reads=[t_pb[bk]], writes=[t_qksq])
                P.op("dve", lambda e: e.reduce_sum(out=qk_ss[:, 0:nh],
                                                   in_=qk_sq[:, 0:ncols].rearrange("p (h d) -> p h d", d=64), axis=AX.X),
                     reads=[t_qksq], writes=[t_qkss])
                pow_rstd("dve", qk_ss[:, 0:nh], qk_ss[:, 0:nh], 1.0 / 64, [t_qkss], [t_qkss])
                P.op("dve", lambda e: e.tensor_tensor(
                    out=qk_sq[:, 0:ncols].rearrange("p (h d) -> p h d", d=64),
                    in0=qk_f[:, 0:ncols].rearrange("p (h d) -> p h d", d=64),
                    in1=qk_ss[:, 0:nh].unsqueeze(2).to_broadcast([128, nh, 64]), op=ALU.mult),
                    reads=[t_qkf, t_qkss], writes=[t_qksq])

            def post_aq(b, bk):
                qk_norm(bk, 512, C_AQG, 8)
                P.op("dve", lambda e: e.tensor_tensor(
                    out=qk_n[:].rearrange("p (h d) -> p h d", d=64),
                    in0=qk_sq[:].rearrange("p (h d) -> p h d", d=64),
                    in1=cst[:, C_AQG:C_AQG + 64].unsqueeze(1).to_broadcast([128, 8, 64]), op=ALU.mult),
                    reads=[t_qksq, t_cst], writes=[t_qkn])
                tb = 0 if b % 2 == 0 else 1
                pT = pb[tb][:].bitcast(BF16)

                def ftr(e):
                    ins = None
                    for c in range(4):
                        ins = e.transpose(pT[:, c * 128:(c + 1) * 128], qk_n[:, c * 128:(c + 1) * 128], ident[:])
                    return ins
                P.op("pe", ftr, reads=[t_qkn, t_ident], writes=[t_pb[tb]])
                P.op("act", lambda e: e.activation(out=aqT[:, :, b * 128:(b + 1) * 128],
                                                   in_=pT[:, 0:512].rearrange("p (c t) -> p c t", t=128), func=AF.Copy),
                     reads=[t_pb[tb]], writes=[t_aqT])
            tm_group(0, 512, post_aq)

            def post_akv(b, bk):
                qk_norm(bk, 128, C_AKG, 2)
                for dup in range(2):
                    P.op("dve", lambda e, dup=dup: e.tensor_tensor(
                        out=qk_n[:, 0:256].rearrange("p (h u d) -> p h u d", u=2, d=64)[:, :, dup, :],
                        in0=qk_sq[:, 0:128].rearrange("p (h d) -> p h d", d=64),
                        in1=cst[:, C_AKG:C_AKG + 64].unsqueeze(1).to_broadcast([128, 2, 64]), op=ALU.mult),
                        reads=[t_qksq, t_cst], writes=[t_qkn])
                tb = 0 if b % 2 == 0 else 1
                pT = pb[tb][:].bitcast(BF16)

                def ftr(e):
                    ins = None
                    for c in range(2):
                        ins = e.transpose(pT[:, c * 128:(c + 1) * 128], qk_n[:, c * 128:(c + 1) * 128], ident[:])
                    return ins
                P.op("pe", ftr, reads=[t_qkn, t_ident], writes=[t_pb[tb]])
                P.op("act", lambda e: e.activation(out=akT2[:, :, (b + 1) * 128:(b + 2) * 128],
                                                   in_=pT[:, 0:256].rearrange("p (c t) -> p c t", t=128), func=AF.Copy),
                     reads=[t_pb[tb]], writes=[t_akT[b + 1]])
                P.op("act", lambda e: e.activation(out=av1[:, b + 1, :, 0:64],
                                                   in_=pb[bk][:, 128:256].rearrange("p (j d) -> p j d", d=64), func=AF.Copy),
                     reads=[t_pb[bk]], writes=[t_av1[b + 1]])
            tm_group(1, 256, post_akv)

            def post_ag(b, bk):
                P.op("act", lambda e: e.activation(out=sag[:, b, :], in_=pb[bk][:, :], func=AF.Silu),
                     reads=[t_pb[bk]], writes=[t_sag])
            tm_group(2, 512, post_ag)

            def fm_chunk(s, col0, bk):
                wv = wsl[s][:].rearrange("p (k c) -> p k c", c=512)
                mm_group(pb[bk][:, :], [(wv[:, k, col0:col0 + 128], hT[:, k, :]) for k in range(8)],
                         t_pb[bk], t_hTb + [t_wsl[s]])

            for gi in range(2):
                ensure(3 + gi + NSLOT - 1)
                s = slot_of[seq[3 + gi]]
                for cc in range(2):
                    c = gi * 2 + cc
                    ba = nextbank(2, 8)
                    fm_chunk(s, cc * 256, ba)
                    bb = nextbank(2, 8)
                    fm_chunk(s, cc * 256 + 128, bb)
                    P.op("act", lambda e, bb=bb: e.activation(out=sigb[:], in_=pb[bb][:, :], func=AF.Sigmoid),
                         reads=[t_pb[bb]], writes=[t_sigb])
                    P.op("dve", lambda e, c=c, ba=ba: e.tensor_tensor(out=u_ext[:, c, 30:542], in0=pb[ba][:, :], in1=sigb[:],
                                                                      op=ALU.mult),
                         reads=[t_pb[ba], t_sigb], writes=[t_uext[c]])
            ensure(5 + NSLOT - 1)
            s = slot_of[seq[5]]
            for c in range(4):
                bk = nextbank(2, 8)
                fm_chunk(s, c * 128, bk)
                P.op("act", lambda e, c=c, bk=bk: e.activation(out=sbg[:, c, :], in_=pb[bk][:, :], func=AF.Silu),
                     reads=[t_pb[bk]], writes=[t_sbg])

            def post_cq(b, bk):
                qk_norm(bk, 512, C_DQG, 8)
                P.op("dve", lambda e: e.tensor_tensor(
                    out=qk_n[:].rearrange("p (h d) -> p h d", d=64),
                    in0=qk_sq[:].rearrange("p (h d) -> p h d", d=64),
                    in1=cst[:, C_DQG:C_DQG + 64].unsqueeze(1).to_broadcast([128, 8, 64]), op=ALU.mult),
                    reads=[t_qksq, t_cst], writes=[t_qkn])
                tb = 0 if b % 2 == 0 else 1
                pT = pb[tb][:].bitcast(BF16)

                def ftr(e):
                    ins = None
                    for c in range(4):
                        ins = e.transpose(pT[:, c * 128:(c + 1) * 128], qk_n[:, c * 128:(c + 1) * 128], ident[:])
                    return ins
                P.op("pe", ftr, reads=[t_qkn, t_ident], writes=[t_pb[tb]])
                P.op("act", lambda e: e.activation(out=cqT[:, :, b * 128:(b + 1) * 128],
                                                   in_=pT[:, 0:512].rearrange("p (c t) -> p c t", t=128), func=AF.Copy),
                     reads=[t_pb[tb]], writes=[t_cqT])
            tm_group(6, 512, post_cq)

            def post_ck(b, bk):
                blk = ti * 4 + b
                qk_norm(bk, 512, C_DKG, 8)
                P.op("dve", lambda e: e.tensor_tensor(
                    out=qk_n[:].rearrange("p (h d) -> p h d", d=64),
                    in0=qk_sq[:].rearrange("p (h d) -> p h d", d=64),
                    in1=cst[:, C_DKG:C_DKG + 64].unsqueeze(1).to_broadcast([128, 8, 64]), op=ALU.mult),
                    reads=[t_qksq, t_cst], writes=[t_qkn])
                tb = 0 if b % 2 == 0 else 1
                pT = pb[tb][:].bitcast(BF16)

                def ftr(e):
                    ins = None
                    for c in range(4):
                        ins = e.transpose(pT[:, c * 128:(c + 1) * 128], qk_n[:, c * 128:(c + 1) * 128], ident[:])
                    return ins
                P.op("pe", ftr, reads=[t_qkn, t_ident], writes=[t_pb[tb]])
                P.op("act", lambda e: e.activation(out=ckT[:, :, blk * 128:(blk + 1) * 128],
                                                   in_=pT[:, 0:512].rearrange("p (c t) -> p c t", t=128), func=AF.Copy),
                     reads=[t_pb[tb]], writes=[t_ckT[blk]])
            tm_group(7, 512, post_ck)

            def post_cv(b, bk):
                blk = ti * 4 + b
                P.op("act", lambda e: e.activation(out=cv1[:, blk, :, 0:128],
                                                   in_=pb[bk][:, :].rearrange("p (h d) -> p h d", d=128), func=AF.Copy),
                     reads=[t_pb[bk]], writes=[t_cv1[blk]])
                P.op("pool", lambda e: e.memset(cv1[:, blk, :, 128:129], 1.0), writes=[t_cv1[blk]], cost=0.2)
            tm_group(8, 512, post_cv)

            ensure(9 + NSLOT - 1)
            s = slot_of[seq[9]]
            for c in range(4):
                bk = nextbank(2, 8)
                fm_chunk(s, c * 128, bk)
                P.op("act", lambda e, c=c, bk=bk: e.activation(out=scg[:, c, :], in_=pb[bk][:, :], func=AF.Silu),
                     reads=[t_pb[bk]], writes=[t_scg])
            ensure(10 + 1)

            tap("aqT", aqT[:], [t_aqT])
            tap("akT2", akT2[:], t_akT)
            tap("av1", av1[:], t_av1)
            tap("sag", sag[:], [t_sag])
            tap("scg", scg[:], [t_scg])
            tap("cqT", cqT[:], [t_cqT])
            tap("ckT", ckT[:, :, 0:512], t_ckT)
            tap("cv1", cv1[:, 0:4], t_cv1)
            tap("u_ext", u_ext[:], t_uext)
            tap("sbg", sbg[:], [t_sbg])
            if ti == 0:
                pass
            for b in range(4):
                blk = ti * 4 + b
                sbs = ([] if blk == 0 else [0]) + [1]
                O = [pb[4], pb[5]]
                for j in range(2):
                    for hf in range(2):
                        bk = hf
                        for sbi in sbs:
                            slot = b + sbi
                            mm_group(pb[bk][:, sbi * 256:(sbi + 1) * 256].rearrange("p (c t) -> p c t", t=128),
                                     [(akT2[hf * 64:(hf + 1) * 64, j, slot * 128:(slot + 1) * 128],
                                       aqT[hf * 64:(hf + 1) * 64, 2 * j:2 * j + 2, b * 128:(b + 1) * 128])],
                                     t_pb[bk], [t_akT[slot], t_aqT])
                        lo = sbs[0] * 256
                        P.op("act", lambda e, hf=hf, bk=bk, lo=lo: e.activation(out=PTa[hf][:, lo:512], in_=pb[bk][:, lo:512],
                                                                                func=AF.Exp, scale=0.125),
                             reads=[t_pb[bk]], writes=[t_PTa[hf]])
                        if 0 in sbs:
                            P.op("pool", lambda e, hf=hf: e.affine_select(
                                out=PTa[hf][:, 0:256], in_=PTa[hf][:, 0:256], compare_op=ALU.is_ge, fill=0.0,
                                base=-1, pattern=[[0, 2], [-1, 128]], channel_multiplier=1),
                                reads=[t_PTa[hf]], writes=[t_PTa[hf]])
                        P.op("pool", lambda e, hf=hf: e.affine_select(
                            out=PTa[hf][:, 256:512], in_=PTa[hf][:, 256:512], compare_op=ALU.is_ge, fill=0.0,
                            base=0, pattern=[[0, 2], [1, 128]], channel_multiplier=-1),
                            reads=[t_PTa[hf]], writes=[t_PTa[hf]])
                    for hf in range(2):
                        for cc in range(2):
                            hl = 2 * cc + hf
                            pairs = []
                            rds = [t_PTa[hf]]
                            for sbi in sbs:
                                slot = b + sbi
                                pairs.append((PTa[hf][:, sbi * 256 + cc * 128: sbi * 256 + (cc + 1) * 128],
                                              av1[:, slot, j, :]))
                                rds.append(t_av1[slot])
                            mm_group(pb[4 + j][:, hl * 65:(hl + 1) * 65], pairs, t_pb[4 + j], rds)
                for j in range(2):
                    Ov = pb[4 + j][:, 0:260].rearrange("p (h e) -> p h e", e=65)
                    P.op("dve", lambda e, j=j, Ov=Ov: e.tensor_tensor(
                        out=den[:, j * 4:(j + 1) * 4].unsqueeze(2), in0=Ov[:, :, 64:65],
                        in1=esink[:, j * 4:(j + 1) * 4].unsqueeze(2), op=ALU.add),
                        reads=[t_pb[4 + j], t_der], writes=[t_den])
                    P.op("dve", lambda e, j=j: e.reciprocal(out=den[:, 8 + j * 4:8 + (j + 1) * 4], in_=den[:, j * 4:(j + 1) * 4]),
                         reads=[t_den], writes=[t_den])
                    P.op("dve", lambda e, j=j, Ov=Ov: e.tensor_tensor(
                        out=ya_f[:, j * 256:(j + 1) * 256].rearrange("p (h d) -> p h d", d=64), in0=Ov[:, :, 0:64],
                        in1=den[:, 8 + j * 4:8 + (j + 1) * 4].unsqueeze(2).to_broadcast([128, 4, 64]), op=ALU.mult),
                        reads=[t_pb[4 + j], t_den], writes=[t_yaf])
                P.op("dve", lambda e, b=b: e.tensor_tensor(out=ya_bf[:], in0=ya_f[:], in1=sag[:, b, :], op=ALU.mult),
                     reads=[t_yaf, t_sag], writes=[t_yabf])
                pT = pb[6][:].bitcast(BF16)

                def ftr(e, pT=pT):
                    ins = None
                    for c in range(4):
                        ins = e.transpose(pT[:, c * 128:(c + 1) * 128], ya_bf[:, c * 128:(c + 1) * 128], ident[:])
                    return ins
                P.op("pe", ftr, reads=[t_yabf, t_ident], writes=[t_pb[6]])
                P.op("act", lambda e, b=b, pT=pT: e.activation(out=yaT[:, :, b * 128:(b + 1) * 128],
                                                               in_=pT[:, 0:512].rearrange("p (c t) -> p c t", t=128), func=AF.Copy),
                     reads=[t_pb[6]], writes=[t_yaT])
            P.op("pool", lambda e: e.tensor_copy(out=akT2[:, :, 0:128], in_=akT2[:, :, 512:640]),
                 reads=[t_akT[4]], writes=[t_akT[0]])
            P.op("pool", lambda e: e.tensor_copy(out=av1[:, 0, :, 0:64], in_=av1[:, 4, :, 0:64]),
                 reads=[t_av1[4]], writes=[t_av1[0]])

            tap("yaT", yaT[:], [t_yaT])
            nsb = ti * 4 + 4
            for h in range(4):
                for jb in range(nsb):
                    pi = jb % 2
                    qb0 = max(0, jb - ti * 4)
                    q0 = qb0 * 128
                    def fsc(e, pi=pi, q0=q0, jb=jb, h=h):
                        ins = None
                        for c in range(2):
                            ins = e.matmul(pb[pi * 2 + c][:, q0:512], ckT[c * 64:(c + 1) * 64, h, jb * 128:(jb + 1) * 128],
                                           cqT[c * 64:(c + 1) * 64, h, q0:512], start=True, stop=True)
                        return ins
                    P.op("pe", fsc, reads=[t_ckT[jb], t_cqT],
                         writes=[t_pb[pi * 2], t_pb[pi * 2 + 1]] + ([t_cgate] if (h == 0 and jb == 0) else []))
                    for c in range(2):
                        bk = pi * 2 + c
                        P.op("act", lambda e, pi=pi, c=c, bk=bk, q0=q0: e.activation(
                            out=PTc[pi][c][:, q0:512], in_=pb[bk][:, q0:512], func=AF.Exp, scale=0.125),
                            reads=[t_pb[bk]], writes=[t_PTc[pi][c]])
                        if jb >= ti * 4:
                            P.op("pool", lambda e, pi=pi, c=c, q0=q0: e.affine_select(
                                out=PTc[pi][c][:, q0:q0 + 128], in_=PTc[pi][c][:, q0:q0 + 128], compare_op=ALU.is_ge,
                                fill=0.0, base=0, pattern=[[1, 128]], channel_multiplier=-1),
                                reads=[t_PTc[pi][c]], writes=[t_PTc[pi][c]])

                    def fpv(e, pi=pi, q0=q0, jb=jb, h=h, nsb=nsb):
                        ins = None
                        for c in range(2):
                            e.matmul(pb[4 + c][:, q0:512], cv1[:, jb, h, 0:128], PTc[pi][c][:, q0:512],
                                     start=(jb == 0), stop=(jb == nsb - 1))
                        for c in range(2):
                            ins = e.matmul(pb[6 + c][:, q0:512], ones_b[:], PTc[pi][c][:, q0:512],
                                           start=(jb == 0), stop=(jb == nsb - 1))
                        return ins
                    P.op("pe", fpv, reads=[t_PTc[pi][0], t_PTc[pi][1], t_cv1[jb], t_ones],
                         writes=[t_pb[4], t_pb[5], t_pb[6], t_pb[7]])
                P.op("dve", lambda e: e.reciprocal(out=e_a[:], in_=pb[6][:, :]), reads=[t_pb[6]], writes=[t_ea])
                P.op("dve", lambda e: e.reciprocal(out=e_b[:], in_=pb[7][:, :]), reads=[t_pb[7]], writes=[t_eb])
                P.op("dve", lambda e: e.tensor_tensor(out=e_a[:], in0=pb[4][:, :], in1=e_a[:], op=ALU.mult),
                     reads=[t_pb[4], t_ea], writes=[t_ea])
                P.op("dve", lambda e: e.tensor_tensor(out=e_b[:], in0=pb[5][:, :], in1=e_b[:], op=ALU.mult),
                     reads=[t_pb[5], t_eb], writes=[t_eb])
                P.op("dve", lambda e: e.scalar_tensor_tensor(out=e_a[:], in0=e_b[:], scalar=lamt[:, 5:6], in1=e_a[:],
                                                             op0=ALU.mult, op1=ALU.add),
                     reads=[t_ea, t_eb, t_der], writes=[t_ea], cost=0.65)
                P.op("act", lambda e: e.activation(out=e_b[:], in_=e_a[:], func=AF.Square), reads=[t_ea], writes=[t_eb])
                mm_group(pb[3][:, :], [(ones_f[:], e_b[:])], t_pb[3], [t_ones, t_eb])
                pow_rstd("dve", e_b[:], pb[3][:, :], 1.0 / 128, [t_pb[3]], [t_eb])
                P.op("dve", lambda e: e.tensor_tensor(out=e_a[:], in0=e_a[:], in1=e_b[:], op=ALU.mult),
                     reads=[t_ea, t_eb], writes=[t_ea])
                P.op("dve", lambda e, h=h: e.scalar_tensor_tensor(out=ycT[:, h, :], in0=e_a[:], scalar=lamt[:, 6:7], in1=scg[:, h, :],
                                                                  op0=ALU.mult, op1=ALU.mult),
                     reads=[t_ea, t_der, t_scg], writes=[t_ycT], cost=0.65)

            tap("ycT", ycT[:], [t_ycT])
            for c in range(4):
                eng = "dve"
                P.op(eng, lambda e, c=c: e.tensor_scalar(out=vcv[:, c, :], in0=u_ext[:, c, 0:512],
                                                         scalar1=cst[:, C_CONVW + c * 31:C_CONVW + c * 31 + 1],
                                                         scalar2=cst[:, C_CONVB + c:C_CONVB + c + 1], op0=ALU.mult, op1=ALU.add),
                     reads=[t_uext[c], t_cst, t_cgate], writes=[t_vcv[c]], prio=CONV_PRIO)
                for jj in range(1, 31):
                    P.op(eng, lambda e, c=c, jj=jj: e.scalar_tensor_tensor(
                        out=vcv[:, c, :], in0=u_ext[:, c, jj:jj + 512],
                        scalar=cst[:, C_CONVW + c * 31 + jj:C_CONVW + c * 31 + jj + 1], in1=vcv[:, c, :],
                        op0=ALU.mult, op1=ALU.add), reads=[t_uext[c], t_cst, t_vcv[c]], writes=[t_vcv[c]], prio=CONV_PRIO, cost=0.65)
                P.op(eng, lambda e, c=c: e.tensor_copy(out=u_ext[:, c, 0:30], in_=u_ext[:, c, 512:542]),
                     reads=[t_uext[c]], writes=[t_uext[c]])
            mm_group(pb[0][:, :], [(ones_f[:], vcv[:, c, :]) for c in range(4)], t_pb[0], [t_ones] + t_vcv)
            P.op("dve", lambda e: e.tensor_scalar(out=mean[:], in0=pb[0][:, :], scalar1=1.0 / 512, scalar2=None, op0=ALU.mult),
                 reads=[t_pb[0]], writes=[t_mean])
            for c in range(4):
                P.op("dve", lambda e, c=c: e.tensor_tensor(out=vcv[:, c, :], in0=vcv[:, c, :], in1=mean[:], op=ALU.subtract),
                     reads=[t_vcv[c], t_mean], writes=[t_vcv[c]])
            for c in range(4):
                P.op("act", lambda e, c=c: e.activation(out=(tmpf if c % 2 == 0 else sigb)[:], in_=vcv[:, c, :], func=AF.Square),
                     reads=[t_vcv[c]], writes=[t_tmpf if c % 2 == 0 else t_sigb])
                P.op("pe", lambda e, c=c: e.matmul(pb[1][:, :], ones_f[:], (tmpf if c % 2 == 0 else sigb)[:],
                                                   start=(c == 0), stop=(c == 3)),
                     reads=[t_ones, t_tmpf if c % 2 == 0 else t_sigb], writes=[t_pb[1]])
            pow_rstd("dve", rstd[:], pb[1][:, :], 1.0 / 512, [t_pb[1]], [t_rstd])
            for c in range(4):
                P.op("dve", lambda e, c=c: e.tensor_tensor(out=vcv[:, c, :], in0=vcv[:, c, :], in1=rstd[:], op=ALU.mult),
                     reads=[t_vcv[c], t_rstd], writes=[t_vcv[c]])
                P.op("act", lambda e, c=c: e.activation(out=vcv[:, c, :], in_=vcv[:, c, :], func=AF.Silu,
                                                        scale=cst[:, C_LNG + c:C_LNG + c + 1], bias=cst[:, C_LNB + c:C_LNB + c + 1]),
                     reads=[t_vcv[c], t_cst], writes=[t_vcv[c]])
                P.op("dve", lambda e, c=c: e.tensor_tensor(out=ybT[:, c, :], in0=vcv[:, c, :], in1=sbg[:, c, :], op=ALU.mult),
                     reads=[t_vcv[c], t_sbg], writes=[t_ybT])

            tap("ybT", ybT[:], [t_ybT])
            yTs = [(yaT, t_yaT), (ybT, t_ybT), (ycT, t_ycT)]
            for mp in range(4):
                gk_wop = 10 + mp * 3
                ensure(gk_wop + NSLOT - 1)
                s_w = slot_of[seq[gk_wop]]
                wop = wsl[s_w][:, 0:3072].rearrange("p (m i k c) -> p m i k c", m=2, i=3, k=4)
                for mm in range(2):
                    m = mp * 2 + mm
                    gk_mg = gk_wop + 1 + mm
                    s_g = slot_of[seq[gk_mg]]
                    mgv = wsl[s_g][:, 0:3072].rearrange("p (k c) -> p k c", c=384)
                    for i in range(3):
                        mm_group(pb[i][:, :], [(mgv[:, k, i * 128:(i + 1) * 128], hT[:, k, :]) for k in range(8)],
                                 t_pb[i], t_hTb + [t_wsl[s_g]])
                        P.op("act", lambda e, i=i: e.activation(out=gt[:, i, :], in_=pb[i][:, :], func=AF.Sigmoid),
                             reads=[t_pb[i]], writes=[t_gt])
                    for i in range(3):
                        yT_, ty_ = yTs[i]
                        mm_group(pb[3 + i][:, :], [(wop[:, mm, i, k, :], yT_[:, k, :]) for k in range(4)],
                                 t_pb[3 + i], [ty_, t_wsl[s_w]])
                    P.op("dve", lambda e: e.tensor_tensor(out=mt0[:], in0=pb[3][:, :], in1=gt[:, 0, :], op=ALU.mult),
                         reads=[t_pb[3], t_gt], writes=[t_mt0])
                    P.op("dve", lambda e: e.tensor_tensor(out=mt1[:], in0=pb[4][:, :], in1=gt[:, 1, :], op=ALU.mult),
                         reads=[t_pb[4], t_gt], writes=[t_mt1])
                    P.op("dve", lambda e: e.tensor_tensor(out=mt0[:], in0=mt0[:], in1=mt1[:], op=ALU.add),
                         reads=[t_mt0, t_mt1], writes=[t_mt0])
                    P.op("dve", lambda e: e.tensor_tensor(out=mt1[:], in0=pb[5][:, :], in1=gt[:, 2, :], op=ALU.mult),
                         reads=[t_pb[5], t_gt], writes=[t_mt1])
                    P.op("dve", lambda e, m=m: e.tensor_tensor(out=mT[:, m, :], in0=mt0[:], in1=mt1[:], op=ALU.add),
                         reads=[t_mt0, t_mt1], writes=[t_aqT if m < 4 else t_cqT])

            tap("mT", mT[:], [t_aqT, t_cqT])
            ensure(23)
            s0 = slot_of[seq[22]]
            s1 = slot_of[seq[23]]
            for b in range(4):
                blk = ti * 4 + b
                xi = blk % 2
                rows = slice(blk * 128, (blk + 1) * 128)
                P.dma("pool", P.chan(f"xr{xi}"), lambda e, xi=xi, rows=rows: e.dma_start(out=xr[xi][:], in_=xin_d[rows, :]),
                      reads=xin_reads(blk), writes=[t_xr[xi]])
                for n, s in enumerate((s0, s1)):
                    bk = 6 + n
                    wv = wsl[s][:].rearrange("p (k c) -> p k c", c=512)
                    mm_group(pb[bk][:, :], [(mT[:, k, b * 128:(b + 1) * 128], wv[:, k, :]) for k in range(8)],
                             t_pb[bk], [t_aqT, t_cqT, t_wsl[s]])
                    P.op("dve", lambda e, xi=xi, n=n, bk=bk: e.tensor_tensor(out=xr[xi][:, n * 512:(n + 1) * 512],
                                                                             in0=pb[bk][:, :], in1=xr[xi][:, n * 512:(n + 1) * 512], op=ALU.add),
                         reads=[t_pb[bk], t_xr[xi]], writes=[t_xr[xi]])
                P.dma("pool", P.chan(f"xst{xi}"), lambda e, xi=xi, rows=rows: e.dma_start(out=xout_d[rows, :], in_=xr[xi][:]),
                      reads=[t_xr[xi]], writes=xout_writes(blk))

    P.emit(schedule=SCHEDULE)
    P.close()
    es.close()
    return nc


def pack_weights(w_in, w_o_attn, w_o_conv, w_o_diff, w_out):
    L = w_in.shape[0]
    wpk = np.zeros((L, NG, 128, GSZ), np.float32)
    for l in range(L):
        wk = w_in[l].reshape(8, 128, 7936).transpose(1, 0, 2)

        def put(g, cols):
            blk = wk[:, :, cols]
            n = blk.shape[2]
            wpk[l, g].reshape(128, 8, 512)[:, :, :n] = blk
        ar = np.arange
        put(0, ar(0, 512))
        put(1, ar(512, 768))
        put(2, ar(768, 1280))
        for gi in range(2):
            cols = []
            for cc in range(2):
                c = gi * 2 + cc
                cols += list(range(1280 + c * 128, 1280 + (c + 1) * 128))
                cols += list(range(1792 + c * 128, 1792 + (c + 1) * 128))
            put(3 + gi, np.array(cols))
        put(5, ar(2304, 2816))
        put(6, ar(2816, 3328))
        put(7, ar(3328, 3840))
        put(8, ar(3840, 4352))
        put(9, ar(4352, 4864))
        wos = [w.reshape(4, 128, 1024).transpose(1, 0, 2) for w in (w_o_attn[l], w_o_conv[l], w_o_diff[l])]
        for mp in range(4):
            g = 10 + mp * 3
            v = wpk[l, g][:, 0:3072].reshape(128, 2, 3, 4, 128)
            for mm in range(2):
                m = mp * 2 + mm
                for i in range(3):
                    v[:, mm, i, :, :] = wos[i][:, :, m * 128:(m + 1) * 128]
                cols = np.concatenate([ar(4864 + i * 1024 + m * 128, 4864 + i * 1024 + (m + 1) * 128) for i in range(3)])
                wpk[l, g + 1 + mm][:, 0:3072].reshape(128, 8, 384)[:] = wk[:, :, cols]
        wo = w_out[l].reshape(8, 128, 1024).transpose(1, 0, 2)
        for n in range(2):
            wpk[l, 22 + n].reshape(128, 8, 512)[:] = wo[:, :, n * 512:(n + 1) * 512]
    return wpk


def pack_consts(norm_g, attn_q_norm_g, attn_k_norm_g, attn_sinks, conv_w, conv_b, conv_norm_g, conv_norm_b,
                diff_q_norm_g, diff_k_norm_g, lambda_q, lambda_k, diff_subln_g):
    L = norm_g.shape[0]
    c = np.zeros((L, 128, NCST), np.float32)
    for l in range(L):
        c[l, :, C_GBC:C_GBC + 1024] = norm_g[l][None, :]
        c[l, :, C_AQG:C_AQG + 64] = attn_q_norm_g[l][None, :]
        c[l, :, C_AKG:C_AKG + 64] = attn_k_norm_g[l][None, :]
        c[l, :, C_DQG:C_DQG + 64] = diff_q_norm_g[l][None, :]
        c[l, :, C_DKG:C_DKG + 64] = diff_k_norm_g[l][None, :]
        c[l, :, C_SUBG:C_SUBG + 128] = diff_subln_g[l][None, :]
        c[l, :, C_SINK:C_SINK + 8] = attn_sinks[l][None, :]
        c[l, :, C_CONVW:C_CONVW + 124] = conv_w[l].reshape(31, 4, 128).transpose(2, 1, 0).reshape(128, 124)
        c[l, :, C_CONVB:C_CONVB + 4] = conv_b[l].reshape(4, 128).T
        c[l, :, C_LNG:C_LNG + 4] = conv_norm_g[l].reshape(4, 128).T
        c[l, :, C_LNB:C_LNB + 4] = conv_norm_b[l].reshape(4, 128).T
        c[l, :, C_LQ:C_LQ + 128] = lambda_q[l].reshape(1, 128)
        c[l, :, C_LK:C_LK + 128] = lambda_k[l].reshape(1, 128)
        c[l, :, C_SUBGC] = diff_subln_g[l]
    return c


_NC_CACHE = {}


def kernel(x, norm_g, w_in, attn_q_norm_g, attn_k_norm_g, attn_sinks, w_o_attn,
           conv_w, conv_b, conv_norm_g, conv_norm_b, w_o_conv,
           diff_q_norm_g, diff_k_norm_g, lambda_q, lambda_k, diff_subln_g, w_o_diff, w_out):
    f = lambda a: np.ascontiguousarray(np.asarray(a, dtype=np.float32))
    x = f(x)
    B, Tn, _ = x.shape
    L = int(np.asarray(norm_g).shape[0])
    wpk = pack_weights(f(w_in), f(w_o_attn), f(w_o_conv), f(w_o_diff), f(w_out))
    cpk = pack_consts(f(norm_g), f(attn_q_norm_g), f(attn_k_norm_g), f(attn_sinks), f(conv_w), f(conv_b),
                      f(conv_norm_g), f(conv_norm_b), f(diff_q_norm_g), f(diff_k_norm_g), f(lambda_q),
                      f(lambda_k), f(diff_subln_g))
    key = (Tn, L)
    if key not in _NC_CACHE:
        _NC_CACHE[key] = build(Tn, L)
    nc = _NC_CACHE[key]
    in_maps = [{"x": x[i], "cpk": cpk, "wpk": wpk} for i in range(B)]
    res = run_bass_kernel_spmd(nc, in_maps, core_ids=list(range(B)))
    return np.stack([np.asarray(r["out"], dtype=np.float32) for r in res.results], axis=0)
```

```python
import contextlib
import math
import types
import numpy as np
import concourse.bass as bass
import concourse.mybir as mybir
from concourse.bass_utils import run_bass_kernel_spmd

F32 = mybir.dt.float32
BF16 = mybir.dt.bfloat16
AF = mybir.ActivationFunctionType
ALU = mybir.AluOpType
AX = mybir.AxisListType

ENGS = ("pe", "act", "dve", "pool", "sp")
SCHEDULE = True
CHECK = True
CONV_PRIO = 2500
NSLOT = 4
ATTACH_WAITS = True
EPS = 1e-6
D = 1024
NG = 24
GSZ = 4096
C_GBC, C_AQG, C_AKG, C_DQG, C_DKG, C_SUBG, C_SINK, C_CONVW, C_CONVB, C_LNG, C_LNB, C_LQ, C_LK, C_SUBGC, NCST = (
    0, 1024, 1088, 1152, 1216, 1280, 1408, 1416, 1540, 1544, 1548, 1552, 1680, 1808, 1812)


class T:
    __slots__ = ("name", "w", "rs")

    def __init__(self, name):
        self.name = name
        self.w = None
        self.rs = []


def _freeze(fn):
    if fn is None or fn.__closure__ is None:
        return fn
    cells = []
    for c in fn.__closure__:
        try:
            cells.append(types.CellType(c.cell_contents))
        except ValueError:
            cells.append(c)
    g = types.FunctionType(fn.__code__, fn.__globals__, fn.__name__, fn.__defaults__, tuple(cells))
    g.__kwdefaults__ = fn.__kwdefaults__
    return g


class Op:
    __slots__ = ("id", "eng", "fn", "chan", "deps", "cost", "lat", "succ", "nun", "fin", "cnt", "semkey", "key", "attach")

    def __init__(self, id_, eng, fn, chan, cost, lat):
        self.id = id_
        self.attach = False
        self.key = id_
        self.eng = eng
        self.fn = fn
        self.chan = chan
        self.deps = {}
        self.cost = cost
        self.lat = lat
        self.succ = []
        self.nun = 0
        self.fin = 0.0
        self.cnt = 0
        self.semkey = None


DEFAULT_COST = {"pe": 0.25, "act": 0.5, "dve": 0.55, "pool": 0.9, "sp": 0.1}


class _MockIns:
    def then_inc(self, *a, **k):
        return self

    def _wait_ge(self, *a, **k):
        return self


def _fsize(ap):
    n = 1
    for d_ in ap.shape[1:]:
        n *= d_
    return n


class _MockEng:
    def __init__(self, eng):
        self.eng = eng
        self.cost = 0.0
        self.ncalls = 0
        self.accum = False

    def __getattr__(self, name):
        def call(*args, **kw):
            eng = self.eng
            self.ncalls += 1
            if kw.get("accum_out") is not None:
                self.accum = True
            if eng == "pe":
                if name == "matmul":
                    rhs = args[2] if len(args) > 2 else kw["rhs"]
                    n = _fsize(rhs)
                    self.cost += 0.02 + n / 2000.0 * (4.0 if rhs.dtype == F32 else 1.0)
                else:
                    self.cost += 0.11
            else:
                out = kw.get("out", args[0] if args else None)
                n = _fsize(out) if out is not None else 64
                if eng == "act":
                    self.cost += 0.2 + n / 1200.0 + (0.1 if kw.get("accum_out") is not None else 0.0)
                elif eng == "dve":
                    self.cost += 0.25 + n / 960.0
                elif eng == "pool":
                    fp32 = out is not None and out.dtype == F32
                    self.cost += 0.2 + n * (0.0032 if fp32 else 0.0012)
                else:
                    self.cost += 0.1
            return _MockIns()
        return call


class Prog:
    def __init__(self, nc):
        self.nc = nc
        self.sems = {}
        self._stack = []
        self.segs = [[]]
        self.nops = 0
        for e in ENGS:
            self._mksem("E_" + e)

    def _mksem(self, key):
        cm = self.nc.semaphore(key)
        h = cm.__enter__()
        self._stack.append(cm)
        self.sems[key] = h
        return h

    def chan(self, name):
        key = "D_" + name
        if key not in self.sems:
            self._mksem(key)
        return key

    def _record(self, o, reads, writes):
        is_dma = o.chan is not None

        def add(d, raw):
            if d is None or d is o:
                return
            need = raw or is_dma or (d.chan is not None) or (d.eng != o.eng) or (o.eng != "pe")
            if need or d not in o.deps:
                o.deps[d] = need or o.deps.get(d, False)
        for t in reads:
            add(t.w, True)
        for t in writes:
            add(t.w, o.eng != "pe")
            for r in t.rs:
                add(r, False)
        for t in reads:
            t.rs.append(o)
        for t in writes:
            t.w = o
            t.rs = []
        self.segs[-1].append(o)

    def op(self, eng, fn, reads=(), writes=(), cost=None, prio=0):
        fn = _freeze(fn)
        att = False
        try:
            m = _MockEng(eng)
            fn(m)
            c = m.cost
            att = ATTACH_WAITS and eng != "pe" and m.ncalls == 1 and not m.accum
        except Exception:
            c = DEFAULT_COST[eng] if cost is None else cost
        o = Op(self.nops, eng, fn, None, c, c + 0.2)
        o.attach = att
        o.key = o.id + prio
        self.nops += 1
        self._record(o, reads, writes)
        return o

    def dma(self, eng, chan, fn, reads=(), writes=(), cost=3.0):
        o = Op(self.nops, eng, _freeze(fn), chan, 0.1 if eng == "sp" else 1.0, cost)
        o.attach = ATTACH_WAITS
        self.nops += 1
        self._record(o, reads, writes)
        return o

    def barrier(self):
        self.segs.append([])

    def _schedule(self, ops):
        import heapq
        inseg = set(o.id for o in ops)
        for o in ops:
            o.succ = []
        for o in ops:
            n = 0
            for d in o.deps:
                if d.id in inseg:
                    d.succ.append(o)
                    n += 1
            o.nun = n
        fut = {e: [] for e in ENGS}
        avail = {e: [] for e in ENGS}
        free = {e: 0.0 for e in ENGS}
        order = {e: [] for e in ENGS}

        def push(o):
            rt = 0.0
            for d in o.deps:
                if d.id in inseg and d.fin > rt:
                    rt = d.fin
            heapq.heappush(fut[o.eng], (rt, o.id, o))
        for o in ops:
            if o.nun == 0:
                push(o)
        left = len(ops)
        while left:
            best = None
            for e in ENGS:
                f, a = fut[e], avail[e]
                while f and f[0][0] <= free[e]:
                    rt, i, o = heapq.heappop(f)
                    heapq.heappush(a, (o.key, i, o))
                if a:
                    st = free[e]
                elif f:
                    st = f[0][0]
                else:
                    continue
                if best is None or st < best[0]:
                    best = (st, e)
            st, e = best
            if avail[e]:
                _k, i, o = heapq.heappop(avail[e])
            else:
                rt, i, o = heapq.heappop(fut[e])
            free[e] = st + o.cost
            o.fin = st + o.lat
            order[e].append(o)
            left -= 1
            for s_ in o.succ:
                s_.nun -= 1
                if s_.nun == 0:
                    push(s_)
        return order, max(free.values())

    def emit(self, schedule=True):
        nc = self.nc
        sems = self.sems
        cnt = {k: 0 for k in sems}
        streams = {e: [] for e in ENGS}
        waited = {e: {} for e in ENGS}
        self.est = 0.0
        for si, seg in enumerate(self.segs):
            if schedule:
                order, est = self._schedule(seg)
                self.est += est
            else:
                order = {e: [o for o in seg if o.eng == e] for e in ENGS}
            for e in ENGS:
                for o in order[e]:
                    if o.chan is None:
                        o.semkey = "E_" + e
                        cnt[o.semkey] += 1
                    else:
                        o.semkey = o.chan
                        cnt[o.semkey] += 16
                    o.cnt = cnt[o.semkey]
            for e in ENGS:
                wd = waited[e]
                for o in order[e]:
                    need = {}
                    for d, ns in o.deps.items():
                        if not ns:
                            continue
                        if need.get(d.semkey, 0) < d.cnt:
                            need[d.semkey] = d.cnt
                    waits = []
                    for k, v in need.items():
                        if wd.get(k, 0) < v:
                            wd[k] = v
                            waits.append((k, v))
                    streams[e].append((waits, o.fn, (o.semkey, 1 if o.chan is None else 16), o))
            for e in ENGS:
                waits = []
                for k, v in cnt.items():
                    if v > 0 and waited[e].get(k, 0) < v:
                        waited[e][k] = v
                        waits.append((k, v))
                streams[e].append((waits, None, None, None))
        self.streams = streams
        if CHECK:
            self.check(streams)

        def replay(e, name):
            for waits, fn, inc, o_ in streams[name]:
                if fn is None:
                    for k, v in waits:
                        e.wait_ge(sems[k], v)
                    continue
                if o_.attach and waits:
                    for k, v in waits[:-1]:
                        e.wait_ge(sems[k], v)
                    ins = fn(e)
                    k, v = waits[-1]
                    ins._wait_ge(sems[k], v)
                else:
                    for k, v in waits:
                        e.wait_ge(sems[k], v)
                    ins = fn(e)
                ins.then_inc(sems[inc[0]], inc[1])

        with nc.Block() as block:
            @block.tensor
            def _(e):
                replay(e, "pe")

            @block.scalar
            def _(e):
                replay(e, "act")

            @block.vector
            def _(e):
                replay(e, "dve")

            @block.gpsimd
            def _(e):
                replay(e, "pool")

            @block.sync
            def _(e):
                replay(e, "sp")

    def check(self, streams):
        val = {k: 0 for k in self.sems}
        ptr = {e: 0 for e in ENGS}
        done = set()
        total = sum(len(v) for v in streams.values())
        n = 0
        while n < total:
            prog = False
            for e in ENGS:
                st = streams[e]
                while ptr[e] < len(st):
                    waits, fn, inc, o = st[ptr[e]]
                    if any(val[k] < v for k, v in waits):
                        break
                    if o is not None:
                        for d in o.deps:
                            assert d.id in done, ("race", e, o.id, d.id, d.eng)
                        done.add(o.id)
                        val[inc[0]] += inc[1]
                        assert val[inc[0]] == o.cnt, ("count mismatch", e, o.id, inc, val[inc[0]], o.cnt)
                    ptr[e] += 1
                    n += 1
                    prog = True
            if not prog:
                info = {e: (ptr[e], len(streams[e]), streams[e][ptr[e]][0] if ptr[e] < len(streams[e]) else None) for e in ENGS}
                raise RuntimeError(f"deadlock: {info} vals={ {k: val[k] for e in ENGS for k, v in (streams[e][ptr[e]][0] if ptr[e] < len(streams[e]) else [])} }")

    def close(self):
        while self._stack:
            self._stack.pop().__exit__(None, None, None)


def build(Tn=4096, L=2, taps=()):
    NB = Tn // 128
    NT = Tn // 512
    nc = bass.Bass("TRN2", target_bir_lowering=False)
    x_d = nc.dram_tensor("x", [Tn, D], F32, kind="ExternalInput").ap()
    cpk_d = nc.dram_tensor("cpk", [L, 128, NCST], F32, kind="ExternalInput").ap()
    wpk_d = nc.dram_tensor("wpk", [L, NG, 128, GSZ], F32, kind="ExternalInput").ap()
    out_d = nc.dram_tensor("out", [Tn, D], F32, kind="ExternalOutput").ap()
    wbf_d = nc.dram_tensor("wbf", [L, NG, 128, GSZ], BF16, kind="Internal").ap()
    x1_d = nc.dram_tensor("x1s", [Tn, D], F32, kind="Internal").ap()
    tap_d = {}
    for name, shape, dt_ in taps:
        tap_d[name] = nc.dram_tensor("tap_" + name, list(shape), dt_, kind="ExternalOutput").ap()

    P = Prog(nc)
    es = contextlib.ExitStack()

    def sb(name, shape, dt, stack=es):
        return stack.enter_context(nc.sbuf_tensor(name, list(shape), dt))

    t_wbf = [[T(f"wbf{l}_{g}") for g in range(NG)] for l in range(L)]

    ident = sb("ident", [128, 128], BF16)
    ones_f = sb("ones_f", [128, 128], F32)
    eps_t = sb("eps_t", [128, 1], F32)
    cst = sb("cst", [128, NCST], F32)
    esink = sb("esink", [128, 8], F32)
    lamt = sb("lamt", [128, 8], F32)
    xld = [sb(f"xld{i}", [128, D], F32) for i in range(2)]
    xr = [sb(f"xr{i}", [128, D], F32) for i in range(2)]
    t_xr = [T("xr0"), T("xr1")]
    hbf0_ = sb("hbf0", [128, D], BF16)
    hbf = [hbf0_, hbf0_]
    nrm = [sb(f"nrm{i}", [128, 4], F32) for i in range(2)]
    hT = sb("hT", [128, 8, 512], BF16)
    wsl = [sb(f"wsl{i}", [128, GSZ], BF16) for i in range(NSLOT)]
    qk_f = sb("qk_f", [128, 512], F32)
    qk_sq = sb("qk_sq", [128, 512], F32)
    qk_ss = sb("qk_ss", [128, 8], F32)
    qk_n = sb("qk_n", [128, 512], BF16)
    mT = sb("mT", [128, 8, 512], BF16)
    aqT = mT[:, 0:4, :]
    akT2 = sb("akT2", [128, 2, 640], BF16)
    av1 = sb("av1", [128, 5, 2, 65], BF16)
    sag = sb("sag", [128, 4, 512], BF16)
    scg = sb("scg", [128, 4, 512], BF16)
    cqT = mT[:, 4:8, :]
    ckT = sb("ckT", [128, 4, Tn], BF16)
    cv1_flat = sb("cv1", [128, NB * 516], BF16)
    cv1 = cv1_flat[:].rearrange("p (a b c) -> p a b c", b=4, c=129)
    u_ext = sb("u_ext", [128, 4, 542], F32)
    sigb = sb("sigb", [128, 512], F32)
    sbg = sb("sbg", [128, 4, 512], BF16)
    vcv = sb("vcv", [128, 4, 512], F32)
    mean = sigb
    rstd = sb("rstd", [128, 512], F32)
    tmpf = sb("tmpf", [128, 512], F32)
    PTc = [[sb(f"PTc{i}{c}", [128, 512], BF16) for c in range(2)] for i in range(2)]
    PTa = PTc[1]
    den = sb("den", [128, 16], F32)
    ya_f = tmpf
    ya_bf = sb("ya_bf", [128, 512], BF16)
    yaT = sb("yaT", [128, 4, 512], BF16)
    ybT = sbg
    ycT = sb("ycT", [128, 4, 512], BF16)
    dsm = sb("dsm", [128, 8], F32)
    d_t = sb("d_t", [128, 128], F32)
    ones_b = sb("ones_b", [128, 128], BF16)
    gt = sb("gt", [128, 3, 512], F32)
    e_a = gt[:, 0, :]
    e_b = gt[:, 1, :]
    mt0 = qk_f
    mt1 = qk_sq

    pb = [es.enter_context(nc.psum_tensor(f"pb{i}", [128, 512], F32)) for i in range(8)]
    t_pb = [T(f"pb{i}") for i in range(8)]

    def TT(*names):
        return [T(n) for n in names]

    (t_ident, t_ones, t_cst, t_der, t_hT, t_qkf, t_qksq, t_qkss, t_qkn, t_aqT, t_sag, t_scg, t_cqT,
     t_sigb, t_sbg, t_mean, t_rstd, t_tmpf, t_den, t_yaf, t_yabf, t_yaT, t_ybT, t_ycT, t_dsm, t_dt, t_dd,
     t_dj, t_ycbf, t_gt, t_mT, t_mt0, t_mt1, t_zrow, t_ones_cols) = TT(
        "ident", "ones", "cst", "der", "hT", "qkf", "qksq", "qkss", "qkn", "aqT", "sag", "scg", "cqT",
        "sigb", "sbg", "mean", "rstd", "tmpf", "den", "yaf", "yabf", "yaT", "ybT", "ycT", "dsm", "dt", "dd",
        "dj", "ycbf", "gt", "mT", "mt0", "mt1", "zrow", "ones_cols")
    t_mt0, t_mt1 = t_qkf, t_qksq
    t_cgate = T("cgate")
    t_ea = t_eb = t_gt
    qkbufs = [(qk_f, qk_sq, qk_ss, qk_n, t_qkf, t_qksq, t_qkss, t_qkn),
              (sb("qk_f2", [128, 512], F32), sb("qk_sq2", [128, 512], F32), sb("qk_ss2", [128, 8], F32),
               sb("qk_n2", [128, 512], BF16), T("qkf2"), T("qksq2"), T("qkss2"), T("qkn2"))]
    qkpar = [0]
    t_ybT = t_sbg
    t_hTb = [T(f"hT{b}") for b in range(4)]
    t_mean = t_sigb
    t_dsmq = [T(f"dsmq{i}") for i in range(4)]
    t_ddq = [T(f"ddq{i}") for i in range(4)]
    t_ss4 = T("ss4")
    t_yaf = t_tmpf
    t_xld = TT("xld0", "xld1")
    t_hbf = [T("hbf0")] * 2
    t_nrm = TT("nrm0", "nrm1")
    t_wsl = [T(f"wsl{i}") for i in range(NSLOT)]
    t_akT = [T(f"akT{i}") for i in range(5)]
    t_av1 = [T(f"av1{i}") for i in range(5)]
    t_ckT = [T(f"ckT{b}") for b in range(NB)]
    t_cv1 = [T(f"cv1{b}") for b in range(NB)]
    t_uext = [T(f"uext{c}") for c in range(4)]
    t_vcv = [T(f"vcv{c}") for c in range(4)]
    t_PTc = [TT("PTc00", "PTc01"), TT("PTc10", "PTc11")]
    t_PTa = t_PTc[1]
    t_x1 = [T(f"x1_{b}") for b in range(NB)]
    t_out = T("out")
    t_tap = T("tap")

    tapped = set()

    def tap(name, src_ap, reads):
        if name in tap_d and name not in tapped:
            tapped.add(name)
            P.dma("sp", P.chan("tap_" + name), lambda e: e.dma_start(out=tap_d[name], in_=src_ap),
                  reads=reads, writes=[t_tap])

    P.op("pool", lambda e: e.memset(ident[:], 0.0), writes=[t_ident])
    P.op("pool", lambda e: e.affine_select(out=ident[:], in_=ident[:], compare_op=ALU.not_equal, fill=1.0,
                                            base=0, pattern=[[-1, 128]], channel_multiplier=1),
         reads=[t_ident], writes=[t_ident])
    P.op("pool", lambda e: e.memset(ones_f[:], 1.0), writes=[t_ones])
    P.op("pool", lambda e: e.memset(ones_b[:], 1.0), writes=[t_ones])
    P.op("pool", lambda e: e.memset(eps_t[:], EPS), writes=[t_ones])
    P.op("pool", lambda e: e.memset(av1[:, :, :, 64:65], 1.0), writes=t_av1)

    wctr = [0]

    if NB >= 28:
        stage = [cv1_flat[:, (4 + 8 * i) * 516:(4 + 8 * i) * 516 + 4096].bitcast(F32) for i in range(3)]
        t_stage = [[t_cv1[b] for b in range(4 + 8 * i, 12 + 8 * i)] for i in range(3)]
    else:
        stage = [sb(f"stage{i}", [128, 2048], F32)[:] for i in range(3)]
        t_stage = [[T(f"stage{i}")] for i in range(3)]
    sctr = [0]

    def wload(l, g, cast):
        i = wctr[0] % NSLOT
        wctr[0] += 1
        if not cast:
            src = wbf_d[l, g, :, :]
            P.dma("sp", P.chan(f"wsl{i}"), lambda e: e.dma_start(out=wsl[i][:], in_=src),
                  reads=[t_wbf[l][g]], writes=[t_wsl[i]], cost=7.0)
            return i
        for hf in range(2):
            si = sctr[0] % 3
            sctr[0] += 1
            src = wpk_d[l, g, :, hf * 2048:(hf + 1) * 2048]
            P.dma("sp", P.chan(f"stage{si}"), lambda e: e.dma_start(out=stage[si], in_=src),
                  writes=t_stage[si], cost=5.0)
            dst = wsl[i][:, hf * 2048:(hf + 1) * 2048]
            if hf == 0 or g % 2 == 1:
                P.op("act", lambda e: e.activation(out=dst, in_=stage[si], func=AF.Copy),
                     reads=t_stage[si], writes=[t_wsl[i]], cost=1.9)
            else:
                P.op("pool", lambda e: e.tensor_copy(out=dst, in_=stage[si]),
                     reads=t_stage[si], writes=[t_wsl[i]], cost=7.0)
        dstd = wbf_d[l, g, :, :]
        P.dma("sp", P.chan(f"wst{i}"), lambda e: e.dma_start(out=dstd, in_=wsl[i][:]),
              reads=[t_wsl[i]], writes=[t_wbf[l][g]], cost=4.0)
        return i

    bankctr = [0]

    def nextbank(lo, hi):
        b = lo + bankctr[0] % (hi - lo)
        bankctr[0] += 1
        return b

    def mm_group(out_ap, pairs, t_out_, reads, skip=False, extra_writes=()):
        n = len(pairs)

        def fn(e):
            ins = None
            for i, (lt, rh) in enumerate(pairs):
                ins = e.matmul(out_ap, lt, rh, start=(i == 0), stop=(i == n - 1))
            return ins
        ncol = 1
        for d_ in pairs[0][1].shape[1:]:
            ncol *= d_
        fp32 = pairs[0][1].dtype == F32
        P.op("pe", fn, reads=reads, writes=[t_out_] + list(extra_writes), cost=n * (0.06 + ncol / 2400.0 * (4 if fp32 else 1)))

    def pow_rstd(eng, out_ap, in_ap, scale, reads, writes):
        P.op("act", lambda e: e.activation(out=out_ap, in_=in_ap, func=AF.Ln, scale=scale, bias=eps_t[:, 0:1]),
             reads=list(reads) + [t_ones], writes=writes)
        P.op("act", lambda e: e.activation(out=out_ap, in_=out_ap, func=AF.Exp, scale=-0.5), reads=writes, writes=writes)

    for l in range(L):
        last = (l == L - 1)
        first = (l == 0)
        lam_init = 0.8 - 0.6 * math.exp(-0.3 * l)
        xin_d = x_d if first else x1_d
        xout_d = out_d if last else x1_d

        def xin_reads(b):
            return [] if first else [t_x1[b]]

        def xout_writes(b):
            return [t_out] if last else [t_x1[b]]

        P.dma("sp", P.chan("cst"), lambda e, l=l: e.dma_start(out=cst[:], in_=cpk_d[l, :, :]), writes=[t_cst])
        P.op("act", lambda e: e.activation(out=esink[:], in_=cst[:, C_SINK:C_SINK + 8], func=AF.Exp),
             reads=[t_cst], writes=[t_der])
        P.op("dve", lambda e: e.tensor_tensor(out=d_t[:, 0:128], in0=cst[:, C_LQ:C_LQ + 128],
                                              in1=cst[:, C_LK:C_LK + 128], op=ALU.mult),
             reads=[t_cst], writes=[t_dt])
        P.op("dve", lambda e: e.reduce_sum(out=lamt[:, 0:2], in_=d_t[:, 0:128].rearrange("p (a b) -> p a b", b=64),
                                           axis=AX.X), reads=[t_dt], writes=[t_der])
        P.op("act", lambda e: e.activation(out=lamt[:, 2:4], in_=lamt[:, 0:2], func=AF.Exp),
             reads=[t_der], writes=[t_der])
        P.op("dve", lambda e: e.tensor_tensor(out=lamt[:, 4:5], in0=lamt[:, 2:3], in1=lamt[:, 3:4], op=ALU.subtract),
             reads=[t_der], writes=[t_der])
        P.op("dve", lambda e, li=lam_init: e.tensor_scalar(out=lamt[:, 5:6], in0=lamt[:, 4:5], scalar1=li,
                                                           scalar2=-1.0, op0=ALU.add, op1=ALU.mult),
             reads=[t_der], writes=[t_der])
        P.op("dve", lambda e, li=lam_init: e.tensor_scalar(out=lamt[:, 6:7], in0=cst[:, C_SUBGC:C_SUBGC + 1],
                                                           scalar1=(1.0 - li), scalar2=None, op0=ALU.mult),
             reads=[t_cst], writes=[t_der])
        for c in range(4):
            P.op("pool", lambda e, c=c: e.memset(u_ext[:, c, 0:30], 0.0), writes=[t_uext[c]])

        wctr_l = []
        for ti in range(NT):
            seq = list(range(NG))
            slot_of = {}
            pend = [(l, g) for g in seq]
            issued = [0]

            def ensure(k):
                while issued[0] <= min(k, NG - 1):
                    g = seq[issued[0]]
                    slot_of[g] = wload(l, g, ti == 0)
                    issued[0] += 1

            ensure(NSLOT - 2)
            for b in range(4):
                blk = ti * 4 + b
                xi = blk % 2
                rows = slice(blk * 128, (blk + 1) * 128)
                P.dma("pool", P.chan(f"xld{xi}"), lambda e, xi=xi, rows=rows: e.dma_start(out=xld[xi][:], in_=xin_d[rows, :]),
                      reads=xin_reads(blk), writes=[t_xld[xi]])
                P.op("act", lambda e, xi=xi: e.activation(out=hbf[xi][:], in_=xld[xi][:], func=AF.Square,
                                                          accum_out=nrm[xi][:, 0:1]),
                     reads=[t_xld[xi]], writes=[t_hbf[xi], t_nrm[xi]])
                pow_rstd("dve", nrm[xi][:, 1:2], nrm[xi][:, 0:1], 1.0 / D, [t_nrm[xi]], [t_nrm[xi]])
                P.op("dve", lambda e, xi=xi: e.scalar_tensor_tensor(out=hbf[xi][:], in0=xld[xi][:], scalar=nrm[xi][:, 1:2],
                                                                    in1=cst[:, C_GBC:C_GBC + D], op0=ALU.mult, op1=ALU.mult),
                     reads=[t_xld[xi], t_nrm[xi], t_cst], writes=[t_hbf[xi]])
                bk = b % 2
                pT = pb[bk][:].bitcast(BF16)

                def ftr(e, xi=xi, pT=pT):
                    ins = None
                    for k in range(8):
                        ins = e.transpose(pT[:, k * 128:(k + 1) * 128], hbf[xi][:, k * 128:(k + 1) * 128], ident[:])
                    return ins
                P.op("pe", ftr, reads=[t_hbf[xi], t_ident], writes=[t_pb[bk]])
                P.op("act", lambda e, b=b, pT=pT: e.activation(out=hT[:, :, b * 128:(b + 1) * 128],
                                                               in_=pT.rearrange("p (k t) -> p k t", t=128), func=AF.Copy),
                     reads=[t_pb[bk]], writes=[t_hTb[b]], cost=0.9)

            tap("hT", hT[:], t_hTb)
            def tm_group(gk, ncols, post):
                ensure(gk + NSLOT - 1)
                s = slot_of[seq[gk]]
                wv = wsl[s][:].rearrange("p (k c) -> p k c", c=512)
                for b in range(4):
                    bk = nextbank(2, 8)
                    mm_group(pb[bk][:, 0:ncols],
                             [(hT[:, k, b * 128:(b + 1) * 128], wv[:, k, 0:ncols]) for k in range(8)],
                             t_pb[bk], [t_hTb[b], t_wsl[s]])
                    post(b, bk)

            def qk_norm(bk, ncols, gcol, nh):
                nonlocal qk_f, qk_sq, qk_ss, qk_n, t_qkf, t_qksq, t_qkss, t_qkn
                (qk_f, qk_sq, qk_ss, qk_n, t_qkf, t_qksq, t_qkss, t_qkn) = qkbufs[qkpar[0] % 2]
                qkpar[0] += 1
                P.op("act", lambda e: e.activation(out=qk_f[:, 0:ncols], in_=pb[bk][:, 0:ncols], func=AF.Copy),
                     reads=[t_pb[bk]], writes=[t_qkf])
                P.op("act", lambda e: e.activation(out=qk_sq[:, 0:ncols], in_=pb[bk][:, 0:ncols], func=AF.Square),
                     reads=[t_pb[bk]], writes=[t_qksq])
                P.op("dve", lambda e: e.reduce_sum(out=qk_ss[:, 0:nh],
                                                   in_=qk_sq[:, 0:ncols].rearrange("p (h d) -> p h d", d=64), axis=AX.X),
                     reads=[t_qksq], writes=[t_qkss])
                pow_rstd("dve", qk_ss[:, 0:nh], qk_ss[:, 0:nh], 1.0 / 64, [t_qkss], [t_qkss])
                P.op("dve", lambda e: e.tensor_tensor(
                    out=qk_sq[:, 0:ncols].rearrange("p (h d) -> p h d", d=64),
                    in0=qk_f[:, 0:ncols].rearrange("p (h d) -> p h d", d=64),
                    in1=qk_ss[:, 0:nh].unsqueeze(2).to_broadcast([128, nh, 64]), op=ALU.mult),
                    reads=[t_qkf, t_qkss], writes=[t_qksq])

            def post_aq(b, bk):
                qk_norm(bk, 512, C_AQG, 8)
                P.op("dve", lambda e: e.tensor_tensor(
                    out=qk_n[:].rearrange("p (h d) -> p h d", d=64),
                    in0=qk_sq[:].rearrange("p (h d) -> p h d", d=64),
                    in1=cst[:, C_AQG:C_AQG + 64].unsqueeze(1).to_broadcast([128, 8, 64]), op=ALU.mult),
                    reads=[t_qksq, t_cst], writes=[t_qkn])
                tb = 0 if b % 2 == 0 else 1
                pT = pb[tb][:].bitcast(BF16)

                def ftr(e):
                    ins = None
                    for c in range(4):
                        ins = e.transpose(pT[:, c * 128:(c + 1) * 128], qk_n[:, c * 128:(c + 1) * 128], ident[:])
                    return ins
                P.op("pe", ftr, reads=[t_qkn, t_ident], writes=[t_pb[tb]])
                P.op("act", lambda e: e.activation(out=aqT[:, :, b * 128:(b + 1) * 128],
                                                   in_=pT[:, 0:512].rearrange("p (c t) -> p c t", t=128), func=AF.Copy),
                     reads=[t_pb[tb]], writes=[t_aqT])
            tm_group(0, 512, post_aq)

            def post_akv(b, bk):
                qk_norm(bk, 128, C_AKG, 2)
                for dup in range(2):
                    P.op("dve", lambda e, dup=dup: e.tensor_tensor(
                        out=qk_n[:, 0:256].rearrange("p (h u d) -> p h u d", u=2, d=64)[:, :, dup, :],
                        in0=qk_sq[:, 0:128].rearrange("p (h d) -> p h d", d=64),
                        in1=cst[:, C_AKG:C_AKG + 64].unsqueeze(1).to_broadcast([128, 2, 64]), op=ALU.mult),
                        reads=[t_qksq, t_cst], writes=[t_qkn])
                tb = 0 if b % 2 == 0 else 1
                pT = pb[tb][:].bitcast(BF16)

                def ftr(e):
                    ins = None
                    for c in range(2):
                        ins = e.transpose(pT[:, c * 128:(c + 1) * 128], qk_n[:, c * 128:(c + 1) * 128], ident[:])
                    return ins
                P.op("pe", ftr, reads=[t_qkn, t_ident], writes=[t_pb[tb]])
                P.op("act", lambda e: e.activation(out=akT2[:, :, (b + 1) * 128:(b + 2) * 128],
                                                   in_=pT[:, 0:256].rearrange("p (c t) -> p c t", t=128), func=AF.Copy),
                     reads=[t_pb[tb]], writes=[t_akT[b + 1]])
                P.op("act", lambda e: e.activation(out=av1[:, b + 1, :, 0:64],
                                                   in_=pb[bk][:, 128:256].rearrange("p (j d) -> p j d", d=64), func=AF.Copy),
                     reads=[t_pb[bk]], writes=[t_av1[b + 1]])
            tm_group(1, 256, post_akv)

            def post_ag(b, bk):
                P.op("act", lambda e: e.activation(out=sag[:, b, :], in_=pb[bk][:, :], func=AF.Silu),
                     reads=[t_pb[bk]], writes=[t_sag])
            tm_group(2, 512, post_ag)

            def fm_chunk(s, col0, bk):
                wv = wsl[s][:].rearrange("p (k c) -> p k c", c=512)
                mm_group(pb[bk][:, :], [(wv[:, k, col0:col0 + 128], hT[:, k, :]) for k in range(8)],
                         t_pb[bk], t_hTb + [t_wsl[s]])

            for gi in range(2):
                ensure(3 + gi + NSLOT - 1)
                s = slot_of[seq[3 + gi]]
                for cc in range(2):
                    c = gi * 2 + cc
                    ba = nextbank(2, 8)
                    fm_chunk(s, cc * 256, ba)
                    bb = nextbank(2, 8)
                    fm_chunk(s, cc * 256 + 128, bb)
                    P.op("act", lambda e, bb=bb: e.activation(out=sigb[:], in_=pb[bb][:, :], func=AF.Sigmoid),
                         reads=[t_pb[bb]], writes=[t_sigb])
                    P.op("dve", lambda e, c=c, ba=ba: e.tensor_tensor(out=u_ext[:, c, 30:542], in0=pb[ba][:, :], in1=sigb[:],
                                                                      op=ALU.mult),
                         reads=[t_pb[ba], t_sigb], writes=[t_uext[c]])
            ensure(5 + NSLOT - 1)
            s = slot_of[seq[5]]
            for c in range(4):
                bk = nextbank(2, 8)
                fm_chunk(s, c * 128, bk)
                P.op("act", lambda e, c=c, bk=bk: e.activation(out=sbg[:, c, :], in_=pb[bk][:, :], func=AF.Silu),
                     reads=[t_pb[bk]], writes=[t_sbg])

            def post_cq(b, bk):
                qk_norm(bk, 512, C_DQG, 8)
                P.op("dve", lambda e: e.tensor_tensor(
                    out=qk_n[:].rearrange("p (h d) -> p h d", d=64),
                    in0=qk_sq[:].rearrange("p (h d) -> p h d", d=64),
                    in1=cst[:, C_DQG:C_DQG + 64].unsqueeze(1).to_broadcast([128, 8, 64]), op=ALU.mult),
                    reads=[t_qksq, t_cst], writes=[t_qkn])
                tb = 0 if b % 2 == 0 else 1
                pT = pb[tb][:].bitcast(BF16)

                def ftr(e):
                    ins = None
                    for c in range(4):
                        ins = e.transpose(pT[:, c * 128:(c + 1) * 128], qk_n[:, c * 128:(c + 1) * 128], ident[:])
                    return ins
                P.op("pe", ftr, reads=[t_qkn, t_ident], writes=[t_pb[tb]])
                P.op("act", lambda e: e.activation(out=cqT[:, :, b * 128:(b + 1) * 128],
                                                   in_=pT[:, 0:512].rearrange("p (c t) -> p c t", t=128), func=AF.Copy),
                     reads=[t_pb[tb]], writes=[t_cqT])
            tm_group(6, 512, post_cq)

            def post_ck(b, bk):
                blk = ti * 4 + b
                qk_norm(bk, 512, C_DKG, 8)
                P.op("dve", lambda e: e.tensor_tensor(
                    out=qk_n[:].rearrange("p (h d) -> p h d", d=64),
                    in0=qk_sq[:].rearrange("p (h d) -> p h d", d=64),
                    in1=cst[:, C_DKG:C_DKG + 64].unsqueeze(1).to_broadcast([128, 8, 64]), op=ALU.mult),
                    reads=[t_qksq, t_cst], writes=[t_qkn])
                tb = 0 if b % 2 == 0 else 1
                pT = pb[tb][:].bitcast(BF16)

                def ftr(e):
                    ins = None
                    for c in range(4):
                        ins = e.transpose(pT[:, c * 128:(c + 1) * 128], qk_n[:, c * 128:(c + 1) * 128], ident[:])
                    return ins
                P.op("pe", ftr, reads=[t_qkn, t_ident], writes=[t_pb[tb]])
                P.op("act", lambda e: e.activation(out=ckT[:, :, blk * 128:(blk + 1) * 128],
                                                   in_=pT[:, 0:512].rearrange("p (c t) -> p c t", t=128), func=AF.Copy),
                     reads=[t_pb[tb]], writes=[t_ckT[blk]])
            tm_group(7, 512, post_ck)

            def post_cv(b, bk):
                blk = ti * 4 + b
                P.op("act", lambda e: e.activation(out=cv1[:, blk, :, 0:128],
                                                   in_=pb[bk][:, :].rearrange("p (h d) -> p h d", d=128), func=AF.Copy),
                     reads=[t_pb[bk]], writes=[t_cv1[blk]])
                P.op("pool", lambda e: e.memset(cv1[:, blk, :, 128:129], 1.0), writes=[t_cv1[blk]], cost=0.2)
            tm_group(8, 512, post_cv)

            ensure(9 + NSLOT - 1)
            s = slot_of[seq[9]]
            for c in range(4):
                bk = nextbank(2, 8)
                fm_chunk(s, c * 128, bk)
                P.op("act", lambda e, c=c, bk=bk: e.activation(out=scg[:, c, :], in_=pb[bk][:, :], func=AF.Silu),
                     reads=[t_pb[bk]], writes=[t_scg])
            ensure(10 + 1)

            tap("aqT", aqT[:], [t_aqT])
            tap("akT2", akT2[:], t_akT)
            tap("av1", av1[:], t_av1)
            tap("sag", sag[:], [t_sag])
            tap("scg", scg[:], [t_scg])
            tap("cqT", cqT[:], [t_cqT])
            tap("ckT", ckT[:, :, 0:512], t_ckT)
            tap("cv1", cv1[:, 0:4], t_cv1)
            tap("u_ext", u_ext[:], t_uext)
            tap("sbg", sbg[:], [t_sbg])
            if ti == 0:
                pass
            for b in range(4):
                blk = ti * 4 + b
                sbs = ([] if blk == 0 else [0]) + [1]
                O = [pb[4], pb[5]]
                for j in range(2):
                    for hf in range(2):
                        bk = hf
                        for sbi in sbs:
                            slot = b + sbi
                            mm_group(pb[bk][:, sbi * 256:(sbi + 1) * 256].rearrange("p (c t) -> p c t", t=128),
                                     [(akT2[hf * 64:(hf + 1) * 64, j, slot * 128:(slot + 1) * 128],
                                       aqT[hf * 64:(hf + 1) * 64, 2 * j:2 * j + 2, b * 128:(b + 1) * 128])],
                                     t_pb[bk], [t_akT[slot], t_aqT])
                        lo = sbs[0] * 256
                        P.op("act", lambda e, hf=hf, bk=bk, lo=lo: e.activation(out=PTa[hf][:, lo:512], in_=pb[bk][:, lo:512],
                                                                                func=AF.Exp, scale=0.125),
                             reads=[t_pb[bk]], writes=[t_PTa[hf]])
                        if 0 in sbs:
                            P.op("pool", lambda e, hf=hf: e.affine_select(
                                out=PTa[hf][:, 0:256], in_=PTa[hf][:, 0:256], compare_op=ALU.is_ge, fill=0.0,
                                base=-1, pattern=[[0, 2], [-1, 128]], channel_multiplier=1),
                                reads=[t_PTa[hf]], writes=[t_PTa[hf]])
                        P.op("pool", lambda e, hf=hf: e.affine_select(
                            out=PTa[hf][:, 256:512], in_=PTa[hf][:, 256:512], compare_op=ALU.is_ge, fill=0.0,
                            base=0, pattern=[[0, 2], [1, 128]], channel_multiplier=-1),
                            reads=[t_PTa[hf]], writes=[t_PTa[hf]])
                    for hf in range(2):
                        for cc in range(2):
                            hl = 2 * cc + hf
                            pairs = []
                            rds = [t_PTa[hf]]
                            for sbi in sbs:
                                slot = b + sbi
                                pairs.append((PTa[hf][:, sbi * 256 + cc * 128: sbi * 256 + (cc + 1) * 128],
                                              av1[:, slot, j, :]))
                                rds.append(t_av1[slot])
                            mm_group(pb[4 + j][:, hl * 65:(hl + 1) * 65], pairs, t_pb[4 + j], rds)
                for j in range(2):
                    Ov = pb[4 + j][:, 0:260].rearrange("p (h e) -> p h e", e=65)
                    P.op("dve", lambda e, j=j, Ov=Ov: e.tensor_tensor(
                        out=den[:, j * 4:(j + 1) * 4].unsqueeze(2), in0=Ov[:, :, 64:65],
                        in1=esink[:, j * 4:(j + 1) * 4].unsqueeze(2), op=ALU.add),
                        reads=[t_pb[4 + j], t_der], writes=[t_den])
                    P.op("dve", lambda e, j=j: e.reciprocal(out=den[:, 8 + j * 4:8 + (j + 1) * 4], in_=den[:, j * 4:(j + 1) * 4]),
                         reads=[t_den], writes=[t_den])
                    P.op("dve", lambda e, j=j, Ov=Ov: e.tensor_tensor(
                        out=ya_f[:, j * 256:(j + 1) * 256].rearrange("p (h d) -> p h d", d=64), in0=Ov[:, :, 0:64],
                        in1=den[:, 8 + j * 4:8 + (j + 1) * 4].unsqueeze(2).to_broadcast([128, 4, 64]), op=ALU.mult),
                        reads=[t_pb[4 + j], t_den], writes=[t_yaf])
                P.op("dve", lambda e, b=b: e.tensor_tensor(out=ya_bf[:], in0=ya_f[:], in1=sag[:, b, :], op=ALU.mult),
                     reads=[t_yaf, t_sag], writes=[t_yabf])
                pT = pb[6][:].bitcast(BF16)

                def ftr(e, pT=pT):
                    ins = None
                    for c in range(4):
                        ins = e.transpose(pT[:, c * 128:(c + 1) * 128], ya_bf[:, c * 128:(c + 1) * 128], ident[:])
                    return ins
                P.op("pe", ftr, reads=[t_yabf, t_ident], writes=[t_pb[6]])
                P.op("act", lambda e, b=b, pT=pT: e.activation(out=yaT[:, :, b * 128:(b + 1) * 128],
                                                               in_=pT[:, 0:512].rearrange("p (c t) -> p c t", t=128), func=AF.Copy),
                     reads=[t_pb[6]], writes=[t_yaT])
            P.op("pool", lambda e: e.tensor_copy(out=akT2[:, :, 0:128], in_=akT2[:, :, 512:640]),
                 reads=[t_akT[4]], writes=[t_akT[0]])
            P.op("pool", lambda e: e.tensor_copy(out=av1[:, 0, :, 0:64], in_=av1[:, 4, :, 0:64]),
                 reads=[t_av1[4]], writes=[t_av1[0]])

            tap("yaT", yaT[:], [t_yaT])
            nsb = ti * 4 + 4
            for h in range(4):
                for jb in range(nsb):
                    pi = jb % 2
                    qb0 = max(0, jb - ti * 4)
                    q0 = qb0 * 128
                    def fsc(e, pi=pi, q0=q0, jb=jb, h=h):
                        ins = None
                        for c in range(2):
                            ins = e.matmul(pb[pi * 2 + c][:, q0:512], ckT[c * 64:(c + 1) * 64, h, jb * 128:(jb + 1) * 128],
                                           cqT[c * 64:(c + 1) * 64, h, q0:512], start=True, stop=True)
                        return ins
                    P.op("pe", fsc, reads=[t_ckT[jb], t_cqT],
                         writes=[t_pb[pi * 2], t_pb[pi * 2 + 1]] + ([t_cgate] if (h == 0 and jb == 0) else []))
                    for c in range(2):
                        bk = pi * 2 + c
                        P.op("act", lambda e, pi=pi, c=c, bk=bk, q0=q0: e.activation(
                            out=PTc[pi][c][:, q0:512], in_=pb[bk][:, q0:512], func=AF.Exp, scale=0.125),
                            reads=[t_pb[bk]], writes=[t_PTc[pi][c]])
                        if jb >= ti * 4:
                            P.op("pool", lambda e, pi=pi, c=c, q0=q0: e.affine_select(
                                out=PTc[pi][c][:, q0:q0 + 128], in_=PTc[pi][c][:, q0:q0 + 128], compare_op=ALU.is_ge,
                                fill=0.0, base=0, pattern=[[1, 128]], channel_multiplier=-1),
                                reads=[t_PTc[pi][c]], writes=[t_PTc[pi][c]])

                    def fpv(e, pi=pi, q0=q0, jb=jb, h=h, nsb=nsb):
                        ins = None
                        for c in range(2):
                            e.matmul(pb[4 + c][:, q0:512], cv1[:, jb, h, 0:128], PTc[pi][c][:, q0:512],
                                     start=(jb == 0), stop=(jb == nsb - 1))
                        for c in range(2):
                            ins = e.matmul(pb[6 + c][:, q0:512], ones_b[:], PTc[pi][c][:, q0:512],
                                           start=(jb == 0), stop=(jb == nsb - 1))
                        return ins
                    P.op("pe", fpv, reads=[t_PTc[pi][0], t_PTc[pi][1], t_cv1[jb], t_ones],
                         writes=[t_pb[4], t_pb[5], t_pb[6], t_pb[7]])
                P.op("dve", lambda e: e.reciprocal(out=e_a[:], in_=pb[6][:, :]), reads=[t_pb[6]], writes=[t_ea])
                P.op("dve", lambda e: e.reciprocal(out=e_b[:], in_=pb[7][:, :]), reads=[t_pb[7]], writes=[t_eb])
                P.op("dve", lambda e: e.tensor_tensor(out=e_a[:], in0=pb[4][:, :], in1=e_a[:], op=ALU.mult),
                     reads=[t_pb[4], t_ea], writes=[t_ea])
                P.op("dve", lambda e: e.tensor_tensor(out=e_b[:], in0=pb[5][:, :], in1=e_b[:], op=ALU.mult),
                     reads=[t_pb[5], t_eb], writes=[t_eb])
                P.op("dve", lambda e: e.scalar_tensor_tensor(out=e_a[:], in0=e_b[:], scalar=lamt[:, 5:6], in1=e_a[:],
                                                             op0=ALU.mult, op1=ALU.add),
                     reads=[t_ea, t_eb, t_der], writes=[t_ea], cost=0.65)
                P.op("act", lambda e: e.activation(out=e_b[:], in_=e_a[:], func=AF.Square), reads=[t_ea], writes=[t_eb])
                mm_group(pb[3][:, :], [(ones_f[:], e_b[:])], t_pb[3], [t_ones, t_eb])
                pow_rstd("dve", e_b[:], pb[3][:, :], 1.0 / 128, [t_pb[3]], [t_eb])
                P.op("dve", lambda e: e.tensor_tensor(out=e_a[:], in0=e_a[:], in1=e_b[:], op=ALU.mult),
                     reads=[t_ea, t_eb], writes=[t_ea])
                P.op("dve", lambda e, h=h: e.scalar_tensor_tensor(out=ycT[:, h, :], in0=e_a[:], scalar=lamt[:, 6:7], in1=scg[:, h, :],
                                                                  op0=ALU.mult, op1=ALU.mult),
                     reads=[t_ea, t_der, t_scg], writes=[t_ycT], cost=0.65)

            tap("ycT", ycT[:], [t_ycT])
            for c in range(4):
                eng = "dve"
                P.op(eng, lambda e, c=c: e.tensor_scalar(out=vcv[:, c, :], in0=u_ext[:, c, 0:512],
                                                         scalar1=cst[:, C_CONVW + c * 31:C_CONVW + c * 31 + 1],
                                                         scalar2=cst[:, C_CONVB + c:C_CONVB + c + 1], op0=ALU.mult, op1=ALU.add),
                     reads=[t_uext[c], t_cst, t_cgate], writes=[t_vcv[c]], prio=CONV_PRIO)
                for jj in range(1, 31):
                    P.op(eng, lambda e, c=c, jj=jj: e.scalar_tensor_tensor(
                        out=vcv[:, c, :], in0=u_ext[:, c, jj:jj + 512],
                        scalar=cst[:, C_CONVW + c * 31 + jj:C_CONVW + c * 31 + jj + 1], in1=vcv[:, c, :],
                        op0=ALU.mult, op1=ALU.add), reads=[t_uext[c], t_cst, t_vcv[c]], writes=[t_vcv[c]], prio=CONV_PRIO, cost=0.65)
                P.op(eng, lambda e, c=c: e.tensor_copy(out=u_ext[:, c, 0:30], in_=u_ext[:, c, 512:542]),
                     reads=[t_uext[c]], writes=[t_uext[c]])
            mm_group(pb[0][:, :], [(ones_f[:], vcv[:, c, :]) for c in range(4)], t_pb[0], [t_ones] + t_vcv)
            P.op("dve", lambda e: e.tensor_scalar(out=mean[:], in0=pb[0][:, :], scalar1=1.0 / 512, scalar2=None, op0=ALU.mult),
                 reads=[t_pb[0]], writes=[t_mean])
            for c in range(4):
                P.op("dve", lambda e, c=c: e.tensor_tensor(out=vcv[:, c, :], in0=vcv[:, c, :], in1=mean[:], op=ALU.subtract),
                     reads=[t_vcv[c], t_mean], writes=[t_vcv[c]])
            for c in range(4):
                P.op("act", lambda e, c=c: e.activation(out=(tmpf if c % 2 == 0 else sigb)[:], in_=vcv[:, c, :], func=AF.Square),
                     reads=[t_vcv[c]], writes=[t_tmpf if c % 2 == 0 else t_sigb])
                P.op("pe", lambda e, c=c: e.matmul(pb[1][:, :], ones_f[:], (tmpf if c % 2 == 0 else sigb)[:],
                                                   start=(c == 0), stop=(c == 3)),
                     reads=[t_ones, t_tmpf if c % 2 == 0 else t_sigb], writes=[t_pb[1]])
            pow_rstd("dve", rstd[:], pb[1][:, :], 1.0 / 512, [t_pb[1]], [t_rstd])
            for c in range(4):
                P.op("dve", lambda e, c=c: e.tensor_tensor(out=vcv[:, c, :], in0=vcv[:, c, :], in1=rstd[:], op=ALU.mult),
                     reads=[t_vcv[c], t_rstd], writes=[t_vcv[c]])
                P.op("act", lambda e, c=c: e.activation(out=vcv[:, c, :], in_=vcv[:, c, :], func=AF.Silu,
                                                        scale=cst[:, C_LNG + c:C_LNG + c + 1], bias=cst[:, C_LNB + c:C_LNB + c + 1]),
                     reads=[t_vcv[c], t_cst], writes=[t_vcv[c]])
                P.op("dve", lambda e, c=c: e.tensor_tensor(out=ybT[:, c, :], in0=vcv[:, c, :], in1=sbg[:, c, :], op=ALU.mult),
                     reads=[t_vcv[c], t_sbg], writes=[t_ybT])

            tap("ybT", ybT[:], [t_ybT])
            yTs = [(yaT, t_yaT), (ybT, t_ybT), (ycT, t_ycT)]
            for mp in range(4):
                gk_wop = 10 + mp * 3
                ensure(gk_wop + NSLOT - 1)
                s_w = slot_of[seq[gk_wop]]
                wop = wsl[s_w][:, 0:3072].rearrange("p (m i k c) -> p m i k c", m=2, i=3, k=4)
                for mm in range(2):
                    m = mp * 2 + mm
                    gk_mg = gk_wop + 1 + mm
                    s_g = slot_of[seq[gk_mg]]
                    mgv = wsl[s_g][:, 0:3072].rearrange("p (k c) -> p k c", c=384)
                    for i in range(3):
                        mm_group(pb[i][:, :], [(mgv[:, k, i * 128:(i + 1) * 128], hT[:, k, :]) for k in range(8)],
                                 t_pb[i], t_hTb + [t_wsl[s_g]])
                        P.op("act", lambda e, i=i: e.activation(out=gt[:, i, :], in_=pb[i][:, :], func=AF.Sigmoid),
                             reads=[t_pb[i]], writes=[t_gt])
                    for i in range(3):
                        yT_, ty_ = yTs[i]
                        mm_group(pb[3 + i][:, :], [(wop[:, mm, i, k, :], yT_[:, k, :]) for k in range(4)],
                                 t_pb[3 + i], [ty_, t_wsl[s_w]])
                    P.op("dve", lambda e: e.tensor_tensor(out=mt0[:], in0=pb[3][:, :], in1=gt[:, 0, :], op=ALU.mult),
                         reads=[t_pb[3], t_gt], writes=[t_mt0])
                    P.op("dve", lambda e: e.tensor_tensor(out=mt1[:], in0=pb[4][:, :], in1=gt[:, 1, :], op=ALU.mult),
                         reads=[t_pb[4], t_gt], writes=[t_mt1])
                    P.op("dve", lambda e: e.tensor_tensor(out=mt0[:], in0=mt0[:], in1=mt1[:], op=ALU.add),
                         reads=[t_mt0, t_mt1], writes=[t_mt0])
                    P.op("dve", lambda e: e.tensor_tensor(out=mt1[:], in0=pb[5][:, :], in1=gt[:, 2, :], op=ALU.mult),
                         reads=[t_pb[5], t_gt], writes=[t_mt1])
                    P.op("dve", lambda e, m=m: e.tensor_tensor(out=mT[:, m, :], in0=mt0[:], in1=mt1[:], op=ALU.add),
                         reads=[t_mt0, t_mt1], writes=[t_aqT if m < 4 else t_cqT])

            tap("mT", mT[:], [t_aqT, t_cqT])
            ensure(23)
            s0 = slot_of[seq[22]]
            s1 = slot_of[seq[23]]
            for b in range(4):
                blk = ti * 4 + b
                xi = blk % 2
                rows = slice(blk * 128, (blk + 1) * 128)
                P.dma("pool", P.chan(f"xr{xi}"), lambda e, xi=xi, rows=rows: e.dma_start(out=xr[xi][:], in_=xin_d[rows, :]),
                      reads=xin_reads(blk), writes=[t_xr[xi]])
                for n, s in enumerate((s0, s1)):
                    bk = 6 + n
                    wv = wsl[s][:].rearrange("p (k c) -> p k c", c=512)
                    mm_group(pb[bk][:, :], [(mT[:, k, b * 128:(b + 1) * 128], wv[:, k, :]) for k in range(8)],
                             t_pb[bk], [t_aqT, t_cqT, t_wsl[s]])
                    P.op("dve", lambda e, xi=xi, n=n, bk=bk: e.tensor_tensor(out=xr[xi][:, n * 512:(n + 1) * 512],
                                                                             in0=pb[bk][:, :], in1=xr[xi][:, n * 512:(n + 1) * 512], op=ALU.add),
                         reads=[t_pb[bk], t_xr[xi]], writes=[t_xr[xi]])
                P.dma("pool", P.chan(f"xst{xi}"), lambda e, xi=xi, rows=rows: e.dma_start(out=xout_d[rows, :], in_=xr[xi][:]),
                      reads=[t_xr[xi]], writes=xout_writes(blk))

    P.emit(schedule=SCHEDULE)
    P.close()
    es.close()
    return nc


def pack_weights(w_in, w_o_attn, w_o_conv, w_o_diff, w_out):
    L = w_in.shape[0]
    wpk = np.zeros((L, NG, 128, GSZ), np.float32)
    for l in range(L):
        wk = w_in[l].reshape(8, 128, 7936).transpose(1, 0, 2)

        def put(g, cols):
            blk = wk[:, :, cols]
            n = blk.shape[2]
            wpk[l, g].reshape(128, 8, 512)[:, :, :n] = blk
        ar = np.arange
        put(0, ar(0, 512))
        put(1, ar(512, 768))
        put(2, ar(768, 1280))
        for gi in range(2):
            cols = []
            for cc in range(2):
                c = gi * 2 + cc
                cols += list(range(1280 + c * 128, 1280 + (c + 1) * 128))
                cols += list(range(1792 + c * 128, 1792 + (c + 1) * 128))
            put(3 + gi, np.array(cols))
        put(5, ar(2304, 2816))
        put(6, ar(2816, 3328))
        put(7, ar(3328, 3840))
        put(8, ar(3840, 4352))
        put(9, ar(4352, 4864))
        wos = [w.reshape(4, 128, 1024).transpose(1, 0, 2) for w in (w_o_attn[l], w_o_conv[l], w_o_diff[l])]
        for mp in range(4):
            g = 10 + mp * 3
            v = wpk[l, g][:, 0:3072].reshape(128, 2, 3, 4, 128)
            for mm in range(2):
                m = mp * 2 + mm
                for i in range(3):
                    v[:, mm, i, :, :] = wos[i][:, :, m * 128:(m + 1) * 128]
                cols = np.concatenate([ar(4864 + i * 1024 + m * 128, 4864 + i * 1024 + (m + 1) * 128) for i in range(3)])
                wpk[l, g + 1 + mm][:, 0:3072].reshape(128, 8, 384)[:] = wk[:, :, cols]
        wo = w_out[l].reshape(8, 128, 1024).transpose(1, 0, 2)
        for n in range(2):
            wpk[l, 22 + n].reshape(128, 8, 512)[:] = wo[:, :, n * 512:(n + 1) * 512]
    return wpk


def pack_consts(norm_g, attn_q_norm_g, attn_k_norm_g, attn_sinks, conv_w, conv_b, conv_norm_g, conv_norm_b,
                diff_q_norm_g, diff_k_norm_g, lambda_q, lambda_k, diff_subln_g):
    L = norm_g.shape[0]
    c = np.zeros((L, 128, NCST), np.float32)
    for l in range(L):
        c[l, :, C_GBC:C_GBC + 1024] = norm_g[l][None, :]
        c[l, :, C_AQG:C_AQG + 64] = attn_q_norm_g[l][None, :]
        c[l, :, C_AKG:C_AKG + 64] = attn_k_norm_g[l][None, :]
        c[l, :, C_DQG:C_DQG + 64] = diff_q_norm_g[l][None, :]
        c[l, :, C_DKG:C_DKG + 64] = diff_k_norm_g[l][None, :]
        c[l, :, C_SUBG:C_SUBG + 128] = diff_subln_g[l][None, :]
        c[l, :, C_SINK:C_SINK + 8] = attn_sinks[l][None, :]
        c[l, :, C_CONVW:C_CONVW + 124] = conv_w[l].reshape(31, 4, 128).transpose(2, 1, 0).reshape(128, 124)
        c[l, :, C_CONVB:C_CONVB + 4] = conv_b[l].reshape(4, 128).T
        c[l, :, C_LNG:C_LNG + 4] = conv_norm_g[l].reshape(4, 128).T
        c[l, :, C_LNB:C_LNB + 4] = conv_norm_b[l].reshape(4, 128).T
        c[l, :, C_LQ:C_LQ + 128] = lambda_q[l].reshape(1, 128)
        c[l, :, C_LK:C_LK + 128] = lambda_k[l].reshape(1, 128)
        c[l, :, C_SUBGC] = diff_subln_g[l]
    return c


_NC_CACHE = {}


def kernel(x, norm_g, w_in, attn_q_norm_g, attn_k_norm_g, attn_sinks, w_o_attn,
           conv_w, conv_b, conv_norm_g, conv_norm_b, w_o_conv,
           diff_q_norm_g, diff_k_norm_g, lambda_q, lambda_k, diff_subln_g, w_o_diff, w_out):
    f = lambda a: np.ascontiguousarray(np.asarray(a, dtype=np.float32))
    x = f(x)
    B, Tn, _ = x.shape
    L = int(np.asarray(norm_g).shape[0])
    wpk = pack_weights(f(w_in), f(w_o_attn), f(w_o_conv), f(w_o_diff), f(w_out))
    cpk = pack_consts(f(norm_g), f(attn_q_norm_g), f(attn_k_norm_g), f(attn_sinks), f(conv_w), f(conv_b),
                      f(conv_norm_g), f(conv_norm_b), f(diff_q_norm_g), f(diff_k_norm_g), f(lambda_q),
                      f(lambda_k), f(diff_subln_g))
    key = (Tn, L)
    if key not in _NC_CACHE:
        _NC_CACHE[key] = build(Tn, L)
    nc = _NC_CACHE[key]
    in_maps = [{"x": x[i], "cpk": cpk, "wpk": wpk} for i in range(B)]
    res = run_bass_kernel_spmd(nc, in_maps, core_ids=list(range(B)))
    return np.stack([np.asarray(r["out"], dtype=np.float32) for r in res.results], axis=0)
```
